# Optimizing a Trainium2 kernel written in Bass

```python
import math
import jax, jax.numpy as jnp
from jax import lax
import numpy as np

D_MODEL = 4096
BATCH = 4
SEQ = 4096
DEPTH = 1

PLE_DIM = 256
ATTN_HEAD_DIM = 128
ATTN_HEADS_PER_GROUP = 4
DILATED_GROUPS = ((128, 1), (512, 4), (2048, 16))
N_ATTN_GROUPS = len(DILATED_GROUPS)
N_ATTN_HEADS = N_ATTN_GROUPS * ATTN_HEADS_PER_GROUP
ATTN_WIDTH = N_ATTN_HEADS * ATTN_HEAD_DIM
ATTN_OUT_WIDTH = ATTN_HEADS_PER_GROUP * ATTN_HEAD_DIM
ATTN_BLOCK = 128
N_BUCKETS = 32
MAX_DISTANCE = 2048
RWKV_HEAD_DIM = 64
RWKV_WIDTH = D_MODEL // 2
RWKV_HEADS = RWKV_WIDTH // RWKV_HEAD_DIM
DECAY_LORA = 96
AAA_LORA = 96
GATE_LORA = 256
RWKV_IN_WIDTH = 3 * RWKV_WIDTH + DECAY_LORA + AAA_LORA + GATE_LORA
D_FF = 4 * D_MODEL
N_BRANCHES = 2
IN_WIDTH = 3 * ATTN_WIDTH + RWKV_IN_WIDTH + N_BRANCHES * D_MODEL
RMS_EPS = 1e-6
GN_EPS = 64e-5

kernel_name = "hybrid_dilated_attn_rwkv7_gated_block"


def rms_norm(x, gain):
    x32 = x.astype(jnp.float32)
    y = x32 * lax.rsqrt(jnp.mean(jnp.square(x32), axis=-1, keepdims=True) + RMS_EPS)
    return (y * gain.astype(jnp.float32)).astype(x.dtype)


def t5_bucket(dist):
    max_exact = N_BUCKETS // 2
    d_f = jnp.maximum(dist, 1).astype(jnp.float32)
    large = max_exact + (jnp.log(d_f / max_exact) / math.log(MAX_DISTANCE / max_exact)
                         * (N_BUCKETS - max_exact)).astype(jnp.int32)
    large = jnp.minimum(large, N_BUCKETS - 1)
    return jnp.where(dist < max_exact, dist, large)


def dilated_window_attention(q, k, v, bias_table, window, dilation):
    batch, seq, heads, hd = q.shape
    n_dist = window // dilation
    blk = ATTN_BLOCK
    span = dilation * blk
    s_pad = -(-seq // span) * span
    length = s_pad // dilation
    nb = length // blk

    def to_blocks(t):
        t = jnp.pad(t, ((0, 0), (0, s_pad - seq), (0, 0), (0, 0)))
        t = t.reshape(batch, length, dilation, heads, hd).transpose(0, 2, 3, 1, 4)
        return t.reshape(batch, dilation, heads, nb, blk, hd)

    def with_prev(t):
        prev = jnp.pad(t[:, :, :, :-1], ((0, 0), (0, 0), (0, 0), (1, 0), (0, 0), (0, 0)))
        return jnp.concatenate([prev, t], axis=4)

    qb = to_blocks(q).astype(jnp.float32)
    kb = with_prev(to_blocks(k)).astype(jnp.float32)
    vb = with_prev(to_blocks(v)).astype(jnp.float32)
    s = jnp.einsum('bdhnqe,bdhnke->bdhnqk', qb, kb) * (hd ** -0.5)

    q_idx = blk + jnp.arange(blk)
    k_idx = jnp.arange(2 * blk)
    rel = q_idx[:, None] - k_idx[None, :]
    band = (rel >= 0) & (rel <= n_dist)
    first = (jnp.arange(nb)[:, None, None] == 0) & (k_idx[None, None, :] < blk)
    valid = band[None] & ~first
    bucket = t5_bucket(jnp.maximum(rel, 0) * dilation)
    bias = jnp.transpose(bias_table[bucket].astype(jnp.float32), (2, 0, 1))
    s = jnp.where(valid, s + bias[:, None], -jnp.inf)

    m = jnp.max(s, axis=-1, keepdims=True)
    e = jnp.exp(s - m)
    l = jnp.sum(e, axis=-1, keepdims=True)
    o = jnp.einsum('bdhnqk,bdhnke->bdhnqe', e, vb) / l
    lse = (m + jnp.log(l))[..., 0]

    o = o.reshape(batch, dilation, heads, length, hd).transpose(0, 3, 1, 2, 4)
    o = o.reshape(batch, s_pad, heads, hd)[:, :seq]
    lse = lse.reshape(batch, dilation, heads, length).transpose(0, 3, 1, 2)
    lse = lse.reshape(batch, s_pad, heads)[:, :seq]
    return o, lse


def token_shift(z, mix):
    prev = jnp.pad(z, ((0, 0), (1, 0), (0, 0)))[:, :-1]
    return z + mix * (prev - z)


def rwkv7_time_mix(z, w0, w_decay_up, a0, w_aaa_up, w_gate_up, k_k, k_a, r_k, gn_w, gn_b):
    batch, seq, _ = z.shape
    f32 = jnp.float32
    z = z.astype(f32)
    c0 = RWKV_WIDTH
    r = z[..., :c0]
    k = z[..., c0:2 * c0]
    v = z[..., 2 * c0:3 * c0]
    xw = z[..., 3 * c0:3 * c0 + DECAY_LORA]
    xa = z[..., 3 * c0 + DECAY_LORA:3 * c0 + DECAY_LORA + AAA_LORA]
    xg = z[..., 3 * c0 + DECAY_LORA + AAA_LORA:]

    w = -jax.nn.softplus(-(w0.astype(f32) + jnp.tanh(xw) @ w_decay_up.astype(f32))) - 0.5
    a = jax.nn.sigmoid(a0.astype(f32) + xa @ w_aaa_up.astype(f32))
    g = jax.nn.sigmoid(xg) @ w_gate_up.astype(f32)
    decay = jnp.exp(-jnp.exp(w))

    hs = (batch, seq, RWKV_HEADS, RWKV_HEAD_DIM)
    kk = (k * k_k.astype(f32)).reshape(hs)
    kk = kk / jnp.maximum(jnp.linalg.norm(kk, axis=-1, keepdims=True), 1e-12)
    k = k * (1.0 + (a - 1.0) * k_a.astype(f32))
    r_h, k_h, v_h, a_h, d_h = (t.reshape(hs) for t in (r, k, v, a, decay))

    def step(state, inp):
        r_t, w_t, k_t, v_t, aa_t, bb_t = inp
        sa = jnp.einsum('bhij,bhj->bhi', state, aa_t)
        state = (state * w_t[:, :, None, :] + sa[..., None] * bb_t[:, :, None, :]
                 + v_t[..., None] * k_t[:, :, None, :])
        return state, jnp.einsum('bhij,bhj->bhi', state, r_t)

    xs = tuple(jnp.moveaxis(t, 1, 0) for t in (r_h, d_h, k_h, v_h, -kk, kk * a_h))
    state0 = jnp.zeros((batch, RWKV_HEADS, RWKV_HEAD_DIM, RWKV_HEAD_DIM), f32)
    _, y = lax.scan(step, state0, xs)
    y = jnp.moveaxis(y, 0, 1)

    mu = jnp.mean(y, axis=-1, keepdims=True)
    var = jnp.mean(jnp.square(y - mu), axis=-1, keepdims=True)
    y = ((y - mu) * lax.rsqrt(var + GN_EPS)).reshape(batch, seq, RWKV_WIDTH)
    y = (y * gn_w.astype(f32) + gn_b.astype(f32)).reshape(hs)
    y = y + jnp.sum(r_h * k_h * r_k.astype(f32), axis=-1, keepdims=True) * v_h
    return y.reshape(batch, seq, RWKV_WIDTH) * g


def setup_inputs(seed: int = 0) -> dict:
    key = jax.random.key(seed)
    ks = jax.random.split(key, 32)
    f32 = jnp.float32

    def nrm(k, shape, scale):
        return jax.random.normal(k, shape, f32) * scale

    def gain(k, shape):
        return 1.0 + 0.05 * jax.random.normal(k, shape, f32)

    L = DEPTH
    return {
        "x": nrm(ks[0], (BATCH, SEQ, D_MODEL), 1.0),
        "p": nrm(ks[1], (DEPTH, BATCH, SEQ, PLE_DIM), 1.0),
        "norm_mix": gain(ks[2], (L, D_MODEL)),
        "w_in": nrm(ks[3], (L, D_MODEL, IN_WIDTH), D_MODEL ** -0.5),
        "q_gain": gain(ks[4], (L, ATTN_HEAD_DIM)),
        "k_gain": gain(ks[5], (L, ATTN_HEAD_DIM)),
        "rel_bias": nrm(ks[6], (N_BUCKETS, N_ATTN_HEADS), 0.5),
        "w_attn_up": nrm(ks[7], (L, ATTN_OUT_WIDTH, D_MODEL), ATTN_OUT_WIDTH ** -0.5),
        "shift_mix": jax.random.uniform(ks[8], (L, RWKV_IN_WIDTH), f32),
        "w0": jax.random.uniform(ks[9], (L, RWKV_WIDTH), f32, minval=-5.0, maxval=1.0),
        "w_decay_up": nrm(ks[10], (L, DECAY_LORA, RWKV_WIDTH), DECAY_LORA ** -0.5),
        "a0": nrm(ks[11], (L, RWKV_WIDTH), 0.1),
        "w_aaa_up": nrm(ks[12], (L, AAA_LORA, RWKV_WIDTH), AAA_LORA ** -0.5),
        "w_gate_up": nrm(ks[13], (L, GATE_LORA, RWKV_WIDTH), GATE_LORA ** -0.5),
        "k_k": 0.85 + nrm(ks[14], (L, RWKV_WIDTH), 0.05),
        "k_a": gain(ks[15], (L, RWKV_WIDTH)),
        "r_k": nrm(ks[16], (L, RWKV_HEADS, RWKV_HEAD_DIM), 0.1),
        "gn_w": gain(ks[17], (L, RWKV_WIDTH)),
        "gn_b": nrm(ks[18], (L, RWKV_WIDTH), 0.02),
        "w_rwkv_up": nrm(ks[19], (L, RWKV_WIDTH, D_MODEL), RWKV_WIDTH ** -0.5),
        "w_out": nrm(ks[20], (L, D_MODEL, D_MODEL), D_MODEL ** -0.5),
        "norm_mlp": gain(ks[21], (L, D_MODEL)),
        "w_mlp_in": nrm(ks[22], (L, D_MODEL, D_FF), D_MODEL ** -0.5),
        "w_mlp_out": nrm(ks[23], (L, D_FF, D_MODEL), D_FF ** -0.5),
        "norm_ple": gain(ks[24], (L, D_MODEL)),
        "w_ple_gate": nrm(ks[25], (L, D_MODEL, D_MODEL), D_MODEL ** -0.5),
        "w_ple_proj": nrm(ks[26], (L, PLE_DIM, D_MODEL), PLE_DIM ** -0.5),
    }


def reference(x, p, norm_mix, w_in, q_gain, k_gain, rel_bias, w_attn_up, shift_mix, w0, w_decay_up,
              a0, w_aaa_up, w_gate_up, k_k, k_a, r_k, gn_w, gn_b, w_rwkv_up, w_out, norm_mlp,
              w_mlp_in, w_mlp_out, norm_ple, w_ple_gate, w_ple_proj):
    batch, seq, _ = x.shape
    a_end = 3 * ATTN_WIDTH
    r_end = a_end + RWKV_IN_WIDTH
    for i in range(DEPTH):
        h = rms_norm(x, norm_mix[i])
        proj = h @ w_in[i]
        qkv = proj[..., :a_end].reshape(batch, seq, 3, N_ATTN_HEADS, ATTN_HEAD_DIM)
        z = proj[..., a_end:r_end]
        gates = jax.nn.sigmoid(proj[..., r_end:]).reshape(batch, seq, N_BRANCHES, D_MODEL)

        q = rms_norm(qkv[:, :, 0], q_gain[i])
        k = rms_norm(qkv[:, :, 1], k_gain[i])
        v = qkv[:, :, 2]
        outs, lses = [], []
        for gi, (window, dilation) in enumerate(DILATED_GROUPS):
            sl = slice(gi * ATTN_HEADS_PER_GROUP, (gi + 1) * ATTN_HEADS_PER_GROUP)
            o, lse = dilated_window_attention(q[:, :, sl], k[:, :, sl], v[:, :, sl],
                                              rel_bias[:, sl], window, dilation)
            outs.append(o)
            lses.append(lse)
        mix_w = jax.nn.softmax(jnp.stack(lses, axis=0), axis=0)
        attn = jnp.sum(mix_w[..., None] * jnp.stack(outs, axis=0), axis=0)
        attn = attn.reshape(batch, seq, ATTN_OUT_WIDTH).astype(x.dtype)
        attn_d = attn @ w_attn_up[i]

        zs = token_shift(z, shift_mix[i])
        rw = rwkv7_time_mix(zs, w0[i], w_decay_up[i], a0[i], w_aaa_up[i], w_gate_up[i],
                            k_k[i], k_a[i], r_k[i], gn_w[i], gn_b[i]).astype(x.dtype)
        rwkv_d = rw @ w_rwkv_up[i]

        merged = gates[:, :, 0] * attn_d + gates[:, :, 1] * rwkv_d
        x = x + merged @ w_out[i]

        h = rms_norm(x, norm_mlp[i])
        x = x + jnp.square(jax.nn.relu(h @ w_mlp_in[i])) @ w_mlp_out[i]

        ple_gate = jax.nn.sigmoid(rms_norm(x, norm_ple[i]) @ w_ple_gate[i])
        x = x + ple_gate * (p[i] @ w_ple_proj[i])
    return x
```

```python
import math
import numpy as np
from contextlib import ExitStack
import concourse.bass as bass
import concourse.mybir as mybir
from concourse.bass_utils import run_bass_kernel_spmd

F32 = mybir.dt.float32
BF16 = mybir.dt.bfloat16
ALU = mybir.AluOpType
AF = mybir.ActivationFunctionType

D = 4096
SEQ = 4096
OWN = 2048
NEG = -30000.0
DILS = (1, 4, 16)
ZST = [128 * i for i in range(48)] + [6144, 6240, 6336, 6464]
ZM = [128] * 48 + [96, 96, 128, 128]
A_END = 4608
R_END = 4608 + 6592


class Buf:
    __slots__ = ("w", "r")

    def __init__(self):
        self.w = {}
        self.r = {}


class Sched:
    ENG = ["pe", "act", "dve", "pool", "sp"]
    NSLOT = 12

    EPOCH = 8000

    def __init__(self, nc, es):
        self.nc = nc
        self.es = es
        self.e = dict(pe=nc.tensor, act=nc.scalar, dve=nc.vector, pool=nc.gpsimd, sp=nc.sync)
        self.sem = {}
        self.slots = {}
        for q in ["sp", "act", "pool"]:
            for i in range(self.NSLOT):
                k = "d%s%d" % (q, i)
                self.slots[k] = 0
        self.slot_rr = {"sp": 0, "act": 0, "pool": 0}
        self.cnt = {k: 0 for k in self.ENG}
        self.seen = {k: {} for k in self.ENG}

    def semof(self, k, v):
        ep = (v - 1) // self.EPOCH
        kk = (k, ep)
        if kk not in self.sem:
            self.sem[kk] = self.es.enter_context(self.nc.semaphore("s_%s_%d" % (k, ep)))
        return self.sem[kk], v - ep * self.EPOCH

    def _wait(self, en, deps):
        need = {}
        for (k, v) in deps:
            if k == "pe" and en == "pe":
                continue
            if v > need.get(k, 0):
                need[k] = v
        sn = self.seen[en]
        for k, v in need.items():
            if sn.get(k, 0) < v:
                sm, rv = self.semof(k, v)
                self.e[en].wait_ge(sm, rv)
                sn[k] = v

    def _deps(self, reads, writes):
        deps = []
        for b in reads:
            deps.extend(b.w.items())
        for b in writes:
            deps.extend(b.w.items())
            deps.extend(b.r.items())
        return deps

    def _mark(self, tok, reads, writes):
        k, v = tok
        for b in reads:
            if b.r.get(k, 0) < v:
                b.r[k] = v
        for b in writes:
            b.w = {k: v}
            b.r = {}

    def op(self, en, fn, reads=(), writes=()):
        self._wait(en, self._deps(reads, writes))
        ins = fn(self.e[en])
        self.cnt[en] += 1
        sm, _ = self.semof(en, self.cnt[en])
        ins.then_inc(sm, 1)
        self._mark((en, self.cnt[en]), reads, writes)

    def dma(self, out, in_, reads=(), writes=(), q="sp", acc=(), **kw):
        i = self.slot_rr[q]
        self.slot_rr[q] = (i + 1) % self.NSLOT
        k = "d%s%d" % (q, i)
        deps = self._deps(reads, writes)
        if self.slots[k] > 0:
            deps.append((k, self.slots[k]))
        self._wait(q, deps)
        ins = self.e[q].dma_start(out=out, in_=in_, **kw)
        self.slots[k] += 16
        sm, _ = self.semof(k, self.slots[k])
        ins.then_inc(sm, 16)
        self._mark((k, self.slots[k]), reads, writes)
        for b in acc:
            b.w[k] = self.slots[k]

    def drain(self, en="sp"):
        for k, val in self.slots.items():
            if val > 0:
                sm, rv = self.semof(k, val)
                self.e[en].wait_ge(sm, rv)


def t5_bucket_np(dist):
    dist = np.asarray(dist, dtype=np.int64)
    max_exact = 16
    d_f = np.maximum(dist, 1).astype(np.float32)
    large = max_exact + (np.log(d_f / np.float32(max_exact)) / np.float32(math.log(2048 / max_exact))
                         * np.float32(32 - max_exact)).astype(np.int32)
    large = np.minimum(large, 31)
    return np.where(dist < max_exact, dist, large)


def host_consts():
    c = {}
    c["ident"] = np.eye(128, dtype=np.float32)
    s = np.arange(64)[:, None]
    t = np.arange(64)[None, :]
    su = (s < t).astype(np.float32)
    iu = (s <= t).astype(np.float32)
    sl = (s > t).astype(np.float32)
    z = np.zeros((64, 64), np.float32)
    mLT = np.block([[su, z], [z, su]])
    mRB = np.block([[z, iu], [iu, z]])
    mAK = np.block([[su, iu], [iu, su]])
    mL = np.block([[sl, z], [z, sl]])
    c["masks"] = np.stack([mLT, mRB, mAK, mL], axis=1).astype(np.float32)
    sm = np.ones((128, 1024), np.float32)
    sm[:, ::64] = 0.0
    c["scanmask"] = sm
    bo = np.zeros((128, 128), np.float32)
    bo[:64, :64] = 1.0
    bo[64:, 64:] = 1.0
    c["blkones"] = bo
    oh = np.zeros((33, 3, 383), np.float32)
    for g, d in enumerate(DILS):
        for u in range(383):
            rel = u - 127
            if 0 <= rel <= 128:
                oh[int(t5_bucket_np(rel * d)), g, u] = 1.0
            else:
                oh[32, g, u] = NEG
    c["oh"] = oh
    return c


def build_program(stop=None, debug=False):
    nc = bass.Bass("TRN2", target_bir_lowering=False)

    def din(name, shape, dt=F32):
        return nc.dram_tensor(name, list(shape), dt, kind="ExternalInput").ap()

    def dscr(name, shape, dt):
        return nc.dram_tensor(name, list(shape), dt, kind="ExternalOutput" if (debug and name in debug) else "Internal").ap()

    x_d = din("x", [SEQ, D])
    p_d = din("p", [OWN, 256])
    pm_d = din("pm", [128, 1])
    w_in = din("w_in", [D, 19392])
    w_attn_up = din("w_attn_up", [512, D])
    w_decay_up = din("w_decay_up", [96, 2048])
    w_aaa_up = din("w_aaa_up", [96, 2048])
    w_gate_up = din("w_gate_up", [256, 2048])
    w_rwkv_up = din("w_rwkv_up", [2048, D])
    w_out = din("w_out", [D, D])
    w_mlp_in = din("w_mlp_in", [D, 16384])
    w_mlp_out = din("w_mlp_out", [16384, D])
    w_ple_gate = din("w_ple_gate", [D, D])
    w_ple_proj = din("w_ple_proj", [256, D])
    norm_mix = din("norm_mix", [1, D])
    qkg_d = din("qkg", [128, 2])
    rb33_d = din("rb33", [33, 12])
    mixT_d = din("mixT", [128, 52])
    rwp_d = din("rwp", [128, 5, 16])
    gnw_d = din("gn_w", [1, 2048])
    gnb_d = din("gn_b", [1, 2048])
    nmlp_d = din("nmlpT", [128, 32])
    nple_d = din("npleT", [128, 32])
    ident_d = din("ident", [128, 128])
    masks_d = din("masks", [128, 4, 128])
    scanmask_d = din("scanmask", [128, 1024])
    blkones_d = din("blkones", [128, 128])
    oh_d = din("oh", [33, 3, 383])
    out_d = nc.dram_tensor("out", [OWN, D], F32, kind="ExternalOutput").ap()

    KT = dscr("KT", [12, 128, SEQ], BF16)
    VT = dscr("VT", [12, 128, SEQ], BF16)
    QT = dscr("QT", [12, 128, OWN], BF16)
    ZT = dscr("ZT", [6656, SEQ], F32)
    GT = dscr("GT", [8192, OWN], BF16)
    OS = dscr("OS", [OWN, 12, 129], F32)
    RWT = dscr("RWT", [2048, OWN], BF16)
    GS = dscr("GS", [12, 130, 383], F32)
    B_KT, B_VT, B_QT, B_ZT, B_GT, B_OS, B_RWT, B_GS = [Buf() for _ in range(8)]
    tail_blocks = []
    for blk_ in range(32):
        tail_blocks.append((w_attn_up, 0, 512, blk_ * 128))
        tail_blocks.append((w_rwkv_up, 0, 2048, blk_ * 128))
    for blk_ in range(32):
        tail_blocks.append((w_out, 0, D, blk_ * 128))
    for ch_ in range(16):
        for fb_ in range(8):
            tail_blocks.append((w_mlp_in, 0, D, ch_ * 1024 + fb_ * 128))
        for blk_ in range(32):
            tail_blocks.append((w_mlp_out, ch_ * 1024, 1024, blk_ * 128))
    for blk_ in range(32):
        tail_blocks.append((w_ple_gate, 0, D, blk_ * 128))
        tail_blocks.append((w_ple_proj, 0, 256, blk_ * 128))
    wb_off = []
    CAP_ = 900000
    sizes_ = [0]
    for (_, _, krows_, _) in tail_blocks:
        n_ = ((krows_ + 127) // 128) * 128
        if sizes_[-1] + n_ > CAP_:
            sizes_.append(0)
        wb_off.append((len(sizes_) - 1, sizes_[-1]))
        sizes_[-1] += n_
    WBs = [dscr("WB%d" % i_, [128, sz_], BF16) for i_, sz_ in enumerate(sizes_)]
    B_WB = Buf()
    B_out = Buf()

    top = ExitStack()
    with top:
        S = Sched(nc, top)

        def SB(es, name, shape, dt):
            return es.enter_context(nc.sbuf_tensor("sb_" + name, list(shape), dt)), Buf()

        def PS(es, name, shape, dt=F32):
            return es.enter_context(nc.psum_tensor("ps_" + name, list(shape), dt)), Buf()

        identf, Bidf = SB(top, "identf", [128, 128], F32)
        identb, Bidb = SB(top, "identb", [128, 128], BF16)
        S.dma(identf[:], ident_d[:], writes=[Bidf])
        S.op("dve", lambda e: e.tensor_copy(identb[:], identf[:]), reads=[Bidf], writes=[Bidb])
        onesb, Bones = SB(top, "onesb", [128, 128], BF16)
        S.op("dve", lambda e: e.memset(onesb[:], 1.0), writes=[Bones])

        wst = [SB(top, "wst%d" % i, [128, 2048], F32) for i in range(4)]
        wbf = [SB(top, "wbf%d" % i, [128, 4096], BF16) for i in range(2)]
        wctr = [0, 0, 0]

        def load_w(w_ap, r0, krows, c0, m, dst=None):
            kc = (krows + 127) // 128
            if dst is not None:
                wb, Bwb = dst
            else:
                wb, Bwb = wbf[wctr[1] % 2]
                wctr[1] += 1
            pieces = [(0, kc)] if kc <= 16 else [(0, 16), (16, kc)]
            first = True
            for (k0, k1) in pieces:
                st, Bst = wst[wctr[0] % 3]
                q = "sp" if wctr[0] % 2 == 0 else "act"
                wctr[0] += 1
                nk = k1 - k0
                src = w_ap[r0 + k0 * 128:r0 + k1 * 128, c0:c0 + m].rearrange("(k p) c -> p k c", p=128)
                S.dma(st[:, 0:nk * m].rearrange("p (k c) -> p k c", k=nk), src, writes=[Bst], q=q)
                ce = "act" if wctr[0] % 2 == 0 else "dve"
                if ce == "act":
                    fn = lambda e: e.copy(wb[:, k0 * m:k1 * m], st[:, 0:nk * m])
                else:
                    fn = lambda e: e.tensor_copy(wb[:, k0 * m:k1 * m], st[:, 0:nk * m])
                if first:
                    S.op(ce, fn, reads=[Bst], writes=[Bwb])
                else:
                    S.op(ce, fn, reads=[Bst, Bwb], writes=[])
                    Bwb.w[ce] = S.cnt[ce]
                first = False
            return wb, Bwb, kc

        class WQ:
            def __init__(self, specs):
                self.specs = specs
                self.pos = 0
                self.dpos = 0
                self.cpos = 0
                self.staged = {}
                self.tiles = {}

            def _dma(self):
                i = self.dpos
                (w_ap, r0, krows, c0, m, dst) = self.specs[i]
                kc = (krows + 127) // 128
                pieces = [(0, kc)] if kc <= 16 else [(0, 16), (16, kc)]
                lst = []
                for (k0, k1) in pieces:
                    st, Bst = wst[wctr[0] % 4]
                    wctr[0] += 1
                    nk = k1 - k0
                    src = w_ap[r0 + k0 * 128:r0 + k1 * 128, c0:c0 + m].rearrange("(k p) c -> p k c", p=128)
                    S.dma(st[:, 0:nk * m].rearrange("p (k c) -> p k c", k=nk), src, writes=[Bst], q="sp")
                    lst.append((st, Bst, k0, k1))
                self.staged[i] = lst
                self.dpos += 1

            def _cast(self):
                i = self.cpos
                (w_ap, r0, krows, c0, m, dst) = self.specs[i]
                kc = (krows + 127) // 128
                if dst is not None:
                    wb, Bwb = dst
                else:
                    wb, Bwb = wbf[wctr[1] % 2]
                    wctr[1] += 1
                first = True
                for (st, Bst, k0, k1) in self.staged.pop(i):
                    nk = k1 - k0
                    wctr[2] += 1
                    ce = "act" if wctr[2] % 2 == 0 else "dve"
                    if ce == "act":
                        fn = lambda e: e.copy(wb[:, k0 * m:k1 * m], st[:, 0:nk * m])
                    else:
                        fn = lambda e: e.tensor_copy(wb[:, k0 * m:k1 * m], st[:, 0:nk * m])
                    if first:
                        S.op(ce, fn, reads=[Bst], writes=[Bwb])
                    else:
                        S.op(ce, fn, reads=[Bst, Bwb], writes=[])
                        Bwb.w[ce] = S.cnt[ce]
                    first = False
                self.tiles[i] = (wb, Bwb, kc)
                self.cpos += 1

            def next(self):
                n = len(self.specs)
                while self.cpos <= self.pos:
                    if self.dpos <= self.cpos:
                        self._dma()
                    self._cast()
                while self.dpos < min(n, self.pos + 3):
                    self._dma()
                r = self.tiles.pop(self.pos)
                self.pos += 1
                return r

            def prefetch(self):
                n = len(self.specs)
                if self.cpos < n and self.cpos <= self.pos:
                    if self.dpos <= self.cpos:
                        self._dma()
                    self._cast()

        esA = ExitStack()
        with esA:
            gmix, Bgmix = SB(esA, "gmix", [128, D], F32)
            S.dma(gmix[:], norm_mix.partition_broadcast(128), writes=[Bgmix])
            qkg, Bqkg = SB(esA, "qkg", [128, 2], F32)
            S.dma(qkg[:], qkg_d[:], writes=[Bqkg])
            S.op("dve", lambda e: e.tensor_scalar(qkg[:, 0:1], qkg[:, 0:1], 128.0 ** -0.5, None, ALU.mult), reads=[Bqkg], writes=[Bqkg])
            mixT, Bmix = SB(esA, "mixT", [128, 52], F32)
            omix, Bomix = SB(esA, "omix", [128, 52], F32)
            S.dma(mixT[:], mixT_d[:], writes=[Bmix])
            S.op("dve", lambda e: e.tensor_scalar(omix[:], mixT[:], -1.0, 1.0, ALU.mult, ALU.add), reads=[Bmix], writes=[Bomix])
            zlast, Bzl = SB(esA, "zlast", [128, 52], F32)
            S.op("dve", lambda e: e.memset(zlast[:], 0.0), writes=[Bzl])
            hT, BhT = SB(esA, "hT", [128, 32, 1024], BF16)
            xin = [SB(esA, "xin%d" % i, [128, D], F32) for i in range(1)]
            hb, Bhb = SB(esA, "hb", [128, D], BF16)
            ss, Bss = SB(esA, "ss", [128, 4], F32)
            ptr = [PS(esA, "ptr%d" % i, [128, 1024], BF16) for i in range(2)]
            pmm = [PS(esA, "pmm%d" % i, [128, 512]) for i in range(3)]
            pn = [PS(esA, "pn%d" % i, [128, 512]) for i in range(2)]
            sqb, Bsqb = SB(esA, "sqb", [128, 512], BF16)
            rsb, Brsb = SB(esA, "rsb", [128, 512], F32)
            stg16 = [SB(esA, "stg16_%d" % i, [128, 1024], BF16) for i in range(2)]
            zbuf, Bzbuf = SB(esA, "zbuf", [128, 1025], F32)
            ztmp, Bztmp = SB(esA, "ztmp", [128, 1024], F32)
            zs = [SB(esA, "zs%d" % i, [128, 1024], F32) for i in range(2)]
            ctr = {"ev": 0, "mm": 0, "pn": 0, "s16": 0, "zs": 0}

            def a_blocks(qi):
                blocks = []
                for h in range(12):
                    blocks.append(("k", 1536 + h * 128, 128, h))
                for h in range(12):
                    blocks.append(("v", 3072 + h * 128, 128, h))
                for b in range(52):
                    blocks.append(("z", A_END + ZST[b], ZM[b], b))
                if qi >= 2:
                    for h in range(12):
                        blocks.append(("q", h * 128, 128, h))
                    for b in range(64):
                        blocks.append(("g", R_END + b * 128, 128, b))
                return blocks
            wqA = WQ([(w_in, 0, D, c0_, m_, None) for qi_ in range(4) for (_, c0_, m_, _) in a_blocks(qi_)])

            for qi in range(4):
                own = qi >= 2
                for sub in range(8):
                    xt, Bxt = xin[0]
                    t0 = qi * 1024 + sub * 128
                    S.dma(xt[:], x_d[t0:t0 + 128, :], writes=[Bxt], q="sp" if sub % 2 == 0 else "act")
                    S.op("dve", lambda e: e.memset(ss[:, 0:1], 0.0), writes=[Bss])
                    S.op("act", lambda e: e.activation(hb[:], xt[:], AF.Square, accum_out=ss[:, 0:1]), reads=[Bxt, Bss], writes=[Bhb, Bss])
                    S.op("act", lambda e: e.activation(ss[:, 1:2], ss[:, 0:1], AF.Ln, bias=1e-6, scale=1.0 / D), reads=[Bss], writes=[Bss])
                    S.op("act", lambda e: e.activation(ss[:, 2:3], ss[:, 1:2], AF.Exp, scale=-0.5), reads=[Bss], writes=[Bss])
                    S.op("dve", lambda e: e.scalar_tensor_tensor(hb[:], xt[:], ss[:, 2:3], gmix[:], ALU.mult, ALU.mult), reads=[Bxt, Bss, Bgmix], writes=[Bhb])
                    for g8 in range(4):
                        pt, Bpt = ptr[g8 % 2]
                        for j in range(8):
                            kc = g8 * 8 + j
                            S.op("pe", lambda e: e.transpose(pt[:, j * 128:(j + 1) * 128], hb[:, kc * 128:(kc + 1) * 128], identb[:]), reads=[Bhb, Bidb], writes=[Bpt])
                        en = "act" if g8 % 2 == 0 else "dve"
                        dst = hT[:, g8 * 8:(g8 + 1) * 8, sub * 128:(sub + 1) * 128]
                        srcp = pt[:].rearrange("p (k t) -> p k t", k=8)
                        if en == "act":
                            S.op("act", lambda e: e.copy(dst, srcp), reads=[Bpt], writes=[BhT])
                        else:
                            S.op("dve", lambda e: e.tensor_copy(dst, srcp), reads=[Bpt], writes=[BhT])
                blocks = []
                for h in range(12):
                    blocks.append(("k", 1536 + h * 128, 128, h))
                for h in range(12):
                    blocks.append(("v", 3072 + h * 128, 128, h))
                for b in range(52):
                    blocks.append(("z", A_END + ZST[b], ZM[b], b))
                if own:
                    for h in range(12):
                        blocks.append(("q", h * 128, 128, h))
                    for b in range(64):
                        blocks.append(("g", R_END + b * 128, 128, b))
                blocks = a_blocks(qi)
                for bi, (kind, c0, m, idx) in enumerate(blocks):
                    wb, Bwb, _ = wqA.next()
                    if kind in ("k", "q", "v", "g"):
                        st, Bstg = stg16[ctr["s16"] % 2]
                        ctr["s16"] += 1
                    for half in range(2):
                        pm_, Bpm = pmm[ctr["mm"] % 3]
                        ctr["mm"] += 1
                        for kc in range(32):
                            S.op("pe", lambda e: e.matmul(pm_[0:m, :], wb[:, kc * m:(kc + 1) * m], hT[:, kc, half * 512:(half + 1) * 512], start=(kc == 0), stop=(kc == 31)), reads=[Bwb, BhT], writes=[Bpm])
                        if kind in ("k", "q"):
                            d = DILS[idx // 4]
                            S.op("act", lambda e: e.activation(sqb[:], pm_[:], AF.Square), reads=[Bpm], writes=[Bsqb])
                            pn_, Bpn = pn[ctr["pn"] % 2]
                            ctr["pn"] += 1
                            S.op("pe", lambda e: e.matmul(pn_[:], onesb[:], sqb[:], start=True, stop=True), reads=[Bones, Bsqb], writes=[Bpn])
                            S.op("act", lambda e: e.activation(rsb[:], pn_[:], AF.Ln, bias=1e-6, scale=1.0 / 128), reads=[Bpn], writes=[Brsb])
                            S.op("act", lambda e: e.activation(rsb[:], rsb[:], AF.Exp, scale=-0.5), reads=[Brsb], writes=[Brsb])
                            gcol = qkg[:, 0:1] if kind == "q" else qkg[:, 1:2]
                            n_i = 512 // d
                            dst = st[:].rearrange("p (r i) -> p r i", r=d)[:, :, half * n_i:(half + 1) * n_i]
                            S.op("dve", lambda e: e.scalar_tensor_tensor(dst, pm_[:].rearrange("p (i r) -> p r i", r=d), gcol, rsb[:].rearrange("p (i r) -> p r i", r=d), ALU.mult, ALU.mult), reads=[Bpm, Bqkg, Brsb], writes=[Bstg])
                        elif kind == "v":
                            d = DILS[idx // 4]
                            n_i = 512 // d
                            dst = st[:].rearrange("p (r i) -> p r i", r=d)[:, :, half * n_i:(half + 1) * n_i]
                            S.op("act", lambda e: e.copy(dst, pm_[:].rearrange("p (i r) -> p r i", r=d)), reads=[Bpm], writes=[Bstg])
                        elif kind == "g":
                            S.op("act", lambda e: e.activation(st[:, half * 512:(half + 1) * 512], pm_[:], AF.Sigmoid), reads=[Bpm], writes=[Bstg])
                        else:
                            S.op("act", lambda e: e.copy(zbuf[0:m, 1 + half * 512:1 + (half + 1) * 512], pm_[0:m, :]), reads=[Bpm], writes=[Bzbuf])
                        if half == 0:
                            wqA.prefetch()
                    if kind in ("k", "v", "q"):
                        d = DILS[idx // 4]
                        if kind == "q":
                            dd = QT[idx].rearrange("e (r i) -> e r i", r=d)[:, :, (qi - 2) * (1024 // d):(qi - 1) * (1024 // d)]
                            Bd = B_QT
                        else:
                            base = KT if kind == "k" else VT
                            dd = base[idx].rearrange("e (r i) -> e r i", r=d)[:, :, qi * (1024 // d):(qi + 1) * (1024 // d)]
                            Bd = B_KT if kind == "k" else B_VT
                        S.dma(dd, st[:].rearrange("p (r i) -> p r i", r=d), reads=[Bstg], acc=[Bd])
                    elif kind == "g":
                        S.dma(GT[idx * 128:(idx + 1) * 128, (qi - 2) * 1024:(qi - 1) * 1024], st[:], reads=[Bstg], acc=[B_GT])
                    else:
                        b = idx
                        S.op("dve", lambda e: e.tensor_copy(zbuf[0:m, 0:1], zlast[0:m, b:b + 1]), reads=[Bzl], writes=[Bzbuf])
                        S.op("pool", lambda e: e.tensor_scalar(ztmp[0:m, :], zbuf[0:m, 0:1024], mixT[0:m, b:b + 1], None, ALU.mult), reads=[Bzbuf, Bmix], writes=[Bztmp])
                        zt, Bzt = zs[ctr["zs"] % 2]
                        ctr["zs"] += 1
                        S.op("dve", lambda e: e.scalar_tensor_tensor(zt[0:m, :], zbuf[0:m, 1:1025], omix[0:m, b:b + 1], ztmp[0:m, :], ALU.mult, ALU.add), reads=[Bzbuf, Bomix, Bztmp], writes=[Bzt])
                        S.op("dve", lambda e: e.tensor_copy(zlast[0:m, b:b + 1], zbuf[0:m, 1024:1025]), reads=[Bzbuf], writes=[Bzl])
                        S.dma(ZT[ZST[b]:ZST[b] + m, qi * 1024:(qi + 1) * 1024], zt[0:m, :], reads=[Bzt], acc=[B_ZT])


        esB = ExitStack()
        if stop == "A":
            S.drain("sp")
            return nc
        with esB:
            rb33, Brb = SB(esB, "rb33", [33, 12], F32)
            oh, Boh = SB(esB, "oh", [33, 3, 383], F32)
            S.dma(rb33[:], rb33_d[:], writes=[Brb])
            S.dma(oh[:], oh_d[:], writes=[Boh])
            pmc, Bpmc = SB(esB, "pmc", [128, 1], F32)
            S.dma(pmc[:], pm_d[:], writes=[Bpmc])
            gsb, Bgsb = SB(esB, "gsb", [4, 383], F32)
            pg, Bpg = PS(esB, "pg", [4, 383])
            for g in range(3):
                S.op("pe", lambda e: e.matmul(pg[:], rb33[:, g * 4:(g + 1) * 4], oh[:, g, :], start=True, stop=True), reads=[Brb, Boh], writes=[Bpg])
                S.op("dve", lambda e: e.tensor_copy(gsb[:], pg[:]), reads=[Bpg], writes=[Bgsb])
                srcb = gsb[:].unsqueeze(1).to_broadcast([4, 130, 383])
                S.dma(GS[g * 4:(g + 1) * 4, :, :], srcb, reads=[Bgsb], acc=[B_GS])
            biasT, Bbias = SB(esB, "biasT", [128, 12, 2, 128], F32)
            for h in range(12):
                for tl, cc in ((0, 255), (1, 127)):
                    src = bass.AP(GS.tensor, GS[h].offset + cc, [[382, 128], [1, 128]])
                    S.dma(biasT[:, h, tl, :], src, reads=[B_GS], writes=[Bbias])
            Ksb = [SB(esB, "Ksb%d" % i, [128, SEQ], BF16) for i in range(2)]
            Vsb = [SB(esB, "Vsb%d" % i, [128, SEQ], BF16) for i in range(2)]
            Qsb = [SB(esB, "Qsb%d" % i, [128, OWN], BF16) for i in range(2)]
            Vtok = [SB(esB, "Vtok%d" % i, [128, 32, 130], BF16) for i in range(2)]
            for i in range(2):
                S.op("dve", lambda e: e.memset(Vtok[i][0][:], 1.0), writes=[Vtok[i][1]])
            pvt = [PS(esB, "pvt%d" % i, [128, 128], BF16) for i in range(2)]
            pss = [PS(esB, "pss%d" % i, [128, 2, 128]) for i in range(2)]
            pso = [PS(esB, "pso%d" % i, [128, 129]) for i in range(2)]
            ssb = [SB(esB, "ssb%d" % i, [128, 2, 128], F32) for i in range(2)]
            esb = [SB(esB, "esb%d" % i, [128, 2, 128], BF16) for i in range(2)]
            osb = [SB(esB, "osb%d" % i, [128, 129], F32) for i in range(3)]
            it = 0
            for h in range(12):
                d = DILS[h // 4]
                Lc = SEQ // d
                nbc = Lc // 128
                nb0 = nbc // 2
                K_, BK = Ksb[h % 2]
                V_, BV = Vsb[h % 2]
                Q_, BQ = Qsb[h % 2]
                Vt, BVt = Vtok[h % 2]
                S.dma(K_[:], KT[h], reads=[B_KT], writes=[BK])
                S.dma(V_[:], VT[h], reads=[B_VT], writes=[BV], q="act")
                S.dma(Q_[:], QT[h], reads=[B_QT], writes=[BQ])
                nkb = nb0 + 1
                for r in range(d):
                    for kb in range(nb0 - 1, nbc):
                        vi = r * nkb + (kb - (nb0 - 1))
                        pv, Bpv = pvt[vi % 2]
                        S.op("pe", lambda e: e.transpose(pv[:], V_[:, r * Lc + kb * 128:r * Lc + (kb + 1) * 128], identb[:]), reads=[BV, Bidb], writes=[Bpv])
                        if vi % 2 == 0:
                            S.op("act", lambda e: e.copy(Vt[:, vi, 0:128], pv[:]), reads=[Bpv], writes=[BVt])
                        else:
                            S.op("dve", lambda e: e.tensor_copy(Vt[:, vi, 0:128], pv[:]), reads=[Bpv], writes=[BVt])
                for r in range(d):
                    for n in range(nb0, nbc):
                        ps_, Bps = pss[it % 2]
                        po, Bpo = pso[it % 2]
                        s_, Bs = ssb[it % 2]
                        e_, Be = esb[it % 2]
                        o_, Bo = osb[it % 3]
                        it += 1
                        qt = Q_[:, r * (Lc // 2) + (n - nb0) * 128:r * (Lc // 2) + (n - nb0 + 1) * 128]
                        for tl in range(2):
                            kb = n - 1 + tl
                            S.op("pe", lambda e: e.matmul(ps_[:, tl, :], K_[:, r * Lc + kb * 128:r * Lc + (kb + 1) * 128], qt, start=True, stop=True), reads=[BK, BQ], writes=[Bps])
                        S.op("dve", lambda e: e.tensor_tensor(s_[:], ps_[:], biasT[:, h], ALU.add), reads=[Bps, Bbias], writes=[Bs])
                        if n == nb0:
                            S.op("act", lambda e: e.activation(e_[:, 0, :], s_[:, 0, :], AF.Exp, bias=pmc[:, 0:1]), reads=[Bs, Bpmc], writes=[Be])
                            S.op("act", lambda e: e.activation(e_[:, 1, :], s_[:, 1, :], AF.Exp), reads=[Bs], writes=[Be])
                        else:
                            S.op("act", lambda e: e.activation(e_[:], s_[:], AF.Exp), reads=[Bs], writes=[Be])
                        for tl in range(2):
                            vi = r * nkb + (n - 1 + tl - (nb0 - 1))
                            S.op("pe", lambda e: e.matmul(po[:], e_[:, tl, :], Vt[:, vi, 0:129], start=(tl == 0), stop=(tl == 1)), reads=[Be, BVt], writes=[Bpo])
                        S.op("dve", lambda e: e.tensor_copy(o_[:], po[:]), reads=[Bpo], writes=[Bo])
                        tok0 = (n * 128) * d + r - OWN
                        dst = bass.AP(OS.tensor, (tok0 * 12 + h) * 129, [[d * 12 * 129, 128], [1, 129]])
                        S.dma(dst, o_[:], reads=[Bo], acc=[B_OS])

        esC = ExitStack()
        if stop == "B":
            S.drain("sp")
            return nc
        with esC:
            masks, Bmk = SB(esC, "masks", [128, 4, 128], F32)
            S.dma(masks[:], masks_d[:], writes=[Bmk])
            scm, Bscm = SB(esC, "scm", [128, 1024], F32)
            S.dma(scm[:], scanmask_d[:], writes=[Bscm])
            blk1, Bblk = SB(esC, "blk1", [128, 128], F32)
            S.dma(blk1[:], blkones_d[:], writes=[Bblk])
            blk1b, Bblkb = SB(esC, "blk1b", [128, 128], BF16)
            S.op("dve", lambda e: e.tensor_copy(blk1b[:], blk1[:]), reads=[Bblk], writes=[Bblkb])
            rwp, Brwp = SB(esC, "rwp", [128, 5, 16], F32)
            S.dma(rwp[:], rwp_d[:], writes=[Brwp])
            nrw, Bnrw = SB(esC, "nrw", [128, 2, 16], F32)
            S.op("dve", lambda e: e.tensor_scalar(nrw[:, 0, :], rwp[:, 0, :], -1.0, None, ALU.mult), reads=[Brwp], writes=[Bnrw])
            S.op("dve", lambda e: e.tensor_scalar(nrw[:, 1, :], rwp[:, 3, :], -1.0, 1.0, ALU.mult, ALU.add), reads=[Brwp], writes=[Bnrw])
            txw, Btxw = SB(esC, "txw", [96, SEQ], BF16)
            xab, Bxab = SB(esC, "xab", [96, SEQ], BF16)
            sxg, Bsxg = SB(esC, "sxg", [128, 2, SEQ], BF16)
            f32n = ["r", "k", "v", "ew", "cle", "t0", "t1", "t2", "asig", "kkn", "k2", "bb"]
            F = {n: SB(esC, "f_" + n, [128, 1024], F32) for n in f32n}
            ltmp = [F["t0"], F["t1"]]
            li = 0
            for sgi in range(4):
                for (row0, m, kind) in ((6144, 96, "w"), (6240, 96, "a"), (6336, 128, "g0"), (6464, 128, "g1")):
                    lt, Blt = ltmp[li % 2]
                    li += 1
                    S.dma(lt[0:m, :], ZT[row0:row0 + m, sgi * 1024:(sgi + 1) * 1024], reads=[B_ZT], writes=[Blt], q="sp" if li % 2 else "act")
                    sl = slice(sgi * 1024, (sgi + 1) * 1024)
                    if kind == "w":
                        S.op("act", lambda e: e.activation(txw[:, sl], lt[0:96, :], AF.Tanh), reads=[Blt], writes=[Btxw])
                    elif kind == "a":
                        S.op("dve", lambda e: e.tensor_copy(xab[:, sl], lt[0:96, :]), reads=[Blt], writes=[Bxab])
                    else:
                        gi = 0 if kind == "g0" else 1
                        S.op("act", lambda e: e.activation(sxg[:, gi, sl], lt[:, :], AF.Sigmoid), reads=[Blt], writes=[Bsxg])
            wdec, Bwdec = SB(esC, "wdec", [96, 2048], BF16)
            waaa, Bwaaa = SB(esC, "waaa", [96, 2048], BF16)
            wgat, Bwgat = SB(esC, "wgat", [128, 2, 2048], BF16)
            for (dst, Bd, src, rows) in ((wdec, Bwdec, w_decay_up, 96), (waaa, Bwaaa, w_aaa_up, 96)):
                for cch in range(2):
                    st, Bst = wst[cch]
                    S.dma(st[0:rows, 0:1024], src[:, cch * 1024:(cch + 1) * 1024], writes=[Bst])
                    S.op("pool", lambda e: e.tensor_copy(dst[:, cch * 1024:(cch + 1) * 1024], st[0:rows, 0:1024]), reads=[Bst], writes=[Bd])
            for kc in range(2):
                for cch in range(2):
                    st, Bst = wst[cch]
                    S.dma(st[:, 0:1024], w_gate_up[kc * 128:(kc + 1) * 128, cch * 1024:(cch + 1) * 1024], writes=[Bst])
                    S.op("pool", lambda e: e.tensor_copy(wgat[:, kc, cch * 1024:(cch + 1) * 1024], st[:, 0:1024]), reads=[Bst], writes=[Bwgat])

            b16n = ["bt", "kt", "btc", "ktc", "vb", "rk"]
            Bt = {n: SB(esC, "b_" + n, [128, 1024], BF16) for n in b16n}
            ARt, BARt = SB(esC, "ARt", [128, 16, 128], BF16)
            wcs, Bwcs = SB(esC, "wcs", [128, 16], F32)
            GNW, BGNW = SB(esC, "GNW", [128, 64], F32)
            GNB, BGNB = SB(esC, "GNB", [128, 64], F32)
            S32, BS32 = SB(esC, "S32", [128, 64], F32)
            Sbf, BSbf = SB(esC, "Sbf", [128, 64], BF16)
            rwTs, BrwTs = SB(esC, "rwTs", [128, 1024], BF16)

            def rot(name, shape, dt, n):
                return [SB(esC, "%s%d" % (name, i), shape, dt) for i in range(n)]
            ARB_ = rot("ARB", [128, 4, 128], BF16, 2)
            AK_ = rot("AK", [128, 4, 128], BF16, 2)
            N_ = rot("N", [128, 4, 128], BF16, 4)
            NT_ = rot("NT", [128, 4, 128], BF16, 4)
            TT_ = rot("TT", [128, 4, 128], BF16, 8)
            TOK_ = rot("TOK", [128, 4, 192], BF16, 2)
            Xb_ = rot("Xb", [128, 64], BF16, 2)
            Ub_ = rot("Ub", [128, 64], BF16, 2)
            ysb_ = rot("ysb", [128, 64], F32, 2)
            ycn_ = rot("ycn", [128, 64], F32, 2)
            yjk_ = rot("yjk", [128, 64], F32, 2)
            st4_ = rot("st4", [128, 8], F32, 2)
            rwb_ = rot("rwb", [128, 64], BF16, 2)
            ps1, Bps1 = PS(esC, "ps1", [128, 512])
            ps2, Bps2 = PS(esC, "ps2", [128, 512])
            ps3, Bps3 = PS(esC, "ps3", [128, 512])
            pN, BpN = PS(esC, "pN", [128, 512])
            pNT, BpNT = PS(esC, "pNT", [128, 512])
            pT, BpT = PS(esC, "pT", [128, 512])
            pxu, Bpxu = PS(esC, "pxu", [128, 512])
            pys, Bpys = PS(esC, "pys", [128, 512])
            plo, Bplo = pT, BpT
            zb, Bzb = SB(esC, "zb", [128, 512], BF16)
            S.op("dve", lambda e: e.memset(zb[:], 0.0), writes=[Bzb])
            S.op("pe", lambda e: e.matmul(ps3[:], zb[:, 0:128], zb[:], start=True, stop=True), reads=[Bzb], writes=[Bps3])

            def Fv(n):
                return F[n][0], F[n][1]

            def precast_gen():
                pi_ = 0
                for bi_, (w_ap, r0, krows, c0_) in enumerate(tail_blocks):
                    kc = (krows + 127) // 128
                    pieces = [(0, kc)] if kc <= 16 else [(0, 16), (16, kc)]
                    for (k0, k1) in pieces:
                        nk = k1 - k0
                        st, Bst = wst[pi_ % 4]
                        ob, Bob = wbf[pi_ % 2]
                        pi_ += 1
                        src = w_ap[r0 + k0 * 128:r0 + k1 * 128, c0_:c0_ + 128].rearrange("(k p) c -> p k c", p=128)
                        S.dma(st[:, 0:nk * 128].rearrange("p (k c) -> p k c", k=nk), src, writes=[Bst], q="sp")
                        S.op("pool", lambda e: e.tensor_copy(ob[:, 0:nk * 128], st[:, 0:nk * 128]), reads=[Bst], writes=[Bob])
                        S.dma(WBs[wb_off[bi_][0]][:, wb_off[bi_][1] + k0 * 128:wb_off[bi_][1] + k1 * 128], ob[:, 0:nk * 128], reads=[Bob], acc=[B_WB], q="pool")
                        yield None
            precast = precast_gen()

            for hp in range(16):
                S.dma(GNW[0:64, :], gnw_d[0:1, hp * 128:hp * 128 + 64].partition_broadcast(64), writes=[BGNW])
                S.dma(GNW[64:128, :], gnw_d[0:1, hp * 128 + 64:hp * 128 + 128].partition_broadcast(64), writes=[BGNW])
                S.dma(GNB[0:64, :], gnb_d[0:1, hp * 128:hp * 128 + 64].partition_broadcast(64), writes=[BGNB])
                S.dma(GNB[64:128, :], gnb_d[0:1, hp * 128 + 64:hp * 128 + 128].partition_broadcast(64), writes=[BGNB])
                S.op("dve", lambda e: e.memset(S32[:], 0.0), writes=[BS32])
                S.op("dve", lambda e: e.memset(Sbf[:], 0.0), writes=[BSbf])
                c0 = hp * 128
                for sgi in range(4):
                    own = sgi >= 2
                    tsl = slice(sgi * 1024, (sgi + 1) * 1024)
                    r_, Br = Fv("r"); k_, Bk = Fv("k"); v_, Bv = Fv("v")
                    S.dma(r_[:], ZT[c0:c0 + 128, tsl], reads=[B_ZT], writes=[Br])
                    S.dma(k_[:], ZT[2048 + c0:2048 + c0 + 128, tsl], reads=[B_ZT], writes=[Bk], q="act")
                    S.dma(v_[:], ZT[4096 + c0:4096 + c0 + 128, tsl], reads=[B_ZT], writes=[Bv])
                    ew, Bew = Fv("ew"); cle, Bcle = Fv("cle"); t0_, Bt0 = Fv("t0"); t1_, Bt1 = Fv("t1"); t2_, Bt2 = Fv("t2")
                    asig, Basig = Fv("asig"); kkn, Bkkn = Fv("kkn"); k2, Bk2 = Fv("k2"); bb, Bbb = Fv("bb")
                    for hf in range(2):
                        hs = slice(hf * 512, (hf + 1) * 512)
                        gs = slice(sgi * 1024 + hf * 512, sgi * 1024 + (hf + 1) * 512)
                        S.op("pe", lambda e: e.matmul(plo[:], wdec[:, c0:c0 + 128], txw[:, gs], start=True, stop=True), reads=[Bwdec, Btxw], writes=[Bplo])
                        S.op("act", lambda e: e.activation(t0_[:, hs], plo[:], AF.Exp, bias=nrw[:, 0, hp:hp + 1], scale=-1.0), reads=[Bplo, Bnrw], writes=[Bt0])
                        S.op("pe", lambda e: e.matmul(plo[:], waaa[:, c0:c0 + 128], xab[:, gs], start=True, stop=True), reads=[Bwaaa, Bxab], writes=[Bplo])
                        S.op("act", lambda e: e.activation(asig[:, hs], plo[:], AF.Sigmoid, bias=rwp[:, 1, hp:hp + 1]), reads=[Bplo, Brwp], writes=[Basig])
                    S.op("act", lambda e: e.activation(t0_[:], t0_[:], AF.Ln, bias=1.0), reads=[Bt0], writes=[Bt0])
                    S.op("act", lambda e: e.activation(ew[:], t0_[:], AF.Exp, bias=-0.5, scale=-1.0), reads=[Bt0], writes=[Bew])
                    S.op("dve", lambda e: e.tensor_tensor_scan(cle[:], scm[:], ew[:], 0.0, ALU.mult, ALU.add), reads=[Bscm, Bew], writes=[Bcle])
                    S.op("pool", lambda e: e.tensor_scalar(kkn[:], k_[:], rwp[:, 2, hp:hp + 1], None, ALU.mult), reads=[Bk, Brwp], writes=[Bkkn])
                    S.op("pool", lambda e: e.tensor_tensor(t1_[:], kkn[:], kkn[:], ALU.mult), reads=[Bkkn], writes=[Bt1])
                    for hf in range(2):
                        hs = slice(hf * 512, (hf + 1) * 512)
                        S.op("pe", lambda e: e.matmul(plo[:], blk1[:], t1_[:, hs], start=True, stop=True), reads=[Bblk, Bt1], writes=[Bplo])
                        S.op("act", lambda e: e.activation(t2_[:, hs], plo[:], AF.Ln, bias=1e-12), reads=[Bplo], writes=[Bt2])
                    S.op("act", lambda e: e.activation(t2_[:], t2_[:], AF.Exp, scale=-0.5), reads=[Bt2], writes=[Bt2])
                    S.op("dve", lambda e: e.tensor_tensor(kkn[:], kkn[:], t2_[:], ALU.mult), reads=[Bkkn, Bt2], writes=[Bkkn])
                    S.op("pool", lambda e: e.tensor_scalar(t1_[:], asig[:], rwp[:, 3, hp:hp + 1], nrw[:, 1, hp:hp + 1], ALU.mult, ALU.add), reads=[Basig, Brwp, Bnrw], writes=[Bt1])
                    S.op("dve", lambda e: e.tensor_tensor(k2[:], k_[:], t1_[:], ALU.mult), reads=[Bk, Bt1], writes=[Bk2])
                    S.op("pool", lambda e: e.tensor_tensor(bb[:], kkn[:], asig[:], ALU.mult), reads=[Bkkn, Basig], writes=[Bbb])
                    S.op("act", lambda e: e.activation(t0_[:], cle[:], AF.Exp), reads=[Bcle], writes=[Bt0])
                    S.op("dve", lambda e: e.tensor_tensor(Bt["bt"][0][:], bb[:], t0_[:], ALU.mult), reads=[Bbb, Bt0], writes=[Bt["bt"][1]])
                    S.op("pool", lambda e: e.tensor_tensor(Bt["kt"][0][:], k2[:], t0_[:], ALU.mult), reads=[Bk2, Bt0], writes=[Bt["kt"][1]])
                    S.op("act", lambda e: e.activation(t1_[:], cle[:], AF.Exp, scale=-1.0), reads=[Bcle], writes=[Bt1])
                    S.op("dve", lambda e: e.tensor_tensor(t2_[:], ew[:], cle[:], ALU.subtract), reads=[Bew, Bcle], writes=[Bt2])
                    S.op("act", lambda e: e.activation(t2_[:], t2_[:], AF.Exp), reads=[Bt2], writes=[Bt2])
                    for pb in (0, 64):
                        ca, cr = pb, 64 - pb
                        ps_ = slice(pb, pb + 64)
                        S.op("dve", lambda e: e.tensor_tensor(ARt[ps_, :, cr:cr + 64], r_[ps_, :].rearrange("p (c t) -> p c t", t=64), t1_[ps_, :].rearrange("p (c t) -> p c t", t=64), ALU.mult), reads=[Br, Bt1], writes=[BARt])
                        S.op("dve", lambda e: e.scalar_tensor_tensor(ARt[ps_, :, ca:ca + 64], kkn[ps_, :].rearrange("p (c t) -> p c t", t=64), -1.0, t2_[ps_, :].rearrange("p (c t) -> p c t", t=64), ALU.mult, ALU.mult), reads=[Bkkn, Bt2], writes=[BARt])
                    cle3 = cle[:].rearrange("p (c t) -> p c t", t=64)
                    S.op("dve", lambda e: e.tensor_tensor(t0_[:].rearrange("p (c t) -> p c t", t=64), cle3, cle3[:, :, 63:64].to_broadcast([128, 16, 64]), ALU.subtract), reads=[Bcle], writes=[Bt0])
                    S.op("act", lambda e: e.activation(t0_[:], t0_[:], AF.Exp), reads=[Bt0], writes=[Bt0])
                    S.op("act", lambda e: e.activation(wcs[:], cle3[:, :, 63], AF.Exp, scale=-1.0), reads=[Bcle], writes=[Bwcs])
                    S.op("dve", lambda e: e.tensor_tensor(Bt["btc"][0][:], bb[:], t0_[:], ALU.mult), reads=[Bbb, Bt0], writes=[Bt["btc"][1]])
                    S.op("pool", lambda e: e.tensor_tensor(Bt["ktc"][0][:], k2[:], t0_[:], ALU.mult), reads=[Bk2, Bt0], writes=[Bt["ktc"][1]])
                    S.op("act", lambda e: e.copy(Bt["vb"][0][:], v_[:]), reads=[Bv], writes=[Bt["vb"][1]])
                    if own:
                        S.op("pool", lambda e: e.tensor_tensor(t1_[:], r_[:], k2[:], ALU.mult), reads=[Br, Bk2, Bt1], writes=[Bt1])
                        S.op("pool", lambda e: e.tensor_scalar(Bt["rk"][0][:], t1_[:], rwp[:, 4, hp:hp + 1], None, ALU.mult), reads=[Bt1, Brwp], writes=[Bt["rk"][1]])
                    bt, Bbt = Bt["bt"]; kt, Bkt = Bt["kt"]; btc, Bbtc = Bt["btc"]; ktc, Bktc = Bt["ktc"]; vb, Bvb = Bt["vb"]; rk, Brk = Bt["rk"]


                    def stage12(b4):
                        par = b4 % 2
                        ARB, BARB = ARB_[par]; AK, BAK = AK_[par]; TOK, BTOK = TOK_[par]
                        for j in range(4):
                            c = b4 * 4 + j
                            cs = slice(c * 64, (c + 1) * 64)
                            pk, Bpk = (pN, BpN) if j < 2 else (pNT, BpNT)
                            ko = (j % 2) * 192
                            for pb in (0, 64):
                                ca = pb
                                P = slice(pb, pb + 64)
                                tp = (pb, pb)
                                S.op("pe", lambda e: e.matmul(ps1[P, j * 128:(j + 1) * 128], bt[P, cs], ARt[P, c, :], start=True, stop=True, tile_position=tp), reads=[Bbt, BARt], writes=[Bps1])
                                S.op("pe", lambda e: e.matmul(ps2[P, j * 128:(j + 1) * 128], kt[P, cs], ARt[P, c, :], start=True, stop=True, tile_position=tp), reads=[Bkt, BARt], writes=[Bps2])
                                S.op("pe", lambda e: e.matmul(ps3[P, j * 128 + ca:j * 128 + ca + 64], ARt[P, c, ca:ca + 64], bt[P, cs], start=True, stop=True, tile_position=tp), reads=[Bbt, BARt], writes=[Bps3])
                                S.op("pe", lambda e: e.matmul(pk[P, ko:ko + 64], btc[P, cs], identb[P, pb:pb + 64], start=True, stop=True, tile_position=tp), reads=[Bbtc, Bidb], writes=[Bpk])
                                S.op("pe", lambda e: e.matmul(pk[P, ko + 64:ko + 128], ktc[P, cs], identb[P, pb:pb + 64], start=True, stop=True, tile_position=tp), reads=[Bktc, Bidb], writes=[Bpk])
                                S.op("pe", lambda e: e.matmul(pk[P, ko + 128:ko + 192], vb[P, cs], identb[P, pb:pb + 64], start=True, stop=True, tile_position=tp), reads=[Bvb, Bidb], writes=[Bpk])
                        N0, BN0 = N_[0]; NT0, BNT0 = NT_[0]; TT0, BTT0 = TT_[par * 4]
                        v4 = lambda t: t[:].rearrange("p (j c) -> p j c", j=4)
                        mb = lambda i: masks[:, i, :].unsqueeze(1).to_broadcast([128, 4, 128])
                        S.op("dve", lambda e: e.tensor_tensor(NT0[:], v4(ps1), mb(0), ALU.mult), reads=[Bps1, Bmk], writes=[BNT0])
                        S.op("dve", lambda e: e.tensor_tensor(N0[:], v4(ps3), mb(3), ALU.mult), reads=[Bps3, Bmk], writes=[BN0])
                        S.op("dve", lambda e: e.tensor_tensor(TT0[:], NT0[:], identb[:].unsqueeze(1).to_broadcast([128, 4, 128]), ALU.add), reads=[BNT0, Bidb], writes=[BTT0])
                        S.op("dve", lambda e: e.tensor_tensor(AK[:], v4(ps2), mb(2), ALU.mult), reads=[Bps2, Bmk], writes=[BAK])
                        S.op("dve", lambda e: e.tensor_tensor(ARB[:], v4(ps1), mb(1), ALU.mult), reads=[Bps1, Bmk], writes=[BARB])
                        S.op("act", lambda e: e.copy(TOK[:, 0:2, :], pN[:, 0:384].rearrange("p (j c) -> p j c", j=2)), reads=[BpN], writes=[BTOK])
                        S.op("act", lambda e: e.copy(TOK[:, 2:4, :], pNT[:, 0:384].rearrange("p (j c) -> p j c", j=2)), reads=[BpNT], writes=[BTOK])
                        yield None
                        Nc, BNc = N0, BN0
                        NTc, BNTc = NT0, BNT0
                        TTc, BTTc = TT0, BTT0
                        for kq in range(1, 6):
                            Nn, BNn = N_[1 + (kq % 3)]
                            NTn, BNTn = NT_[1 + (kq % 3)]
                            TTn, BTTn = TT_[par * 4 + 1 + (kq % 3)]
                            for j in range(4):
                                S.op("pe", lambda e: e.matmul(pN[:, j * 128:(j + 1) * 128], NTc[:, j, :], Nc[:, j, :], start=True, stop=True), reads=[BNTc, BNc], writes=[BpN])
                            if kq < 5:
                                for j in range(4):
                                    S.op("pe", lambda e: e.matmul(pNT[:, j * 128:(j + 1) * 128], Nc[:, j, :], NTc[:, j, :], start=True, stop=True), reads=[BNTc, BNc], writes=[BpNT])
                            S.op("dve", lambda e: e.tensor_copy(Nn[:], v4(pN)), reads=[BpN], writes=[BNn])
                            if kq < 5:
                                S.op("act", lambda e: e.copy(NTn[:], v4(pNT)), reads=[BpNT], writes=[BNTn])
                            for j in range(4):
                                S.op("pe", lambda e: e.matmul(pT[:, j * 128:(j + 1) * 128], Nn[:, j, :], TTc[:, j, :], start=True, stop=True), reads=[BNn, BTTc], writes=[BpT])
                            S.op("dve", lambda e: e.tensor_tensor(TTn[:], v4(pT), TTc[:], ALU.add), reads=[BpT, BTTc], writes=[BTTn])
                            Nc, BNc, NTc, BNTc, TTc, BTTc = Nn, BNn, NTn, BNTn, TTn, BTTn
                            yield None
                        yield (TTc, BTTc)

                    def recur(b4, j, TTc, BTTc):
                        par = b4 % 2
                        ARB, BARB = ARB_[par]; AK, BAK = AK_[par]; TOK, BTOK = TOK_[par]
                        c = b4 * 4 + j
                        gc = sgi * 16 + c
                        cs = slice(c * 64, (c + 1) * 64)
                        Xb, BXb = Xb_[gc % 2]; Ub, BUb = Ub_[gc % 2]
                        for pb in (0, 64):
                            ca = pb
                            P = slice(pb, pb + 64)
                            tp = (pb, pb)
                            S.op("pe", lambda e: e.matmul(pxu[P, 0:64], ARt[P, c, ca:ca + 64], Sbf[P, :], start=True, stop=False, tile_position=tp), reads=[BARt, BSbf], writes=[Bpxu])
                            S.op("pe", lambda e: e.matmul(pxu[P, 0:64], AK[P, j, ca:ca + 64], TOK[P, j, 128:192], start=False, stop=True, tile_position=tp), reads=[BAK, BTOK], writes=[Bpxu])
                        S.op("act", lambda e: e.copy(Xb[:], pxu[:, 0:64]), reads=[Bpxu], writes=[BXb])
                        for pb in (0, 64):
                            P = slice(pb, pb + 64)
                            S.op("pe", lambda e: e.matmul(pxu[P, 64:128], TTc[P, j, pb:pb + 64], Xb[P, :], start=True, stop=True, tile_position=(pb, pb)), reads=[BTTc, BXb], writes=[Bpxu])
                        S.op("dve", lambda e: e.tensor_copy(Ub[:], pxu[:, 64:128]), reads=[Bpxu], writes=[BUb])
                        for pb in (0, 64):
                            P = slice(pb, pb + 64)
                            tp = (pb, pb)
                            S.op("pe", lambda e: e.matmul(pxu[P, 128:192], TOK[P, j, 0:64], Ub[P, :], start=True, stop=False, tile_position=tp), reads=[BTOK, BUb], writes=[Bpxu])
                            S.op("pe", lambda e: e.matmul(pxu[P, 128:192], TOK[P, j, 64:128], TOK[P, j, 128:192], start=False, stop=True, tile_position=tp), reads=[BTOK], writes=[Bpxu])
                        if own:
                            for pb in (0, 64):
                                cr = 64 - pb
                                P = slice(pb, pb + 64)
                                tp = (pb, pb)
                                S.op("pe", lambda e: e.matmul(pys[P, 0:64], ARt[P, c, cr:cr + 64], Sbf[P, :], start=True, stop=False, tile_position=tp), reads=[BARt, BSbf], writes=[Bpys])
                                S.op("pe", lambda e: e.matmul(pys[P, 0:64], AK[P, j, cr:cr + 64], TOK[P, j, 128:192], start=False, stop=False, tile_position=tp), reads=[BAK, BTOK], writes=[Bpys])
                                S.op("pe", lambda e: e.matmul(pys[P, 0:64], ARB[P, j, cr:cr + 64], Ub[P, :], start=False, stop=True, tile_position=tp), reads=[BARB, BUb], writes=[Bpys])
                                S.op("pe", lambda e: e.matmul(pys[P, 128:129], rk[P, cs], blk1b[P, pb:pb + 1], start=True, stop=True, tile_position=tp), reads=[Brk, Bblkb], writes=[Bpys])
                                gtok = slice(sgi * 1024 + c * 64, sgi * 1024 + (c + 1) * 64)
                                hc = c0 + pb
                                for kc in range(2):
                                    S.op("pe", lambda e: e.matmul(pys[P, 64:128], sxg[:, kc, gtok], wgat[:, kc, hc:hc + 64], start=(kc == 0), stop=(kc == 1), tile_position=(0, pb)), reads=[Bsxg, Bwgat], writes=[Bpys])
                        S.op("dve", lambda e: e.scalar_tensor_tensor(S32[:], S32[:], wcs[:, c:c + 1], pxu[:, 128:192], ALU.mult, ALU.add), reads=[BS32, Bwcs, Bpxu], writes=[BS32])
                        S.op("act", lambda e: e.copy(Sbf[:], S32[:]), reads=[BS32], writes=[BSbf])
                        if own:
                            ysb, Bysb = ysb_[gc % 2]; ycn, Bycn = ycn_[gc % 2]; yjk, Byjk = yjk_[gc % 2]
                            s4, Bs4 = st4_[gc % 2]; rwb, Brwb = rwb_[gc % 2]
                            S.op("dve", lambda e: e.memset(s4[:], 0.0), writes=[Bs4])
                            S.op("act", lambda e: e.activation(ysb[:], pys[:, 0:64], AF.Identity, accum_out=s4[:, 0:1]), reads=[Bpys, Bs4], writes=[Bysb, Bs4])
                            S.op("dve", lambda e: e.tensor_scalar(s4[:, 1:2], s4[:, 0:1], -1.0 / 64, None, ALU.mult), reads=[Bs4], writes=[Bs4])
                            S.op("dve", lambda e: e.tensor_scalar(ycn[:], ysb[:], s4[:, 1:2], None, ALU.add), reads=[Bysb, Bs4], writes=[Bycn])
                            S.op("act", lambda e: e.activation(yjk[:], ycn[:], AF.Square, accum_out=s4[:, 2:3]), reads=[Bycn], writes=[Byjk, Bs4])
                            S.op("act", lambda e: e.activation(s4[:, 3:4], s4[:, 2:3], AF.Ln, bias=64e-5, scale=1.0 / 64), reads=[Bs4], writes=[Bs4])
                            S.op("act", lambda e: e.activation(s4[:, 4:5], s4[:, 3:4], AF.Exp, scale=-0.5), reads=[Bs4], writes=[Bs4])
                            S.op("dve", lambda e: e.scalar_tensor_tensor(ycn[:], ycn[:], s4[:, 4:5], GNW[:], ALU.mult, ALU.mult), reads=[Bycn, Bs4, BGNW], writes=[Bycn])
                            S.op("dve", lambda e: e.tensor_tensor(ycn[:], ycn[:], GNB[:], ALU.add), reads=[Bycn, BGNB], writes=[Bycn])
                            S.op("act", lambda e: e.copy(s4[:, 5:6], pys[:, 128:129]), reads=[Bpys], writes=[Bs4])
                            S.op("dve", lambda e: e.scalar_tensor_tensor(ycn[:], TOK[:, j, 128:192], s4[:, 5:6], ycn[:], ALU.mult, ALU.add), reads=[BTOK, Bs4, Bycn], writes=[Bycn])
                            S.op("dve", lambda e: e.tensor_tensor(rwb[:], ycn[:], pys[:, 64:128], ALU.mult), reads=[Bycn, Bpys], writes=[Brwb])
                            for pb in (0, 64):
                                P = slice(pb, pb + 64)
                                S.op("pe", lambda e: e.matmul(pys[P, 192:256], rwb[P, :], identb[P, pb:pb + 64], start=True, stop=True, tile_position=(pb, pb)), reads=[Brwb, Bidb], writes=[Bpys])
                            S.op("act", lambda e: e.copy(rwTs[:, cs], pys[:, 192:256]), reads=[Bpys], writes=[BrwTs])

                    def run_all(g):
                        r = None
                        for r in g:
                            pass
                        return r

                    cur = run_all(stage12(0))
                    for b4 in range(4):
                        g = stage12(b4 + 1) if b4 < 3 else None
                        if g is not None:
                            next(g)
                        for j in range(4):
                            recur(b4, j, cur[0], cur[1])
                            next(precast, None)
                            if g is not None:
                                next(g)
                                if j == 3:
                                    next(g)
                                    cur = next(g)

                    if own:
                        S.dma(RWT[c0:c0 + 128, (sgi - 2) * 1024:(sgi - 1) * 1024], rwTs[:], reads=[BrwTs], acc=[B_RWT])
            for _ in precast:
                pass


        esD = ExitStack()
        if stop == "C":
            S.drain("sp")
            return nc
        with esD:
            accT, Bacc = SB(esD, "accT", [128, 32, 512], F32)
            actA, BactA = SB(esD, "actA", [128, 32, 512], BF16)
            actB, BactB = SB(esD, "actB", [128, 20, 512], BF16)
            hid, Bhid = actB, BactB
            gts = [SB(esD, "gts%d" % i, [128, 2, 512], BF16) for i in range(2)]
            nmlp, Bnmlp = SB(esD, "nmlp", [128, 32], F32)
            nple, Bnple = SB(esD, "nple", [128, 32], F32)
            S.dma(nmlp[:], nmlp_d[:], writes=[Bnmlp])
            S.dma(nple[:], nple_d[:], writes=[Bnple])
            xrow = [SB(esD, "xrow%d" % i, [128, 1024], F32) for i in range(2)]
            osd = [SB(esD, "osd%d" % i, [128, 12, 129], F32) for i in range(1)]
            numt, Bnum = SB(esD, "numt", [128, 4, 129], F32)
            rden, Brden = SB(esD, "rden", [128, 4], F32)
            attb, Battb = SB(esD, "attb", [128, 4, 128], BF16)
            pld, Bpld = SB(esD, "pld", [128, 256], F32)
            pldb, Bpldb = SB(esD, "pldb", [128, 256], BF16)
            pT, BpT = SB(esD, "pT", [128, 2, 512], BF16)
            tmpA = [SB(esD, "tmpA%d" % i, [128, 512], F32) for i in range(2)]
            sqd, Bsqd = SB(esD, "sqd", [128, 512], BF16)
            rstd, Brstd = SB(esD, "rstd", [128, 512], F32)
            pdm = [PS(esD, "pdm%d" % i, [128, 512]) for i in range(3)]
            pdb = [PS(esD, "pdb%d" % i, [128, 512]) for i in range(2)]
            pdn, Bpdn = PS(esD, "pdn", [128, 512])
            pdt = [PS(esD, "pdt%d" % i, [128, 1024], BF16) for i in range(1)]
            pdf, Bpdf = PS(esD, "pdf", [128, 512])
            wpb = [SB(esD, "wpb%d" % i, [128, 256], BF16) for i in range(2)]
            dctr = {"mm": 0, "b": 0, "ta": 0, "x": 0}

            def nextp():
                dctr["mm"] += 1
                return pdm[dctr["mm"] % 3]

            def norm_to(actdst, Bactdst, gainT, BgainT):
                for blk in range(32):
                    S.op("act", lambda e: e.activation(sqd[:], accT[:, blk, :], AF.Square), reads=[Bacc], writes=[Bsqd])
                    S.op("pe", lambda e: e.matmul(pdn[:], onesb[:], sqd[:], start=(blk == 0), stop=(blk == 31)), reads=[Bones, Bsqd], writes=[Bpdn])
                S.op("act", lambda e: e.activation(rstd[:], pdn[:], AF.Ln, bias=1e-6, scale=1.0 / D), reads=[Bpdn], writes=[Brstd])
                S.op("act", lambda e: e.activation(rstd[:], rstd[:], AF.Exp, scale=-0.5), reads=[Brstd], writes=[Brstd])
                for blk in range(32):
                    S.op("dve", lambda e: e.scalar_tensor_tensor(actdst[:, blk, :], accT[:, blk, :], gainT[:, blk:blk + 1], rstd[:], ALU.mult, ALU.mult), reads=[Bacc, BgainT, Brstd], writes=[Bactdst])

            dspecs = []
            for ti_ in range(4):
                for blk_ in range(32):
                    dspecs.append((w_attn_up, 0, 512, blk_ * 128, 128, None))
                    dspecs.append((w_rwkv_up, 0, 2048, blk_ * 128, 128, None))
                for blk_ in range(32):
                    dspecs.append((w_out, 0, D, blk_ * 128, 128, None))
                for ch_ in range(16):
                    for fb_ in range(8):
                        dspecs.append((w_mlp_in, 0, D, ch_ * 1024 + fb_ * 128, 128, None))
                    for blk_ in range(32):
                        dspecs.append((w_mlp_out, ch_ * 1024, 1024, blk_ * 128, 128, None))
                for blk_ in range(32):
                    dspecs.append((w_ple_gate, 0, D, blk_ * 128, 128, None))
                    dspecs.append((w_ple_proj, 0, 256, blk_ * 128, 128, wpb[blk_ % 2]))
            ring = [wbf[0], wbf[1]] + [(w_[:].bitcast(BF16), bw_) for (w_, bw_) in wst]
            key2off = {}
            for bi_, (w_ap, r0, krows, c0_) in enumerate(tail_blocks):
                key2off[(w_ap.tensor.name, r0, krows, c0_)] = wb_off[bi_]

            class WQ2:
                def __init__(self, specs):
                    self.specs = specs
                    self.pos = 0
                    self.dpos = 0
                    self.tiles = {}
                    self.rr = 0

                def _dma(self):
                    (w_ap, r0, krows, c0_, m, dst) = self.specs[self.dpos]
                    kc = (krows + 127) // 128
                    wti, off = key2off[(w_ap.tensor.name, r0, krows, c0_)]
                    if dst is not None:
                        wb, Bwb = dst
                        view = wb[:, 0:kc * 128]
                    else:
                        wb, Bwb = ring[self.rr % len(ring)]
                        self.rr += 1
                        view = wb[:, 0:kc * 128]
                    S.dma(view, WBs[wti][:, off:off + kc * 128], reads=[B_WB], writes=[Bwb], q="sp")
                    self.tiles[self.dpos] = (wb, Bwb, kc)
                    self.dpos += 1

                def next(self):
                    n = len(self.specs)
                    while self.dpos < min(n, self.pos + 4):
                        self._dma()
                    r = self.tiles.pop(self.pos)
                    self.pos += 1
                    return r
            wqD = WQ2(dspecs)

            for ti in range(4):
                T0 = ti * 512
                for sub in range(4):
                    for cq in range(4):
                        xr, Bxr = xrow[dctr["x"] % 2]
                        dctr["x"] += 1
                        S.dma(xr[:], x_d[OWN + T0 + sub * 128:OWN + T0 + (sub + 1) * 128, cq * 1024:(cq + 1) * 1024], writes=[Bxr], q="sp" if cq % 2 == 0 else "act")
                        for half in range(2):
                            for j in range(4):
                                S.op("pe", lambda e: e.transpose(pdf[:, j * 128:(j + 1) * 128], xr[:, (half * 4 + j) * 128:(half * 4 + j + 1) * 128], identf[:]), reads=[Bxr, Bidf], writes=[Bpdf])
                            blk0 = cq * 8 + half * 4
                            S.op("dve", lambda e: e.tensor_copy(accT[:, blk0:blk0 + 4, sub * 128:(sub + 1) * 128], pdf[:].rearrange("p (k t) -> p k t", k=4)), reads=[Bpdf], writes=[Bacc])
                for sub in range(4):
                    od, Bod = osd[0]
                    tk = T0 + sub * 128
                    S.dma(od[:], OS[tk:tk + 128], reads=[B_OS], writes=[Bod])
                    S.op("dve", lambda e: e.tensor_tensor(numt[:], od[:, 0:4, :], od[:, 4:8, :], ALU.add), reads=[Bod], writes=[Bnum])
                    S.op("dve", lambda e: e.tensor_tensor(numt[:], numt[:], od[:, 8:12, :], ALU.add), reads=[Bod, Bnum], writes=[Bnum])
                    S.op("dve", lambda e: e.reciprocal(rden[:], numt[:, :, 128]), reads=[Bnum], writes=[Brden])
                    S.op("dve", lambda e: e.tensor_tensor(attb[:], numt[:, :, 0:128], rden[:].unsqueeze(2).to_broadcast([128, 4, 128]), ALU.mult), reads=[Bnum, Brden], writes=[Battb])
                    pt, Bpt = pdt[0]
                    for j in range(4):
                        S.op("pe", lambda e: e.transpose(pt[:, j * 128:(j + 1) * 128], attb[:, j, :], identb[:]), reads=[Battb, Bidb], writes=[Bpt])
                    S.op("act", lambda e: e.copy(actB[:, 0:4, sub * 128:(sub + 1) * 128], pt[:, 0:512].rearrange("p (k t) -> p k t", k=4)), reads=[Bpt], writes=[BactB])
                    S.dma(pld[:], p_d[tk:tk + 128, :], writes=[Bpld], q="act")
                    S.op("dve", lambda e: e.tensor_copy(pldb[:], pld[:]), reads=[Bpld], writes=[Bpldb])
                    for j in range(2):
                        S.op("pe", lambda e: e.transpose(pt[:, 512 + j * 128:512 + (j + 1) * 128], pldb[:, j * 128:(j + 1) * 128], identb[:]), reads=[Bpldb, Bidb], writes=[Bpt])
                    S.op("act", lambda e: e.copy(pT[:, :, sub * 128:(sub + 1) * 128], pt[:, 512:768].rearrange("p (k t) -> p k t", k=2)), reads=[Bpt], writes=[BpT])
                S.dma(actB[:, 4:20, :], RWT[:, T0:T0 + 512].rearrange("(k p) t -> p k t", p=128), reads=[B_RWT], writes=[BactB])
                for blk in range(32):
                    gt, Bgt = gts[blk % 2]
                    S.dma(gt[:, 0, :], GT[blk * 128:(blk + 1) * 128, T0:T0 + 512], reads=[B_GT], writes=[Bgt])
                    S.dma(gt[:, 1, :], GT[4096 + blk * 128:4096 + (blk + 1) * 128, T0:T0 + 512], reads=[B_GT], writes=[Bgt], q="act")
                    pa, Bpa = pdb[0]
                    pr, Bpr = pdb[1]
                    wa, Bwa, _ = wqD.next()
                    for kc in range(4):
                        S.op("pe", lambda e: e.matmul(pa[:], wa[:, kc * 128:(kc + 1) * 128], actB[:, kc, :], start=(kc == 0), stop=(kc == 3)), reads=[Bwa, BactB], writes=[Bpa])
                    wr, Bwr, _ = wqD.next()
                    for kc in range(16):
                        S.op("pe", lambda e: e.matmul(pr[:], wr[:, kc * 128:(kc + 1) * 128], actB[:, 4 + kc, :], start=(kc == 0), stop=(kc == 15)), reads=[Bwr, BactB], writes=[Bpr])
                    ta, Bta = tmpA[0]
                    tb, Btb = tmpA[1]
                    S.op("dve", lambda e: e.tensor_tensor(ta[:], pa[:], gt[:, 0, :], ALU.mult), reads=[Bpa, Bgt], writes=[Bta])
                    S.op("dve", lambda e: e.tensor_tensor(tb[:], pr[:], gt[:, 1, :], ALU.mult), reads=[Bpr, Bgt], writes=[Btb])
                    S.op("pool", lambda e: e.tensor_tensor(actA[:, blk, :], ta[:], tb[:], ALU.add), reads=[Bta, Btb], writes=[BactA])
                for blk in range(32):
                    wb, Bwb, _ = wqD.next()
                    pm_, Bpm = nextp()
                    for kc in range(32):
                        S.op("pe", lambda e: e.matmul(pm_[:], wb[:, kc * 128:(kc + 1) * 128], actA[:, kc, :], start=(kc == 0), stop=(kc == 31)), reads=[Bwb, BactA], writes=[Bpm])
                    S.op("dve", lambda e: e.tensor_tensor(accT[:, blk, :], accT[:, blk, :], pm_[:], ALU.add), reads=[Bacc, Bpm], writes=[Bacc])
                norm_to(actA, BactA, nmlp, Bnmlp)
                for ch in range(16):
                    for fb in range(8):
                        wb, Bwb, _ = wqD.next()
                        pm_, Bpm = nextp()
                        for kc in range(32):
                            S.op("pe", lambda e: e.matmul(pm_[:], wb[:, kc * 128:(kc + 1) * 128], actA[:, kc, :], start=(kc == 0), stop=(kc == 31)), reads=[Bwb, BactA], writes=[Bpm])
                        ta, Bta = tmpA[fb % 2]
                        S.op("act", lambda e: e.activation(ta[:], pm_[:], AF.Relu), reads=[Bpm], writes=[Bta])
                        S.op("dve", lambda e: e.tensor_tensor(hid[:, fb, :], ta[:], ta[:], ALU.mult), reads=[Bta], writes=[Bhid])
                    for blk in range(32):
                        wb, Bwb, _ = wqD.next()
                        pm_, Bpm = nextp()
                        for kc in range(8):
                            S.op("pe", lambda e: e.matmul(pm_[:], wb[:, kc * 128:(kc + 1) * 128], hid[:, kc, :], start=(kc == 0), stop=(kc == 7)), reads=[Bwb, Bhid], writes=[Bpm])
                        S.op("dve", lambda e: e.tensor_tensor(accT[:, blk, :], accT[:, blk, :], pm_[:], ALU.add), reads=[Bacc, Bpm], writes=[Bacc])
                norm_to(actA, BactA, nple, Bnple)
                for blk in range(32):
                    wb, Bwb, _ = wqD.next()
                    pm_, Bpm = nextp()
                    pp, Bpp = pdb[blk % 2]
                    for kc in range(32):
                        S.op("pe", lambda e: e.matmul(pm_[:], wb[:, kc * 128:(kc + 1) * 128], actA[:, kc, :], start=(kc == 0), stop=(kc == 31)), reads=[Bwb, BactA], writes=[Bpm])
                    wp, Bwp, _ = wqD.next()
                    for kc in range(2):
                        S.op("pe", lambda e: e.matmul(pp[:], wp[:, kc * 128:(kc + 1) * 128], pT[:, kc, :], start=(kc == 0), stop=(kc == 1)), reads=[Bwp, BpT], writes=[Bpp])
                    ta, Bta = tmpA[blk % 2]
                    S.op("act", lambda e: e.activation(ta[:], pm_[:], AF.Sigmoid), reads=[Bpm], writes=[Bta])
                    S.op("dve", lambda e: e.tensor_tensor(ta[:], ta[:], pp[:], ALU.mult), reads=[Bta, Bpp], writes=[Bta])
                    S.op("pool", lambda e: e.tensor_tensor(accT[:, blk, :], accT[:, blk, :], ta[:], ALU.add), reads=[Bacc, Bta], writes=[Bacc])
                for sub in range(4):
                    for cq in range(4):
                        xr, Bxr = xrow[dctr["x"] % 2]
                        dctr["x"] += 1
                        for half in range(2):
                            for j in range(4):
                                blk = cq * 8 + half * 4 + j
                                S.op("pe", lambda e: e.transpose(pdf[:, j * 128:(j + 1) * 128], accT[:, blk, sub * 128:(sub + 1) * 128], identf[:]), reads=[Bacc, Bidf], writes=[Bpdf])
                            S.op("act", lambda e: e.copy(xr[:, half * 512:(half + 1) * 512], pdf[:]), reads=[Bpdf], writes=[Bxr])
                        S.dma(out_d[T0 + sub * 128:T0 + (sub + 1) * 128, cq * 1024:(cq + 1) * 1024], xr[:], reads=[Bxr], writes=[])
        S.drain("sp")
        print("instr counts", S.cnt)
    return nc


_CACHE = {}


def make_in_maps(inp):
    f32 = np.float32
    x = np.asarray(inp["x"], f32)
    p = np.asarray(inp["p"], f32)[0]
    c = host_consts()
    shift_mix = np.asarray(inp["shift_mix"], f32)[0]
    mixT = np.zeros((128, 52), f32)
    for b in range(52):
        mixT[:ZM[b], b] = shift_mix[ZST[b]:ZST[b] + ZM[b]]
    rwp = np.stack([np.asarray(inp[k], f32).reshape(2048).reshape(16, 128).T
                    for k in ("w0", "a0", "k_k", "k_a", "r_k")], axis=1)
    rb33 = np.concatenate([np.asarray(inp["rel_bias"], f32), np.ones((1, 12), f32)], axis=0)
    qkg = np.stack([np.asarray(inp["q_gain"], f32)[0], np.asarray(inp["k_gain"], f32)[0]], axis=1)
    shared = {
        "w_in": np.asarray(inp["w_in"], f32)[0],
        "w_attn_up": np.asarray(inp["w_attn_up"], f32)[0],
        "w_decay_up": np.asarray(inp["w_decay_up"], f32)[0],
        "w_aaa_up": np.asarray(inp["w_aaa_up"], f32)[0],
        "w_gate_up": np.asarray(inp["w_gate_up"], f32)[0],
        "w_rwkv_up": np.asarray(inp["w_rwkv_up"], f32)[0],
        "w_out": np.asarray(inp["w_out"], f32)[0],
        "w_mlp_in": np.asarray(inp["w_mlp_in"], f32)[0],
        "w_mlp_out": np.asarray(inp["w_mlp_out"], f32)[0],
        "w_ple_gate": np.asarray(inp["w_ple_gate"], f32)[0],
        "w_ple_proj": np.asarray(inp["w_ple_proj"], f32)[0],
        "norm_mix": np.asarray(inp["norm_mix"], f32).reshape(1, D),
        "qkg": np.ascontiguousarray(qkg),
        "rb33": rb33,
        "mixT": mixT,
        "rwp": np.ascontiguousarray(rwp),
        "gn_w": np.asarray(inp["gn_w"], f32).reshape(1, 2048),
        "gn_b": np.asarray(inp["gn_b"], f32).reshape(1, 2048),
        "nmlpT": np.ascontiguousarray(np.asarray(inp["norm_mlp"], f32).reshape(32, 128).T),
        "npleT": np.ascontiguousarray(np.asarray(inp["norm_ple"], f32).reshape(32, 128).T),
        "ident": c["ident"], "masks": c["masks"], "scanmask": c["scanmask"],
        "blkones": c["blkones"], "oh": c["oh"],
    }
    in_maps = []
    for core in range(8):
        b, th = core // 2, core % 2
        if th == 0:
            xl = np.concatenate([np.zeros((OWN, D), f32), x[b, :OWN]], axis=0)
        else:
            xl = x[b]
        m = dict(shared)
        m["x"] = np.ascontiguousarray(xl)
        m["p"] = np.ascontiguousarray(p[b, th * OWN:(th + 1) * OWN])
        m["pm"] = np.full((128, 1), NEG if th == 0 else 0.0, f32)
        in_maps.append(m)
    return in_maps


def kernel(**inp):
    f32 = np.float32
    x = np.asarray(inp["x"], f32)
    in_maps = make_in_maps(inp)
    if "nc" not in _CACHE:
        _CACHE["nc"] = build_program()
    res = run_bass_kernel_spmd(_CACHE["nc"], in_maps, core_ids=list(range(8)))
    out = np.zeros((4, SEQ, D), f32)
    for core in range(8):
        b, th = core // 2, core % 2
        out[b, th * OWN:(th + 1) * OWN] = np.asarray(res.results[core]["out"], f32)
    return out
```

```python
import math
import numpy as np
from contextlib import ExitStack
import concourse.bass as bass
import concourse.mybir as mybir
from concourse.bass_utils import run_bass_kernel_spmd

F32 = mybir.dt.float32
BF16 = mybir.dt.bfloat16
ALU = mybir.AluOpType
AF = mybir.ActivationFunctionType

D = 4096
SEQ = 4096
OWN = 2048
NEG = -30000.0
DILS = (1, 4, 16)
ZST = [128 * i for i in range(48)] + [6144, 6240, 6336, 6464]
ZM = [128] * 48 + [96, 96, 128, 128]
A_END = 4608
R_END = 4608 + 6592


class Buf:
    __slots__ = ("w", "r")

    def __init__(self):
        self.w = {}
        self.r = {}


class Sched:
    ENG = ["pe", "act", "dve", "pool", "sp"]
    NSLOT = 12

    EPOCH = 8000

    def __init__(self, nc, es):
        self.nc = nc
        self.es = es
        self.e = dict(pe=nc.tensor, act=nc.scalar, dve=nc.vector, pool=nc.gpsimd, sp=nc.sync)
        self.sem = {}
        self.slots = {}
        for q in ["sp", "act", "pool"]:
            for i in range(self.NSLOT):
                k = "d%s%d" % (q, i)
                self.slots[k] = 0
        self.slot_rr = {"sp": 0, "act": 0, "pool": 0}
        self.cnt = {k: 0 for k in self.ENG}
        self.seen = {k: {} for k in self.ENG}

    def semof(self, k, v):
        ep = (v - 1) // self.EPOCH
        kk = (k, ep)
        if kk not in self.sem:
            self.sem[kk] = self.es.enter_context(self.nc.semaphore("s_%s_%d" % (k, ep)))
        return self.sem[kk], v - ep * self.EPOCH

    def _wait(self, en, deps):
        need = {}
        for (k, v) in deps:
            if k == "pe" and en == "pe":
                continue
            if v > need.get(k, 0):
                need[k] = v
        sn = self.seen[en]
        for k, v in need.items():
            if sn.get(k, 0) < v:
                sm, rv = self.semof(k, v)
                self.e[en].wait_ge(sm, rv)
                sn[k] = v

    def _deps(self, reads, writes):
        deps = []
        for b in reads:
            deps.extend(b.w.items())
        for b in writes:
            deps.extend(b.w.items())
            deps.extend(b.r.items())
        return deps

    def _mark(self, tok, reads, writes):
        k, v = tok
        for b in reads:
            if b.r.get(k, 0) < v:
                b.r[k] = v
        for b in writes:
            b.w = {k: v}
            b.r = {}

    def op(self, en, fn, reads=(), writes=()):
        self._wait(en, self._deps(reads, writes))
        ins = fn(self.e[en])
        self.cnt[en] += 1
        sm, _ = self.semof(en, self.cnt[en])
        ins.then_inc(sm, 1)
        self._mark((en, self.cnt[en]), reads, writes)

    def dma(self, out, in_, reads=(), writes=(), q="sp", acc=(), **kw):
        i = self.slot_rr[q]
        self.slot_rr[q] = (i + 1) % self.NSLOT
        k = "d%s%d" % (q, i)
        deps = self._deps(reads, writes)
        if self.slots[k] > 0:
            deps.append((k, self.slots[k]))
        self._wait(q, deps)
        ins = self.e[q].dma_start(out=out, in_=in_, **kw)
        self.slots[k] += 16
        sm, _ = self.semof(k, self.slots[k])
        ins.then_inc(sm, 16)
        self._mark((k, self.slots[k]), reads, writes)
        for b in acc:
            b.w[k] = self.slots[k]

    def drain(self, en="sp"):
        for k, val in self.slots.items():
            if val > 0:
                sm, rv = self.semof(k, val)
                self.e[en].wait_ge(sm, rv)


def t5_bucket_np(dist):
    dist = np.asarray(dist, dtype=np.int64)
    max_exact = 16
    d_f = np.maximum(dist, 1).astype(np.float32)
    large = max_exact + (np.log(d_f / np.float32(max_exact)) / np.float32(math.log(2048 / max_exact))
                         * np.float32(32 - max_exact)).astype(np.int32)
    large = np.minimum(large, 31)
    return np.where(dist < max_exact, dist, large)


def host_consts():
    c = {}
    c["ident"] = np.eye(128, dtype=np.float32)
    s = np.arange(64)[:, None]
    t = np.arange(64)[None, :]
    su = (s < t).astype(np.float32)
    iu = (s <= t).astype(np.float32)
    sl = (s > t).astype(np.float32)
    z = np.zeros((64, 64), np.float32)
    mLT = np.block([[su, z], [z, su]])
    mRB = np.block([[z, iu], [iu, z]])
    mAK = np.block([[su, iu], [iu, su]])
    mL = np.block([[sl, z], [z, sl]])
    c["masks"] = np.stack([mLT, mRB, mAK, mL], axis=1).astype(np.float32)
    sm = np.ones((128, 1024), np.float32)
    sm[:, ::64] = 0.0
    c["scanmask"] = sm
    bo = np.zeros((128, 128), np.float32)
    bo[:64, :64] = 1.0
    bo[64:, 64:] = 1.0
    c["blkones"] = bo
    oh = np.zeros((33, 3, 383), np.float32)
    for g, d in enumerate(DILS):
        for u in range(383):
            rel = u - 127
            if 0 <= rel <= 128:
                oh[int(t5_bucket_np(rel * d)), g, u] = 1.0
            else:
                oh[32, g, u] = NEG
    c["oh"] = oh
    return c


def build_program(stop=None, debug=False):
    nc = bass.Bass("TRN2", target_bir_lowering=False)

    def din(name, shape, dt=F32):
        return nc.dram_tensor(name, list(shape), dt, kind="ExternalInput").ap()

    def dscr(name, shape, dt):
        return nc.dram_tensor(name, list(shape), dt, kind="ExternalOutput" if (debug and name in debug) else "Internal").ap()

    x_d = din("x", [SEQ, D])
    p_d = din("p", [OWN, 256])
    pm_d = din("pm", [128, 1])
    w_in = din("w_in", [D, 19392])
    w_attn_up = din("w_attn_up", [512, D])
    w_decay_up = din("w_decay_up", [96, 2048])
    w_aaa_up = din("w_aaa_up", [96, 2048])
    w_gate_up = din("w_gate_up", [256, 2048])
    w_rwkv_up = din("w_rwkv_up", [2048, D])
    w_out = din("w_out", [D, D])
    w_mlp_in = din("w_mlp_in", [D, 16384])
    w_mlp_out = din("w_mlp_out", [16384, D])
    w_ple_gate = din("w_ple_gate", [D, D])
    w_ple_proj = din("w_ple_proj", [256, D])
    norm_mix = din("norm_mix", [1, D])
    qkg_d = din("qkg", [128, 2])
    rb33_d = din("rb33", [33, 12])
    mixT_d = din("mixT", [128, 52])
    rwp_d = din("rwp", [128, 5, 16])
    gnw_d = din("gn_w", [1, 2048])
    gnb_d = din("gn_b", [1, 2048])
    nmlp_d = din("nmlpT", [128, 32])
    nple_d = din("npleT", [128, 32])
    ident_d = din("ident", [128, 128])
    masks_d = din("masks", [128, 4, 128])
    scanmask_d = din("scanmask", [128, 1024])
    blkones_d = din("blkones", [128, 128])
    oh_d = din("oh", [33, 3, 383])
    out_d = nc.dram_tensor("out", [OWN, D], F32, kind="ExternalOutput").ap()

    KT = dscr("KT", [12, 128, SEQ], BF16)
    VT = dscr("VT", [12, 128, SEQ], BF16)
    QT = dscr("QT", [12, 128, OWN], BF16)
    ZT = dscr("ZT", [6656, SEQ], F32)
    GT = dscr("GT", [8192, OWN], BF16)
    OS = dscr("OS", [OWN, 12, 129], F32)
    RWT = dscr("RWT", [2048, OWN], BF16)
    GS = dscr("GS", [12, 130, 383], F32)
    B_KT, B_VT, B_QT, B_ZT, B_GT, B_OS, B_RWT, B_GS = [Buf() for _ in range(8)]
    tail_blocks = []
    for blk_ in range(32):
        tail_blocks.append((w_attn_up, 0, 512, blk_ * 128))
        tail_blocks.append((w_rwkv_up, 0, 2048, blk_ * 128))
    for blk_ in range(32):
        tail_blocks.append((w_out, 0, D, blk_ * 128))
    for ch_ in range(16):
        for fb_ in range(8):
            tail_blocks.append((w_mlp_in, 0, D, ch_ * 1024 + fb_ * 128))
        for blk_ in range(32):
            tail_blocks.append((w_mlp_out, ch_ * 1024, 1024, blk_ * 128))
    for blk_ in range(32):
        tail_blocks.append((w_ple_gate, 0, D, blk_ * 128))
        tail_blocks.append((w_ple_proj, 0, 256, blk_ * 128))
    wb_off = []
    CAP_ = 900000
    sizes_ = [0]
    for (_, _, krows_, _) in tail_blocks:
        n_ = ((krows_ + 127) // 128) * 128
        if sizes_[-1] + n_ > CAP_:
            sizes_.append(0)
        wb_off.append((len(sizes_) - 1, sizes_[-1]))
        sizes_[-1] += n_
    WBs = [dscr("WB%d" % i_, [128, sz_], BF16) for i_, sz_ in enumerate(sizes_)]
    B_WB = Buf()
    B_out = Buf()

    top = ExitStack()
    with top:
        S = Sched(nc, top)

        def SB(es, name, shape, dt):
            return es.enter_context(nc.sbuf_tensor("sb_" + name, list(shape), dt)), Buf()

        def PS(es, name, shape, dt=F32):
            return es.enter_context(nc.psum_tensor("ps_" + name, list(shape), dt)), Buf()

        identf, Bidf = SB(top, "identf", [128, 128], F32)
        identb, Bidb = SB(top, "identb", [128, 128], BF16)
        S.dma(identf[:], ident_d[:], writes=[Bidf])
        S.op("dve", lambda e: e.tensor_copy(identb[:], identf[:]), reads=[Bidf], writes=[Bidb])
        onesb, Bones = SB(top, "onesb", [128, 128], BF16)
        S.op("dve", lambda e: e.memset(onesb[:], 1.0), writes=[Bones])

        wst = [SB(top, "wst%d" % i, [128, 2048], F32) for i in range(4)]
        wbf = [SB(top, "wbf%d" % i, [128, 4096], BF16) for i in range(2)]
        wctr = [0, 0, 0]

        def load_w(w_ap, r0, krows, c0, m, dst=None):
            kc = (krows + 127) // 128
            if dst is not None:
                wb, Bwb = dst
            else:
                wb, Bwb = wbf[wctr[1] % 2]
                wctr[1] += 1
            pieces = [(0, kc)] if kc <= 16 else [(0, 16), (16, kc)]
            first = True
            for (k0, k1) in pieces:
                st, Bst = wst[wctr[0] % 3]
                q = "sp" if wctr[0] % 2 == 0 else "act"
                wctr[0] += 1
                nk = k1 - k0
                src = w_ap[r0 + k0 * 128:r0 + k1 * 128, c0:c0 + m].rearrange("(k p) c -> p k c", p=128)
                S.dma(st[:, 0:nk * m].rearrange("p (k c) -> p k c", k=nk), src, writes=[Bst], q=q)
                ce = "act" if wctr[0] % 2 == 0 else "dve"
                if ce == "act":
                    fn = lambda e: e.copy(wb[:, k0 * m:k1 * m], st[:, 0:nk * m])
                else:
                    fn = lambda e: e.tensor_copy(wb[:, k0 * m:k1 * m], st[:, 0:nk * m])
                if first:
                    S.op(ce, fn, reads=[Bst], writes=[Bwb])
                else:
                    S.op(ce, fn, reads=[Bst, Bwb], writes=[])
                    Bwb.w[ce] = S.cnt[ce]
                first = False
            return wb, Bwb, kc

        class WQ:
            def __init__(self, specs):
                self.specs = specs
                self.pos = 0
                self.dpos = 0
                self.cpos = 0
                self.staged = {}
                self.tiles = {}

            def _dma(self):
                i = self.dpos
                (w_ap, r0, krows, c0, m, dst) = self.specs[i]
                kc = (krows + 127) // 128
                pieces = [(0, kc)] if kc <= 16 else [(0, 16), (16, kc)]
                lst = []
                for (k0, k1) in pieces:
                    st, Bst = wst[wctr[0] % 4]
                    wctr[0] += 1
                    nk = k1 - k0
                    src = w_ap[r0 + k0 * 128:r0 + k1 * 128, c0:c0 + m].rearrange("(k p) c -> p k c", p=128)
                    S.dma(st[:, 0:nk * m].rearrange("p (k c) -> p k c", k=nk), src, writes=[Bst], q="sp")
                    lst.append((st, Bst, k0, k1))
                self.staged[i] = lst
                self.dpos += 1

            def _cast(self):
                i = self.cpos
                (w_ap, r0, krows, c0, m, dst) = self.specs[i]
                kc = (krows + 127) // 128
                if dst is not None:
                    wb, Bwb = dst
                else:
                    wb, Bwb = wbf[wctr[1] % 2]
                    wctr[1] += 1
                first = True
                for (st, Bst, k0, k1) in self.staged.pop(i):
                    nk = k1 - k0
                    wctr[2] += 1
                    ce = "act" if wctr[2] % 2 == 0 else "dve"
                    if ce == "act":
                        fn = lambda e: e.copy(wb[:, k0 * m:k1 * m], st[:, 0:nk * m])
                    else:
                        fn = lambda e: e.tensor_copy(wb[:, k0 * m:k1 * m], st[:, 0:nk * m])
                    if first:
                        S.op(ce, fn, reads=[Bst], writes=[Bwb])
                    else:
                        S.op(ce, fn, reads=[Bst, Bwb], writes=[])
                        Bwb.w[ce] = S.cnt[ce]
                    first = False
                self.tiles[i] = (wb, Bwb, kc)
                self.cpos += 1

            def next(self):
                n = len(self.specs)
                while self.cpos < min(n, self.pos + 2):
                    if self.dpos <= self.cpos:
                        self._dma()
                    self._cast()
                while self.dpos < min(n, self.pos + 3):
                    self._dma()
                r = self.tiles.pop(self.pos)
                self.pos += 1
                return r

        esA = ExitStack()
        with esA:
            gmix, Bgmix = SB(esA, "gmix", [128, D], F32)
            S.dma(gmix[:], norm_mix.partition_broadcast(128), writes=[Bgmix])
            qkg, Bqkg = SB(esA, "qkg", [128, 2], F32)
            S.dma(qkg[:], qkg_d[:], writes=[Bqkg])
            S.op("dve", lambda e: e.tensor_scalar(qkg[:, 0:1], qkg[:, 0:1], 128.0 ** -0.5, None, ALU.mult), reads=[Bqkg], writes=[Bqkg])
            mixT, Bmix = SB(esA, "mixT", [128, 52], F32)
            omix, Bomix = SB(esA, "omix", [128, 52], F32)
            S.dma(mixT[:], mixT_d[:], writes=[Bmix])
            S.op("dve", lambda e: e.tensor_scalar(omix[:], mixT[:], -1.0, 1.0, ALU.mult, ALU.add), reads=[Bmix], writes=[Bomix])
            zlast, Bzl = SB(esA, "zlast", [128, 52], F32)
            S.op("dve", lambda e: e.memset(zlast[:], 0.0), writes=[Bzl])
            hT, BhT = SB(esA, "hT", [128, 32, 1024], BF16)
            xin = [SB(esA, "xin%d" % i, [128, D], F32) for i in range(1)]
            hb, Bhb = SB(esA, "hb", [128, D], BF16)
            ss, Bss = SB(esA, "ss", [128, 4], F32)
            ptr = [PS(esA, "ptr%d" % i, [128, 1024], BF16) for i in range(2)]
            pmm = [PS(esA, "pmm%d" % i, [128, 512]) for i in range(3)]
            pn = [PS(esA, "pn%d" % i, [128, 512]) for i in range(2)]
            sqb, Bsqb = SB(esA, "sqb", [128, 512], BF16)
            rsb, Brsb = SB(esA, "rsb", [128, 512], F32)
            stg16 = [SB(esA, "stg16_%d" % i, [128, 1024], BF16) for i in range(2)]
            zbufs = [SB(esA, "zbuf%d" % i, [128, 1025], F32) for i in range(2)]
            ztmp, Bztmp = SB(esA, "ztmp", [128, 1024], F32)
            zs = [SB(esA, "zs%d" % i, [128, 1024], F32) for i in range(2)]
            ctr = {"ev": 0, "mm": 0, "pn": 0, "s16": 0, "zs": 0}

            def a_blocks(qi):
                blocks = []
                for h in range(12):
                    blocks.append(("k", 1536 + h * 128, 128, h))
                for h in range(12):
                    blocks.append(("v", 3072 + h * 128, 128, h))
                for b in range(52):
                    blocks.append(("z", A_END + ZST[b], ZM[b], b))
                if qi >= 2:
                    for h in range(12):
                        blocks.append(("q", h * 128, 128, h))
                    for b in range(64):
                        blocks.append(("g", R_END + b * 128, 128, b))
                return blocks
            wqA = WQ([(w_in, 0, D, c0_, m_, None) for qi_ in range(4) for (_, c0_, m_, _) in a_blocks(qi_)])

            for qi in range(4):
                own = qi >= 2
                for sub in range(8):
                    xt, Bxt = xin[0]
                    t0 = qi * 1024 + sub * 128
                    S.dma(xt[:], x_d[t0:t0 + 128, :], writes=[Bxt], q="sp" if sub % 2 == 0 else "act")
                    S.op("dve", lambda e: e.memset(ss[:, 0:1], 0.0), writes=[Bss])
                    S.op("act", lambda e: e.activation(hb[:], xt[:], AF.Square, accum_out=ss[:, 0:1]), reads=[Bxt, Bss], writes=[Bhb, Bss])
                    S.op("act", lambda e: e.activation(ss[:, 1:2], ss[:, 0:1], AF.Ln, bias=1e-6, scale=1.0 / D), reads=[Bss], writes=[Bss])
                    S.op("act", lambda e: e.activation(ss[:, 2:3], ss[:, 1:2], AF.Exp, scale=-0.5), reads=[Bss], writes=[Bss])
                    S.op("dve", lambda e: e.scalar_tensor_tensor(hb[:], xt[:], ss[:, 2:3], gmix[:], ALU.mult, ALU.mult), reads=[Bxt, Bss, Bgmix], writes=[Bhb])
                    for g8 in range(4):
                        pt, Bpt = ptr[g8 % 2]
                        for j in range(8):
                            kc = g8 * 8 + j
                            S.op("pe", lambda e: e.transpose(pt[:, j * 128:(j + 1) * 128], hb[:, kc * 128:(kc + 1) * 128], identb[:]), reads=[Bhb, Bidb], writes=[Bpt])
                        en = "act" if g8 % 2 == 0 else "dve"
                        dst = hT[:, g8 * 8:(g8 + 1) * 8, sub * 128:(sub + 1) * 128]
                        srcp = pt[:].rearrange("p (k t) -> p k t", k=8)
                        if en == "act":
                            S.op("act", lambda e: e.copy(dst, srcp), reads=[Bpt], writes=[BhT])
                        else:
                            S.op("dve", lambda e: e.tensor_copy(dst, srcp), reads=[Bpt], writes=[BhT])
                blocks = []
                for h in range(12):
                    blocks.append(("k", 1536 + h * 128, 128, h))
                for h in range(12):
                    blocks.append(("v", 3072 + h * 128, 128, h))
                for b in range(52):
                    blocks.append(("z", A_END + ZST[b], ZM[b], b))
                if own:
                    for h in range(12):
                        blocks.append(("q", h * 128, 128, h))
                    for b in range(64):
                        blocks.append(("g", R_END + b * 128, 128, b))
                blocks = a_blocks(qi)
                for bi, (kind, c0, m, idx) in enumerate(blocks):
                    wb, Bwb, _ = wqA.next()
                    zbuf, Bzbuf = zbufs[bi % 2]
                    if kind in ("k", "q", "v", "g"):
                        st, Bstg = stg16[ctr["s16"] % 2]
                        ctr["s16"] += 1
                    for half in range(2):
                        pm_, Bpm = pmm[ctr["mm"] % 3]
                        ctr["mm"] += 1
                        for kc in range(32):
                            S.op("pe", lambda e: e.matmul(pm_[0:m, :], wb[:, kc * m:(kc + 1) * m], hT[:, kc, half * 512:(half + 1) * 512], start=(kc == 0), stop=(kc == 31)), reads=[Bwb, BhT], writes=[Bpm])
                        if kind in ("k", "q"):
                            d = DILS[idx // 4]
                            S.op("act", lambda e: e.activation(sqb[:], pm_[:], AF.Square), reads=[Bpm], writes=[Bsqb])
                            pn_, Bpn = pn[ctr["pn"] % 2]
                            ctr["pn"] += 1
                            S.op("pe", lambda e: e.matmul(pn_[:], onesb[:], sqb[:], start=True, stop=True), reads=[Bones, Bsqb], writes=[Bpn])
                            S.op("act", lambda e: e.activation(rsb[:], pn_[:], AF.Ln, bias=1e-6, scale=1.0 / 128), reads=[Bpn], writes=[Brsb])
                            S.op("act", lambda e: e.activation(rsb[:], rsb[:], AF.Exp, scale=-0.5), reads=[Brsb], writes=[Brsb])
                            gcol = qkg[:, 0:1] if kind == "q" else qkg[:, 1:2]
                            n_i = 512 // d
                            dst = st[:].rearrange("p (r i) -> p r i", r=d)[:, :, half * n_i:(half + 1) * n_i]
                            S.op("dve", lambda e: e.scalar_tensor_tensor(dst, pm_[:].rearrange("p (i r) -> p r i", r=d), gcol, rsb[:].rearrange("p (i r) -> p r i", r=d), ALU.mult, ALU.mult), reads=[Bpm, Bqkg, Brsb], writes=[Bstg])
                        elif kind == "v":
                            d = DILS[idx // 4]
                            n_i = 512 // d
                            dst = st[:].rearrange("p (r i) -> p r i", r=d)[:, :, half * n_i:(half + 1) * n_i]
                            S.op("act", lambda e: e.copy(dst, pm_[:].rearrange("p (i r) -> p r i", r=d)), reads=[Bpm], writes=[Bstg])
                        elif kind == "g":
                            S.op("act", lambda e: e.activation(st[:, half * 512:(half + 1) * 512], pm_[:], AF.Sigmoid), reads=[Bpm], writes=[Bstg])
                        else:
                            S.op("act", lambda e: e.copy(zbuf[0:m, 1 + half * 512:1 + (half + 1) * 512], pm_[0:m, :]), reads=[Bpm], writes=[Bzbuf])
                    if kind in ("k", "v", "q"):
                        d = DILS[idx // 4]
                        if kind == "q":
                            dd = QT[idx].rearrange("e (r i) -> e r i", r=d)[:, :, (qi - 2) * (1024 // d):(qi - 1) * (1024 // d)]
                            Bd = B_QT
                        else:
                            base = KT if kind == "k" else VT
                            dd = base[idx].rearrange("e (r i) -> e r i", r=d)[:, :, qi * (1024 // d):(qi + 1) * (1024 // d)]
                            Bd = B_KT if kind == "k" else B_VT
                        S.dma(dd, st[:].rearrange("p (r i) -> p r i", r=d), reads=[Bstg], acc=[Bd])
                    elif kind == "g":
                        S.dma(GT[idx * 128:(idx + 1) * 128, (qi - 2) * 1024:(qi - 1) * 1024], st[:], reads=[Bstg], acc=[B_GT])
                    else:
                        b = idx
                        S.op("dve", lambda e: e.tensor_copy(zbuf[0:m, 0:1], zlast[0:m, b:b + 1]), reads=[Bzl], writes=[Bzbuf])
                        S.op("dve", lambda e: e.tensor_scalar(ztmp[0:m, :], zbuf[0:m, 0:1024], mixT[0:m, b:b + 1], None, ALU.mult), reads=[Bzbuf, Bmix], writes=[Bztmp])
                        zt, Bzt = zs[ctr["zs"] % 2]
                        ctr["zs"] += 1
                        S.op("dve", lambda e: e.scalar_tensor_tensor(zt[0:m, :], zbuf[0:m, 1:1025], omix[0:m, b:b + 1], ztmp[0:m, :], ALU.mult, ALU.add), reads=[Bzbuf, Bomix, Bztmp], writes=[Bzt])
                        S.op("dve", lambda e: e.tensor_copy(zlast[0:m, b:b + 1], zbuf[0:m, 1024:1025]), reads=[Bzbuf], writes=[Bzl])
                        S.dma(ZT[ZST[b]:ZST[b] + m, qi * 1024:(qi + 1) * 1024], zt[0:m, :], reads=[Bzt], acc=[B_ZT])


        esB = ExitStack()
        if stop == "A":
            S.drain("sp")
            return nc
        with esB:
            rb33, Brb = SB(esB, "rb33", [33, 12], F32)
            oh, Boh = SB(esB, "oh", [33, 3, 383], F32)
            S.dma(rb33[:], rb33_d[:], writes=[Brb])
            S.dma(oh[:], oh_d[:], writes=[Boh])
            pmc, Bpmc = SB(esB, "pmc", [128, 1], F32)
            S.dma(pmc[:], pm_d[:], writes=[Bpmc])
            gsb, Bgsb = SB(esB, "gsb", [4, 383], F32)
            pg, Bpg = PS(esB, "pg", [4, 383])
            for g in range(3):
                S.op("pe", lambda e: e.matmul(pg[:], rb33[:, g * 4:(g + 1) * 4], oh[:, g, :], start=True, stop=True), reads=[Brb, Boh], writes=[Bpg])
                S.op("dve", lambda e: e.tensor_copy(gsb[:], pg[:]), reads=[Bpg], writes=[Bgsb])
                srcb = gsb[:].unsqueeze(1).to_broadcast([4, 130, 383])
                S.dma(GS[g * 4:(g + 1) * 4, :, :], srcb, reads=[Bgsb], acc=[B_GS])
            biasT, Bbias = SB(esB, "biasT", [128, 12, 2, 128], F32)
            for h in range(12):
                for tl, cc in ((0, 255), (1, 127)):
                    src = bass.AP(GS.tensor, GS[h].offset + cc, [[382, 128], [1, 128]])
                    S.dma(biasT[:, h, tl, :], src, reads=[B_GS], writes=[Bbias])
            Ksb = [SB(esB, "Ksb%d" % i, [128, SEQ], BF16) for i in range(2)]
            Vsb = [SB(esB, "Vsb%d" % i, [128, SEQ], BF16) for i in range(2)]
            Qsb = [SB(esB, "Qsb%d" % i, [128, OWN], BF16) for i in range(2)]
            Vtok = [SB(esB, "Vtok%d" % i, [128, 32, 130], BF16) for i in range(2)]
            for i in range(2):
                S.op("dve", lambda e: e.memset(Vtok[i][0][:], 1.0), writes=[Vtok[i][1]])
            pvt = [PS(esB, "pvt%d" % i, [128, 128], BF16) for i in range(2)]
            pss = [PS(esB, "pss%d" % i, [128, 2, 128]) for i in range(2)]
            pso = [PS(esB, "pso%d" % i, [128, 129]) for i in range(2)]
            ssb = [SB(esB, "ssb%d" % i, [128, 2, 128], F32) for i in range(2)]
            esb = [SB(esB, "esb%d" % i, [128, 2, 128], BF16) for i in range(2)]
            osb = [SB(esB, "osb%d" % i, [128, 129], F32) for i in range(3)]
            it = 0
            for h in range(12):
                d = DILS[h // 4]
                Lc = SEQ // d
                nbc = Lc // 128
                nb0 = nbc // 2
                K_, BK = Ksb[h % 2]
                V_, BV = Vsb[h % 2]
                Q_, BQ = Qsb[h % 2]
                Vt, BVt = Vtok[h % 2]
                S.dma(K_[:], KT[h], reads=[B_KT], writes=[BK])
                S.dma(V_[:], VT[h], reads=[B_VT], writes=[BV], q="act")
                S.dma(Q_[:], QT[h], reads=[B_QT], writes=[BQ])
                nkb = nb0 + 1
                for r in range(d):
                    for kb in range(nb0 - 1, nbc):
                        vi = r * nkb + (kb - (nb0 - 1))
                        pv, Bpv = pvt[vi % 2]
                        S.op("pe", lambda e: e.transpose(pv[:], V_[:, r * Lc + kb * 128:r * Lc + (kb + 1) * 128], identb[:]), reads=[BV, Bidb], writes=[Bpv])
                        if vi % 2 == 0:
                            S.op("act", lambda e: e.copy(Vt[:, vi, 0:128], pv[:]), reads=[Bpv], writes=[BVt])
                        else:
                            S.op("dve", lambda e: e.tensor_copy(Vt[:, vi, 0:128], pv[:]), reads=[Bpv], writes=[BVt])
                for r in range(d):
                    for n in range(nb0, nbc):
                        ps_, Bps = pss[it % 2]
                        po, Bpo = pso[it % 2]
                        s_, Bs = ssb[it % 2]
                        e_, Be = esb[it % 2]
                        o_, Bo = osb[it % 3]
                        it += 1
                        qt = Q_[:, r * (Lc // 2) + (n - nb0) * 128:r * (Lc // 2) + (n - nb0 + 1) * 128]
                        for tl in range(2):
                            kb = n - 1 + tl
                            S.op("pe", lambda e: e.matmul(ps_[:, tl, :], K_[:, r * Lc + kb * 128:r * Lc + (kb + 1) * 128], qt, start=True, stop=True), reads=[BK, BQ], writes=[Bps])
                        S.op("dve", lambda e: e.tensor_tensor(s_[:], ps_[:], biasT[:, h], ALU.add), reads=[Bps, Bbias], writes=[Bs])
                        if n == nb0:
                            S.op("act", lambda e: e.activation(e_[:, 0, :], s_[:, 0, :], AF.Exp, bias=pmc[:, 0:1]), reads=[Bs, Bpmc], writes=[Be])
                            S.op("act", lambda e: e.activation(e_[:, 1, :], s_[:, 1, :], AF.Exp), reads=[Bs], writes=[Be])
                        else:
                            S.op("act", lambda e: e.activation(e_[:], s_[:], AF.Exp), reads=[Bs], writes=[Be])
                        for tl in range(2):
                            vi = r * nkb + (n - 1 + tl - (nb0 - 1))
                            S.op("pe", lambda e: e.matmul(po[:], e_[:, tl, :], Vt[:, vi, 0:129], start=(tl == 0), stop=(tl == 1)), reads=[Be, BVt], writes=[Bpo])
                        S.op("dve", lambda e: e.tensor_copy(o_[:], po[:]), reads=[Bpo], writes=[Bo])
                        tok0 = (n * 128) * d + r - OWN
                        dst = bass.AP(OS.tensor, (tok0 * 12 + h) * 129, [[d * 12 * 129, 128], [1, 129]])
                        S.dma(dst, o_[:], reads=[Bo], acc=[B_OS])

        esC = ExitStack()
        if stop == "B":
            S.drain("sp")
            return nc
        with esC:
            masks, Bmk = SB(esC, "masks", [128, 4, 128], F32)
            S.dma(masks[:], masks_d[:], writes=[Bmk])
            scm, Bscm = SB(esC, "scm", [128, 1024], F32)
            S.dma(scm[:], scanmask_d[:], writes=[Bscm])
            blk1, Bblk = SB(esC, "blk1", [128, 128], F32)
            S.dma(blk1[:], blkones_d[:], writes=[Bblk])
            blk1b, Bblkb = SB(esC, "blk1b", [128, 128], BF16)
            S.op("dve", lambda e: e.tensor_copy(blk1b[:], blk1[:]), reads=[Bblk], writes=[Bblkb])
            rwp, Brwp = SB(esC, "rwp", [128, 5, 16], F32)
            S.dma(rwp[:], rwp_d[:], writes=[Brwp])
            nrw, Bnrw = SB(esC, "nrw", [128, 2, 16], F32)
            S.op("dve", lambda e: e.tensor_scalar(nrw[:, 0, :], rwp[:, 0, :], -1.0, None, ALU.mult), reads=[Brwp], writes=[Bnrw])
            S.op("dve", lambda e: e.tensor_scalar(nrw[:, 1, :], rwp[:, 3, :], -1.0, 1.0, ALU.mult, ALU.add), reads=[Brwp], writes=[Bnrw])
            txw, Btxw = SB(esC, "txw", [96, SEQ], BF16)
            xab, Bxab = SB(esC, "xab", [96, SEQ], BF16)
            sxg, Bsxg = SB(esC, "sxg", [128, 2, SEQ], BF16)
            f32n = ["r", "k", "v", "ew", "cle", "t0", "t1", "t2", "asig", "kkn", "k2", "bb"]
            F = {n: SB(esC, "f_" + n, [128, 1024], F32) for n in f32n}
            ltmp = [F["t0"], F["t1"]]
            li = 0
            for sgi in range(4):
                for (row0, m, kind) in ((6144, 96, "w"), (6240, 96, "a"), (6336, 128, "g0"), (6464, 128, "g1")):
                    lt, Blt = ltmp[li % 2]
                    li += 1
                    S.dma(lt[0:m, :], ZT[row0:row0 + m, sgi * 1024:(sgi + 1) * 1024], reads=[B_ZT], writes=[Blt], q="sp" if li % 2 else "act")
                    sl = slice(sgi * 1024, (sgi + 1) * 1024)
                    if kind == "w":
                        S.op("act", lambda e: e.activation(txw[:, sl], lt[0:96, :], AF.Tanh), reads=[Blt], writes=[Btxw])
                    elif kind == "a":
                        S.op("dve", lambda e: e.tensor_copy(xab[:, sl], lt[0:96, :]), reads=[Blt], writes=[Bxab])
                    else:
                        gi = 0 if kind == "g0" else 1
                        S.op("act", lambda e: e.activation(sxg[:, gi, sl], lt[:, :], AF.Sigmoid), reads=[Blt], writes=[Bsxg])
            wdec, Bwdec = SB(esC, "wdec", [96, 2048], BF16)
            waaa, Bwaaa = SB(esC, "waaa", [96, 2048], BF16)
            wgat, Bwgat = SB(esC, "wgat", [128, 2, 2048], BF16)
            for (dst, Bd, src, rows) in ((wdec, Bwdec, w_decay_up, 96), (waaa, Bwaaa, w_aaa_up, 96)):
                for cch in range(2):
                    st, Bst = wst[cch]
                    S.dma(st[0:rows, 0:1024], src[:, cch * 1024:(cch + 1) * 1024], writes=[Bst])
                    S.op("pool", lambda e: e.tensor_copy(dst[:, cch * 1024:(cch + 1) * 1024], st[0:rows, 0:1024]), reads=[Bst], writes=[Bd])
            for kc in range(2):
                for cch in range(2):
                    st, Bst = wst[cch]
                    S.dma(st[:, 0:1024], w_gate_up[kc * 128:(kc + 1) * 128, cch * 1024:(cch + 1) * 1024], writes=[Bst])
                    S.op("pool", lambda e: e.tensor_copy(wgat[:, kc, cch * 1024:(cch + 1) * 1024], st[:, 0:1024]), reads=[Bst], writes=[Bwgat])

            b16n = ["bt", "kt", "btc", "ktc", "vb", "rk"]
            Bt = {n: SB(esC, "b_" + n, [128, 1024], BF16) for n in b16n}
            ARt, BARt = SB(esC, "ARt", [128, 16, 128], BF16)
            wcs, Bwcs = SB(esC, "wcs", [128, 16], F32)
            GNW, BGNW = SB(esC, "GNW", [128, 64], F32)
            GNB, BGNB = SB(esC, "GNB", [128, 64], F32)
            S32, BS32 = SB(esC, "S32", [128, 64], F32)
            Sbf, BSbf = SB(esC, "Sbf", [128, 64], BF16)
            rwTs, BrwTs = SB(esC, "rwTs", [128, 1024], BF16)

            def rot(name, shape, dt, n):
                return [SB(esC, "%s%d" % (name, i), shape, dt) for i in range(n)]
            ARB_ = rot("ARB", [128, 4, 128], BF16, 2)
            AK_ = rot("AK", [128, 4, 128], BF16, 2)
            N_ = rot("N", [128, 4, 128], BF16, 4)
            NT_ = rot("NT", [128, 4, 128], BF16, 4)
            TT_ = rot("TT", [128, 4, 128], BF16, 8)
            TOK_ = rot("TOK", [128, 4, 192], BF16, 2)
            Xb_ = rot("Xb", [128, 64], BF16, 2)
            Ub_ = rot("Ub", [128, 64], BF16, 2)
            ysb_ = rot("ysb", [128, 64], F32, 2)
            ycn_ = rot("ycn", [128, 64], F32, 2)
            yjk_ = rot("yjk", [128, 64], F32, 2)
            st4_ = rot("st4", [128, 8], F32, 2)
            rwb_ = rot("rwb", [128, 64], BF16, 2)
            ps1, Bps1 = PS(esC, "ps1", [128, 512])
            ps2, Bps2 = PS(esC, "ps2", [128, 512])
            ps3, Bps3 = PS(esC, "ps3", [128, 512])
            pN, BpN = PS(esC, "pN", [128, 512])
            pNT, BpNT = PS(esC, "pNT", [128, 512])
            pT, BpT = PS(esC, "pT", [128, 512])
            pxu, Bpxu = PS(esC, "pxu", [128, 512])
            pys, Bpys = PS(esC, "pys", [128, 512])
            plo, Bplo = pT, BpT
            zb, Bzb = SB(esC, "zb", [128, 512], BF16)
            S.op("dve", lambda e: e.memset(zb[:], 0.0), writes=[Bzb])
            S.op("pe", lambda e: e.matmul(ps3[:], zb[:, 0:128], zb[:], start=True, stop=True), reads=[Bzb], writes=[Bps3])

            def Fv(n):
                return F[n][0], F[n][1]

            def precast_gen():
                pi_ = 0
                for bi_, (w_ap, r0, krows, c0_) in enumerate(tail_blocks):
                    kc = (krows + 127) // 128
                    pieces = [(0, kc)] if kc <= 16 else [(0, 16), (16, kc)]
                    for (k0, k1) in pieces:
                        nk = k1 - k0
                        st, Bst = wst[pi_ % 4]
                        ob, Bob = wbf[pi_ % 2]
                        pi_ += 1
                        src = w_ap[r0 + k0 * 128:r0 + k1 * 128, c0_:c0_ + 128].rearrange("(k p) c -> p k c", p=128)
                        S.dma(st[:, 0:nk * 128].rearrange("p (k c) -> p k c", k=nk), src, writes=[Bst], q="sp")
                        S.op("pool", lambda e: e.tensor_copy(ob[:, 0:nk * 128], st[:, 0:nk * 128]), reads=[Bst], writes=[Bob])
                        S.dma(WBs[wb_off[bi_][0]][:, wb_off[bi_][1] + k0 * 128:wb_off[bi_][1] + k1 * 128], ob[:, 0:nk * 128], reads=[Bob], acc=[B_WB], q="pool")
                        yield None
            precast = precast_gen()

            for hp in range(16):
                S.dma(GNW[0:64, :], gnw_d[0:1, hp * 128:hp * 128 + 64].partition_broadcast(64), writes=[BGNW])
                S.dma(GNW[64:128, :], gnw_d[0:1, hp * 128 + 64:hp * 128 + 128].partition_broadcast(64), writes=[BGNW])
                S.dma(GNB[0:64, :], gnb_d[0:1, hp * 128:hp * 128 + 64].partition_broadcast(64), writes=[BGNB])
                S.dma(GNB[64:128, :], gnb_d[0:1, hp * 128 + 64:hp * 128 + 128].partition_broadcast(64), writes=[BGNB])
                S.op("dve", lambda e: e.memset(S32[:], 0.0), writes=[BS32])
                S.op("dve", lambda e: e.memset(Sbf[:], 0.0), writes=[BSbf])
                c0 = hp * 128
                for sgi in range(4):
                    own = sgi >= 2
                    tsl = slice(sgi * 1024, (sgi + 1) * 1024)
                    r_, Br = Fv("r"); k_, Bk = Fv("k"); v_, Bv = Fv("v")
                    S.dma(r_[:], ZT[c0:c0 + 128, tsl], reads=[B_ZT], writes=[Br])
                    S.dma(k_[:], ZT[2048 + c0:2048 + c0 + 128, tsl], reads=[B_ZT], writes=[Bk], q="act")
                    S.dma(v_[:], ZT[4096 + c0:4096 + c0 + 128, tsl], reads=[B_ZT], writes=[Bv])
                    ew, Bew = Fv("ew"); cle, Bcle = Fv("cle"); t0_, Bt0 = Fv("t0"); t1_, Bt1 = Fv("t1"); t2_, Bt2 = Fv("t2")
                    asig, Basig = Fv("asig"); kkn, Bkkn = Fv("kkn"); k2, Bk2 = Fv("k2"); bb, Bbb = Fv("bb")
                    for hf in range(2):
                        hs = slice(hf * 512, (hf + 1) * 512)
                        gs = slice(sgi * 1024 + hf * 512, sgi * 1024 + (hf + 1) * 512)
                        S.op("pe", lambda e: e.matmul(plo[:], wdec[:, c0:c0 + 128], txw[:, gs], start=True, stop=True), reads=[Bwdec, Btxw], writes=[Bplo])
                        S.op("act", lambda e: e.activation(t0_[:, hs], plo[:], AF.Exp, bias=nrw[:, 0, hp:hp + 1], scale=-1.0), reads=[Bplo, Bnrw], writes=[Bt0])
                        S.op("pe", lambda e: e.matmul(plo[:], waaa[:, c0:c0 + 128], xab[:, gs], start=True, stop=True), reads=[Bwaaa, Bxab], writes=[Bplo])
                        S.op("act", lambda e: e.activation(asig[:, hs], plo[:], AF.Sigmoid, bias=rwp[:, 1, hp:hp + 1]), reads=[Bplo, Brwp], writes=[Basig])
                    S.op("act", lambda e: e.activation(t0_[:], t0_[:], AF.Ln, bias=1.0), reads=[Bt0], writes=[Bt0])
                    S.op("act", lambda e: e.activation(ew[:], t0_[:], AF.Exp, bias=-0.5, scale=-1.0), reads=[Bt0], writes=[Bew])
                    S.op("dve", lambda e: e.tensor_tensor_scan(cle[:], scm[:], ew[:], 0.0, ALU.mult, ALU.add), reads=[Bscm, Bew], writes=[Bcle])
                    S.op("pool", lambda e: e.tensor_scalar(kkn[:], k_[:], rwp[:, 2, hp:hp + 1], None, ALU.mult), reads=[Bk, Brwp], writes=[Bkkn])
                    S.op("pool", lambda e: e.tensor_tensor(t1_[:], kkn[:], kkn[:], ALU.mult), reads=[Bkkn], writes=[Bt1])
                    for hf in range(2):
                        hs = slice(hf * 512, (hf + 1) * 512)
                        S.op("pe", lambda e: e.matmul(plo[:], blk1[:], t1_[:, hs], start=True, stop=True), reads=[Bblk, Bt1], writes=[Bplo])
                        S.op("act", lambda e: e.activation(t2_[:, hs], plo[:], AF.Ln, bias=1e-12), reads=[Bplo], writes=[Bt2])
                    S.op("act", lambda e: e.activation(t2_[:], t2_[:], AF.Exp, scale=-0.5), reads=[Bt2], writes=[Bt2])
                    S.op("dve", lambda e: e.tensor_tensor(kkn[:], kkn[:], t2_[:], ALU.mult), reads=[Bkkn, Bt2], writes=[Bkkn])
                    S.op("pool", lambda e: e.tensor_scalar(t1_[:], asig[:], rwp[:, 3, hp:hp + 1], nrw[:, 1, hp:hp + 1], ALU.mult, ALU.add), reads=[Basig, Brwp, Bnrw], writes=[Bt1])
                    S.op("dve", lambda e: e.tensor_tensor(k2[:], k_[:], t1_[:], ALU.mult), reads=[Bk, Bt1], writes=[Bk2])
                    S.op("pool", lambda e: e.tensor_tensor(bb[:], kkn[:], asig[:], ALU.mult), reads=[Bkkn, Basig], writes=[Bbb])
                    S.op("act", lambda e: e.activation(t0_[:], cle[:], AF.Exp), reads=[Bcle], writes=[Bt0])
                    S.op("dve", lambda e: e.tensor_tensor(Bt["bt"][0][:], bb[:], t0_[:], ALU.mult), reads=[Bbb, Bt0], writes=[Bt["bt"][1]])
                    S.op("pool", lambda e: e.tensor_tensor(Bt["kt"][0][:], k2[:], t0_[:], ALU.mult), reads=[Bk2, Bt0], writes=[Bt["kt"][1]])
                    S.op("act", lambda e: e.activation(t1_[:], cle[:], AF.Exp, scale=-1.0), reads=[Bcle], writes=[Bt1])
                    S.op("dve", lambda e: e.tensor_tensor(t2_[:], ew[:], cle[:], ALU.subtract), reads=[Bew, Bcle], writes=[Bt2])
                    S.op("act", lambda e: e.activation(t2_[:], t2_[:], AF.Exp), reads=[Bt2], writes=[Bt2])
                    for pb in (0, 64):
                        ca, cr = pb, 64 - pb
                        ps_ = slice(pb, pb + 64)
                        S.op("dve", lambda e: e.tensor_tensor(ARt[ps_, :, cr:cr + 64], r_[ps_, :].rearrange("p (c t) -> p c t", t=64), t1_[ps_, :].rearrange("p (c t) -> p c t", t=64), ALU.mult), reads=[Br, Bt1], writes=[BARt])
                        S.op("dve", lambda e: e.scalar_tensor_tensor(ARt[ps_, :, ca:ca + 64], kkn[ps_, :].rearrange("p (c t) -> p c t", t=64), -1.0, t2_[ps_, :].rearrange("p (c t) -> p c t", t=64), ALU.mult, ALU.mult), reads=[Bkkn, Bt2], writes=[BARt])
                    cle3 = cle[:].rearrange("p (c t) -> p c t", t=64)
                    S.op("dve", lambda e: e.tensor_tensor(t0_[:].rearrange("p (c t) -> p c t", t=64), cle3, cle3[:, :, 63:64].to_broadcast([128, 16, 64]), ALU.subtract), reads=[Bcle], writes=[Bt0])
                    S.op("act", lambda e: e.activation(t0_[:], t0_[:], AF.Exp), reads=[Bt0], writes=[Bt0])
                    S.op("act", lambda e: e.activation(wcs[:], cle3[:, :, 63], AF.Exp, scale=-1.0), reads=[Bcle], writes=[Bwcs])
                    S.op("dve", lambda e: e.tensor_tensor(Bt["btc"][0][:], bb[:], t0_[:], ALU.mult), reads=[Bbb, Bt0], writes=[Bt["btc"][1]])
                    S.op("pool", lambda e: e.tensor_tensor(Bt["ktc"][0][:], k2[:], t0_[:], ALU.mult), reads=[Bk2, Bt0], writes=[Bt["ktc"][1]])
                    S.op("act", lambda e: e.copy(Bt["vb"][0][:], v_[:]), reads=[Bv], writes=[Bt["vb"][1]])
                    if own:
                        S.op("pool", lambda e: e.tensor_tensor(t1_[:], r_[:], k2[:], ALU.mult), reads=[Br, Bk2, Bt1], writes=[Bt1])
                        S.op("pool", lambda e: e.tensor_scalar(Bt["rk"][0][:], t1_[:], rwp[:, 4, hp:hp + 1], None, ALU.mult), reads=[Bt1, Brwp], writes=[Bt["rk"][1]])
                    bt, Bbt = Bt["bt"]; kt, Bkt = Bt["kt"]; btc, Bbtc = Bt["btc"]; ktc, Bktc = Bt["ktc"]; vb, Bvb = Bt["vb"]; rk, Brk = Bt["rk"]


                    def stage12(b4):
                        par = b4 % 2
                        ARB, BARB = ARB_[par]; AK, BAK = AK_[par]; TOK, BTOK = TOK_[par]
                        for j in range(4):
                            c = b4 * 4 + j
                            cs = slice(c * 64, (c + 1) * 64)
                            pk, Bpk = (pN, BpN) if j < 2 else (pNT, BpNT)
                            ko = (j % 2) * 192
                            for pb in (0, 64):
                                ca = pb
                                P = slice(pb, pb + 64)
                                tp = (pb, pb)
                                S.op("pe", lambda e: e.matmul(ps1[P, j * 128:(j + 1) * 128], bt[P, cs], ARt[P, c, :], start=True, stop=True, tile_position=tp), reads=[Bbt, BARt], writes=[Bps1])
                                S.op("pe", lambda e: e.matmul(ps2[P, j * 128:(j + 1) * 128], kt[P, cs], ARt[P, c, :], start=True, stop=True, tile_position=tp), reads=[Bkt, BARt], writes=[Bps2])
                                S.op("pe", lambda e: e.matmul(ps3[P, j * 128 + ca:j * 128 + ca + 64], ARt[P, c, ca:ca + 64], bt[P, cs], start=True, stop=True, tile_position=tp), reads=[Bbt, BARt], writes=[Bps3])
                                S.op("pe", lambda e: e.matmul(pk[P, ko:ko + 64], btc[P, cs], identb[P, pb:pb + 64], start=True, stop=True, tile_position=tp), reads=[Bbtc, Bidb], writes=[Bpk])
                                S.op("pe", lambda e: e.matmul(pk[P, ko + 64:ko + 128], ktc[P, cs], identb[P, pb:pb + 64], start=True, stop=True, tile_position=tp), reads=[Bktc, Bidb], writes=[Bpk])
                                S.op("pe", lambda e: e.matmul(pk[P, ko + 128:ko + 192], vb[P, cs], identb[P, pb:pb + 64], start=True, stop=True, tile_position=tp), reads=[Bvb, Bidb], writes=[Bpk])
                        N0, BN0 = N_[0]; NT0, BNT0 = NT_[0]; TT0, BTT0 = TT_[par * 4]
                        v4 = lambda t: t[:].rearrange("p (j c) -> p j c", j=4)
                        mb = lambda i: masks[:, i, :].unsqueeze(1).to_broadcast([128, 4, 128])
                        S.op("dve", lambda e: e.tensor_tensor(NT0[:], v4(ps1), mb(0), ALU.mult), reads=[Bps1, Bmk], writes=[BNT0])
                        S.op("dve", lambda e: e.tensor_tensor(N0[:], v4(ps3), mb(3), ALU.mult), reads=[Bps3, Bmk], writes=[BN0])
                        S.op("dve", lambda e: e.tensor_tensor(TT0[:], NT0[:], identb[:].unsqueeze(1).to_broadcast([128, 4, 128]), ALU.add), reads=[BNT0, Bidb], writes=[BTT0])
                        S.op("dve", lambda e: e.tensor_tensor(AK[:], v4(ps2), mb(2), ALU.mult), reads=[Bps2, Bmk], writes=[BAK])
                        S.op("dve", lambda e: e.tensor_tensor(ARB[:], v4(ps1), mb(1), ALU.mult), reads=[Bps1, Bmk], writes=[BARB])
                        S.op("act", lambda e: e.copy(TOK[:, 0:2, :], pN[:, 0:384].rearrange("p (j c) -> p j c", j=2)), reads=[BpN], writes=[BTOK])
                        S.op("act", lambda e: e.copy(TOK[:, 2:4, :], pNT[:, 0:384].rearrange("p (j c) -> p j c", j=2)), reads=[BpNT], writes=[BTOK])
                        yield None
                        Nc, BNc = N0, BN0
                        NTc, BNTc = NT0, BNT0
                        TTc, BTTc = TT0, BTT0
                        for kq in range(1, 6):
                            Nn, BNn = N_[1 + (kq % 3)]
                            NTn, BNTn = NT_[1 + (kq % 3)]
                            TTn, BTTn = TT_[par * 4 + 1 + (kq % 3)]
                            for j in range(4):
                                S.op("pe", lambda e: e.matmul(pN[:, j * 128:(j + 1) * 128], NTc[:, j, :], Nc[:, j, :], start=True, stop=True), reads=[BNTc, BNc], writes=[BpN])
                            if kq < 5:
                                for j in range(4):
                                    S.op("pe", lambda e: e.matmul(pNT[:, j * 128:(j + 1) * 128], Nc[:, j, :], NTc[:, j, :], start=True, stop=True), reads=[BNTc, BNc], writes=[BpNT])
                            S.op("dve", lambda e: e.tensor_copy(Nn[:], v4(pN)), reads=[BpN], writes=[BNn])
                            if kq < 5:
                                S.op("act", lambda e: e.copy(NTn[:], v4(pNT)), reads=[BpNT], writes=[BNTn])
                            for j in range(4):
                                S.op("pe", lambda e: e.matmul(pT[:, j * 128:(j + 1) * 128], Nn[:, j, :], TTc[:, j, :], start=True, stop=True), reads=[BNn, BTTc], writes=[BpT])
                            S.op("dve", lambda e: e.tensor_tensor(TTn[:], v4(pT), TTc[:], ALU.add), reads=[BpT, BTTc], writes=[BTTn])
                            Nc, BNc, NTc, BNTc, TTc, BTTc = Nn, BNn, NTn, BNTn, TTn, BTTn
                            yield None
                        yield (TTc, BTTc)

                    def recur(b4, j, TTc, BTTc):
                        par = b4 % 2
                        ARB, BARB = ARB_[par]; AK, BAK = AK_[par]; TOK, BTOK = TOK_[par]
                        c = b4 * 4 + j
                        gc = sgi * 16 + c
                        cs = slice(c * 64, (c + 1) * 64)
                        Xb, BXb = Xb_[gc % 2]; Ub, BUb = Ub_[gc % 2]
                        for pb in (0, 64):
                            ca = pb
                            P = slice(pb, pb + 64)
                            tp = (pb, pb)
                            S.op("pe", lambda e: e.matmul(pxu[P, 0:64], ARt[P, c, ca:ca + 64], Sbf[P, :], start=True, stop=False, tile_position=tp), reads=[BARt, BSbf], writes=[Bpxu])
                            S.op("pe", lambda e: e.matmul(pxu[P, 0:64], AK[P, j, ca:ca + 64], TOK[P, j, 128:192], start=False, stop=True, tile_position=tp), reads=[BAK, BTOK], writes=[Bpxu])
                        S.op("act", lambda e: e.copy(Xb[:], pxu[:, 0:64]), reads=[Bpxu], writes=[BXb])
                        for pb in (0, 64):
                            P = slice(pb, pb + 64)
                            S.op("pe", lambda e: e.matmul(pxu[P, 64:128], TTc[P, j, pb:pb + 64], Xb[P, :], start=True, stop=True, tile_position=(pb, pb)), reads=[BTTc, BXb], writes=[Bpxu])
                        S.op("dve", lambda e: e.tensor_copy(Ub[:], pxu[:, 64:128]), reads=[Bpxu], writes=[BUb])
                        for pb in (0, 64):
                            P = slice(pb, pb + 64)
                            tp = (pb, pb)
                            S.op("pe", lambda e: e.matmul(pxu[P, 128:192], TOK[P, j, 0:64], Ub[P, :], start=True, stop=False, tile_position=tp), reads=[BTOK, BUb], writes=[Bpxu])
                            S.op("pe", lambda e: e.matmul(pxu[P, 128:192], TOK[P, j, 64:128], TOK[P, j, 128:192], start=False, stop=True, tile_position=tp), reads=[BTOK], writes=[Bpxu])
                        if own:
                            for pb in (0, 64):
                                cr = 64 - pb
                                P = slice(pb, pb + 64)
                                tp = (pb, pb)
                                S.op("pe", lambda e: e.matmul(pys[P, 0:64], ARt[P, c, cr:cr + 64], Sbf[P, :], start=True, stop=False, tile_position=tp), reads=[BARt, BSbf], writes=[Bpys])
                                S.op("pe", lambda e: e.matmul(pys[P, 0:64], AK[P, j, cr:cr + 64], TOK[P, j, 128:192], start=False, stop=False, tile_position=tp), reads=[BAK, BTOK], writes=[Bpys])
                                S.op("pe", lambda e: e.matmul(pys[P, 0:64], ARB[P, j, cr:cr + 64], Ub[P, :], start=False, stop=True, tile_position=tp), reads=[BARB, BUb], writes=[Bpys])
                                S.op("pe", lambda e: e.matmul(pys[P, 128:129], rk[P, cs], blk1b[P, pb:pb + 1], start=True, stop=True, tile_position=tp), reads=[Brk, Bblkb], writes=[Bpys])
                                gtok = slice(sgi * 1024 + c * 64, sgi * 1024 + (c + 1) * 64)
                                hc = c0 + pb
                                for kc in range(2):
                                    S.op("pe", lambda e: e.matmul(pys[P, 64:128], sxg[:, kc, gtok], wgat[:, kc, hc:hc + 64], start=(kc == 0), stop=(kc == 1), tile_position=(0, pb)), reads=[Bsxg, Bwgat], writes=[Bpys])
                        S.op("dve", lambda e: e.scalar_tensor_tensor(S32[:], S32[:], wcs[:, c:c + 1], pxu[:, 128:192], ALU.mult, ALU.add), reads=[BS32, Bwcs, Bpxu], writes=[BS32])
                        S.op("act", lambda e: e.copy(Sbf[:], S32[:]), reads=[BS32], writes=[BSbf])
                        if own:
                            ysb, Bysb = ysb_[gc % 2]; ycn, Bycn = ycn_[gc % 2]; yjk, Byjk = yjk_[gc % 2]
                            s4, Bs4 = st4_[gc % 2]; rwb, Brwb = rwb_[gc % 2]
                            S.op("dve", lambda e: e.memset(s4[:], 0.0), writes=[Bs4])
                            S.op("act", lambda e: e.activation(ysb[:], pys[:, 0:64], AF.Identity, accum_out=s4[:, 0:1]), reads=[Bpys, Bs4], writes=[Bysb, Bs4])
                            S.op("dve", lambda e: e.tensor_scalar(s4[:, 1:2], s4[:, 0:1], -1.0 / 64, None, ALU.mult), reads=[Bs4], writes=[Bs4])
                            S.op("dve", lambda e: e.tensor_scalar(ycn[:], ysb[:], s4[:, 1:2], None, ALU.add), reads=[Bysb, Bs4], writes=[Bycn])
                            S.op("act", lambda e: e.activation(yjk[:], ycn[:], AF.Square, accum_out=s4[:, 2:3]), reads=[Bycn], writes=[Byjk, Bs4])
                            S.op("act", lambda e: e.activation(s4[:, 3:4], s4[:, 2:3], AF.Ln, bias=64e-5, scale=1.0 / 64), reads=[Bs4], writes=[Bs4])
                            S.op("act", lambda e: e.activation(s4[:, 4:5], s4[:, 3:4], AF.Exp, scale=-0.5), reads=[Bs4], writes=[Bs4])
                            S.op("dve", lambda e: e.scalar_tensor_tensor(ycn[:], ycn[:], s4[:, 4:5], GNW[:], ALU.mult, ALU.mult), reads=[Bycn, Bs4, BGNW], writes=[Bycn])
                            S.op("dve", lambda e: e.tensor_tensor(ycn[:], ycn[:], GNB[:], ALU.add), reads=[Bycn, BGNB], writes=[Bycn])
                            S.op("act", lambda e: e.copy(s4[:, 5:6], pys[:, 128:129]), reads=[Bpys], writes=[Bs4])
                            S.op("dve", lambda e: e.scalar_tensor_tensor(ycn[:], TOK[:, j, 128:192], s4[:, 5:6], ycn[:], ALU.mult, ALU.add), reads=[BTOK, Bs4, Bycn], writes=[Bycn])
                            S.op("dve", lambda e: e.tensor_tensor(rwb[:], ycn[:], pys[:, 64:128], ALU.mult), reads=[Bycn, Bpys], writes=[Brwb])
                            for pb in (0, 64):
                                P = slice(pb, pb + 64)
                                S.op("pe", lambda e: e.matmul(pys[P, 192:256], rwb[P, :], identb[P, pb:pb + 64], start=True, stop=True, tile_position=(pb, pb)), reads=[Brwb, Bidb], writes=[Bpys])
                            S.op("act", lambda e: e.copy(rwTs[:, cs], pys[:, 192:256]), reads=[Bpys], writes=[BrwTs])

                    def run_all(g):
                        r = None
                        for r in g:
                            pass
                        return r

                    cur = run_all(stage12(0))
                    for b4 in range(4):
                        g = stage12(b4 + 1) if b4 < 3 else None
                        if g is not None:
                            next(g)
                        for j in range(4):
                            recur(b4, j, cur[0], cur[1])
                            next(precast, None)
                            if g is not None:
                                next(g)
                                if j == 3:
                                    next(g)
                                    cur = next(g)

                    if own:
                        S.dma(RWT[c0:c0 + 128, (sgi - 2) * 1024:(sgi - 1) * 1024], rwTs[:], reads=[BrwTs], acc=[B_RWT])
            for _ in precast:
                pass


        esD = ExitStack()
        if stop == "C":
            S.drain("sp")
            return nc
        with esD:
            accT, Bacc = SB(esD, "accT", [128, 32, 512], F32)
            actA, BactA = SB(esD, "actA", [128, 32, 512], BF16)
            actB, BactB = SB(esD, "actB", [128, 20, 512], BF16)
            hid, Bhid = actB, BactB
            gts = [SB(esD, "gts%d" % i, [128, 2, 512], BF16) for i in range(2)]
            nmlp, Bnmlp = SB(esD, "nmlp", [128, 32], F32)
            nple, Bnple = SB(esD, "nple", [128, 32], F32)
            S.dma(nmlp[:], nmlp_d[:], writes=[Bnmlp])
            S.dma(nple[:], nple_d[:], writes=[Bnple])
            xrow = [SB(esD, "xrow%d" % i, [128, 1024], F32) for i in range(2)]
            osd = [SB(esD, "osd%d" % i, [128, 12, 129], F32) for i in range(1)]
            numt, Bnum = SB(esD, "numt", [128, 4, 129], F32)
            rden, Brden = SB(esD, "rden", [128, 4], F32)
            attb, Battb = SB(esD, "attb", [128, 4, 128], BF16)
            pld, Bpld = SB(esD, "pld", [128, 256], F32)
            pldb, Bpldb = SB(esD, "pldb", [128, 256], BF16)
            pT, BpT = SB(esD, "pT", [128, 2, 512], BF16)
            tmpA = [SB(esD, "tmpA%d" % i, [128, 512], F32) for i in range(2)]
            sqd, Bsqd = SB(esD, "sqd", [128, 512], BF16)
            rstd, Brstd = SB(esD, "rstd", [128, 512], F32)
            pdm = [PS(esD, "pdm%d" % i, [128, 512]) for i in range(3)]
            pdb = [PS(esD, "pdb%d" % i, [128, 512]) for i in range(2)]
            pdn, Bpdn = PS(esD, "pdn", [128, 512])
            pdt = [PS(esD, "pdt%d" % i, [128, 1024], BF16) for i in range(1)]
            pdf, Bpdf = PS(esD, "pdf", [128, 512])
            wpb = [SB(esD, "wpb%d" % i, [128, 256], BF16) for i in range(2)]
            dctr = {"mm": 0, "b": 0, "ta": 0, "x": 0}

            def nextp():
                dctr["mm"] += 1
                return pdm[dctr["mm"] % 3]

            def norm_to(actdst, Bactdst, gainT, BgainT):
                for blk in range(32):
                    S.op("act", lambda e: e.activation(sqd[:], accT[:, blk, :], AF.Square), reads=[Bacc], writes=[Bsqd])
                    S.op("pe", lambda e: e.matmul(pdn[:], onesb[:], sqd[:], start=(blk == 0), stop=(blk == 31)), reads=[Bones, Bsqd], writes=[Bpdn])
                S.op("act", lambda e: e.activation(rstd[:], pdn[:], AF.Ln, bias=1e-6, scale=1.0 / D), reads=[Bpdn], writes=[Brstd])
                S.op("act", lambda e: e.activation(rstd[:], rstd[:], AF.Exp, scale=-0.5), reads=[Brstd], writes=[Brstd])
                for blk in range(32):
                    S.op("dve", lambda e: e.scalar_tensor_tensor(actdst[:, blk, :], accT[:, blk, :], gainT[:, blk:blk + 1], rstd[:], ALU.mult, ALU.mult), reads=[Bacc, BgainT, Brstd], writes=[Bactdst])

            dspecs = []
            for ti_ in range(4):
                for blk_ in range(32):
                    dspecs.append((w_attn_up, 0, 512, blk_ * 128, 128, None))
                    dspecs.append((w_rwkv_up, 0, 2048, blk_ * 128, 128, None))
                for blk_ in range(32):
                    dspecs.append((w_out, 0, D, blk_ * 128, 128, None))
                for ch_ in range(16):
                    for fb_ in range(8):
                        dspecs.append((w_mlp_in, 0, D, ch_ * 1024 + fb_ * 128, 128, None))
                    for blk_ in range(32):
                        dspecs.append((w_mlp_out, ch_ * 1024, 1024, blk_ * 128, 128, None))
                for blk_ in range(32):
                    dspecs.append((w_ple_gate, 0, D, blk_ * 128, 128, None))
                    dspecs.append((w_ple_proj, 0, 256, blk_ * 128, 128, wpb[blk_ % 2]))
            ring = [wbf[0], wbf[1]] + [(w_[:].bitcast(BF16), bw_) for (w_, bw_) in wst]
            key2off = {}
            for bi_, (w_ap, r0, krows, c0_) in enumerate(tail_blocks):
                key2off[(w_ap.tensor.name, r0, krows, c0_)] = wb_off[bi_]

            class WQ2:
                def __init__(self, specs):
                    self.specs = specs
                    self.pos = 0
                    self.dpos = 0
                    self.tiles = {}
                    self.rr = 0

                def _dma(self):
                    (w_ap, r0, krows, c0_, m, dst) = self.specs[self.dpos]
                    kc = (krows + 127) // 128
                    wti, off = key2off[(w_ap.tensor.name, r0, krows, c0_)]
                    if dst is not None:
                        wb, Bwb = dst
                        view = wb[:, 0:kc * 128]
                    else:
                        wb, Bwb = ring[self.rr % len(ring)]
                        self.rr += 1
                        view = wb[:, 0:kc * 128]
                    S.dma(view, WBs[wti][:, off:off + kc * 128], reads=[B_WB], writes=[Bwb], q="sp")
                    self.tiles[self.dpos] = (wb, Bwb, kc)
                    self.dpos += 1

                def next(self):
                    n = len(self.specs)
                    while self.dpos < min(n, self.pos + 4):
                        self._dma()
                    r = self.tiles.pop(self.pos)
                    self.pos += 1
                    return r
            wqD = WQ2(dspecs)

            for ti in range(4):
                T0 = ti * 512
                for sub in range(4):
                    for cq in range(4):
                        xr, Bxr = xrow[dctr["x"] % 2]
                        dctr["x"] += 1
                        S.dma(xr[:], x_d[OWN + T0 + sub * 128:OWN + T0 + (sub + 1) * 128, cq * 1024:(cq + 1) * 1024], writes=[Bxr], q="sp" if cq % 2 == 0 else "act")
                        for half in range(2):
                            for j in range(4):
                                S.op("pe", lambda e: e.transpose(pdf[:, j * 128:(j + 1) * 128], xr[:, (half * 4 + j) * 128:(half * 4 + j + 1) * 128], identf[:]), reads=[Bxr, Bidf], writes=[Bpdf])
                            blk0 = cq * 8 + half * 4
                            S.op("dve", lambda e: e.tensor_copy(accT[:, blk0:blk0 + 4, sub * 128:(sub + 1) * 128], pdf[:].rearrange("p (k t) -> p k t", k=4)), reads=[Bpdf], writes=[Bacc])
                for sub in range(4):
                    od, Bod = osd[0]
                    tk = T0 + sub * 128
                    S.dma(od[:], OS[tk:tk + 128], reads=[B_OS], writes=[Bod])
                    S.op("dve", lambda e: e.tensor_tensor(numt[:], od[:, 0:4, :], od[:, 4:8, :], ALU.add), reads=[Bod], writes=[Bnum])
                    S.op("dve", lambda e: e.tensor_tensor(numt[:], numt[:], od[:, 8:12, :], ALU.add), reads=[Bod, Bnum], writes=[Bnum])
                    S.op("dve", lambda e: e.reciprocal(rden[:], numt[:, :, 128]), reads=[Bnum], writes=[Brden])
                    S.op("dve", lambda e: e.tensor_tensor(attb[:], numt[:, :, 0:128], rden[:].unsqueeze(2).to_broadcast([128, 4, 128]), ALU.mult), reads=[Bnum, Brden], writes=[Battb])
                    pt, Bpt = pdt[0]
                    for j in range(4):
                        S.op("pe", lambda e: e.transpose(pt[:, j * 128:(j + 1) * 128], attb[:, j, :], identb[:]), reads=[Battb, Bidb], writes=[Bpt])
                    S.op("act", lambda e: e.copy(actB[:, 0:4, sub * 128:(sub + 1) * 128], pt[:, 0:512].rearrange("p (k t) -> p k t", k=4)), reads=[Bpt], writes=[BactB])
                    S.dma(pld[:], p_d[tk:tk + 128, :], writes=[Bpld], q="act")
                    S.op("dve", lambda e: e.tensor_copy(pldb[:], pld[:]), reads=[Bpld], writes=[Bpldb])
                    for j in range(2):
                        S.op("pe", lambda e: e.transpose(pt[:, 512 + j * 128:512 + (j + 1) * 128], pldb[:, j * 128:(j + 1) * 128], identb[:]), reads=[Bpldb, Bidb], writes=[Bpt])
                    S.op("act", lambda e: e.copy(pT[:, :, sub * 128:(sub + 1) * 128], pt[:, 512:768].rearrange("p (k t) -> p k t", k=2)), reads=[Bpt], writes=[BpT])
                S.dma(actB[:, 4:20, :], RWT[:, T0:T0 + 512].rearrange("(k p) t -> p k t", p=128), reads=[B_RWT], writes=[BactB])
                for blk in range(32):
                    gt, Bgt = gts[blk % 2]
                    S.dma(gt[:, 0, :], GT[blk * 128:(blk + 1) * 128, T0:T0 + 512], reads=[B_GT], writes=[Bgt])
                    S.dma(gt[:, 1, :], GT[4096 + blk * 128:4096 + (blk + 1) * 128, T0:T0 + 512], reads=[B_GT], writes=[Bgt], q="act")
                    pa, Bpa = pdb[0]
                    pr, Bpr = pdb[1]
                    wa, Bwa, _ = wqD.next()
                    for kc in range(4):
                        S.op("pe", lambda e: e.matmul(pa[:], wa[:, kc * 128:(kc + 1) * 128], actB[:, kc, :], start=(kc == 0), stop=(kc == 3)), reads=[Bwa, BactB], writes=[Bpa])
                    wr, Bwr, _ = wqD.next()
                    for kc in range(16):
                        S.op("pe", lambda e: e.matmul(pr[:], wr[:, kc * 128:(kc + 1) * 128], actB[:, 4 + kc, :], start=(kc == 0), stop=(kc == 15)), reads=[Bwr, BactB], writes=[Bpr])
                    ta, Bta = tmpA[0]
                    tb, Btb = tmpA[1]
                    S.op("dve", lambda e: e.tensor_tensor(ta[:], pa[:], gt[:, 0, :], ALU.mult), reads=[Bpa, Bgt], writes=[Bta])
                    S.op("dve", lambda e: e.tensor_tensor(tb[:], pr[:], gt[:, 1, :], ALU.mult), reads=[Bpr, Bgt], writes=[Btb])
                    S.op("pool", lambda e: e.tensor_tensor(actA[:, blk, :], ta[:], tb[:], ALU.add), reads=[Bta, Btb], writes=[BactA])
                for blk in range(32):
                    wb, Bwb, _ = wqD.next()
                    pm_, Bpm = nextp()
                    for kc in range(32):
                        S.op("pe", lambda e: e.matmul(pm_[:], wb[:, kc * 128:(kc + 1) * 128], actA[:, kc, :], start=(kc == 0), stop=(kc == 31)), reads=[Bwb, BactA], writes=[Bpm])
                    S.op("dve", lambda e: e.tensor_tensor(accT[:, blk, :], accT[:, blk, :], pm_[:], ALU.add), reads=[Bacc, Bpm], writes=[Bacc])
                norm_to(actA, BactA, nmlp, Bnmlp)
                for ch in range(16):
                    for fb in range(8):
                        wb, Bwb, _ = wqD.next()
                        pm_, Bpm = nextp()
                        for kc in range(32):
                            S.op("pe", lambda e: e.matmul(pm_[:], wb[:, kc * 128:(kc + 1) * 128], actA[:, kc, :], start=(kc == 0), stop=(kc == 31)), reads=[Bwb, BactA], writes=[Bpm])
                        ta, Bta = tmpA[fb % 2]
                        S.op("act", lambda e: e.activation(ta[:], pm_[:], AF.Relu), reads=[Bpm], writes=[Bta])
                        S.op("dve", lambda e: e.tensor_tensor(hid[:, fb, :], ta[:], ta[:], ALU.mult), reads=[Bta], writes=[Bhid])
                    for blk in range(32):
                        wb, Bwb, _ = wqD.next()
                        pm_, Bpm = nextp()
                        for kc in range(8):
                            S.op("pe", lambda e: e.matmul(pm_[:], wb[:, kc * 128:(kc + 1) * 128], hid[:, kc, :], start=(kc == 0), stop=(kc == 7)), reads=[Bwb, Bhid], writes=[Bpm])
                        S.op("dve", lambda e: e.tensor_tensor(accT[:, blk, :], accT[:, blk, :], pm_[:], ALU.add), reads=[Bacc, Bpm], writes=[Bacc])
                norm_to(actA, BactA, nple, Bnple)
                for blk in range(32):
                    wb, Bwb, _ = wqD.next()
                    pm_, Bpm = nextp()
                    pp, Bpp = pdb[blk % 2]
                    for kc in range(32):
                        S.op("pe", lambda e: e.matmul(pm_[:], wb[:, kc * 128:(kc + 1) * 128], actA[:, kc, :], start=(kc == 0), stop=(kc == 31)), reads=[Bwb, BactA], writes=[Bpm])
                    wp, Bwp, _ = wqD.next()
                    for kc in range(2):
                        S.op("pe", lambda e: e.matmul(pp[:], wp[:, kc * 128:(kc + 1) * 128], pT[:, kc, :], start=(kc == 0), stop=(kc == 1)), reads=[Bwp, BpT], writes=[Bpp])
                    ta, Bta = tmpA[blk % 2]
                    S.op("act", lambda e: e.activation(ta[:], pm_[:], AF.Sigmoid), reads=[Bpm], writes=[Bta])
                    S.op("dve", lambda e: e.tensor_tensor(ta[:], ta[:], pp[:], ALU.mult), reads=[Bta, Bpp], writes=[Bta])
                    S.op("pool", lambda e: e.tensor_tensor(accT[:, blk, :], accT[:, blk, :], ta[:], ALU.add), reads=[Bacc, Bta], writes=[Bacc])
                for sub in range(4):
                    for cq in range(4):
                        xr, Bxr = xrow[dctr["x"] % 2]
                        dctr["x"] += 1
                        for half in range(2):
                            for j in range(4):
                                blk = cq * 8 + half * 4 + j
                                S.op("pe", lambda e: e.transpose(pdf[:, j * 128:(j + 1) * 128], accT[:, blk, sub * 128:(sub + 1) * 128], identf[:]), reads=[Bacc, Bidf], writes=[Bpdf])
                            S.op("act", lambda e: e.copy(xr[:, half * 512:(half + 1) * 512], pdf[:]), reads=[Bpdf], writes=[Bxr])
                        S.dma(out_d[T0 + sub * 128:T0 + (sub + 1) * 128, cq * 1024:(cq + 1) * 1024], xr[:], reads=[Bxr], writes=[])
        S.drain("sp")
        print("instr counts", S.cnt)
    return nc


_CACHE = {}


def make_in_maps(inp):
    f32 = np.float32
    x = np.asarray(inp["x"], f32)
    p = np.asarray(inp["p"], f32)[0]
    c = host_consts()
    shift_mix = np.asarray(inp["shift_mix"], f32)[0]
    mixT = np.zeros((128, 52), f32)
    for b in range(52):
        mixT[:ZM[b], b] = shift_mix[ZST[b]:ZST[b] + ZM[b]]
    rwp = np.stack([np.asarray(inp[k], f32).reshape(2048).reshape(16, 128).T
                    for k in ("w0", "a0", "k_k", "k_a", "r_k")], axis=1)
    rb33 = np.concatenate([np.asarray(inp["rel_bias"], f32), np.ones((1, 12), f32)], axis=0)
    qkg = np.stack([np.asarray(inp["q_gain"], f32)[0], np.asarray(inp["k_gain"], f32)[0]], axis=1)
    shared = {
        "w_in": np.asarray(inp["w_in"], f32)[0],
        "w_attn_up": np.asarray(inp["w_attn_up"], f32)[0],
        "w_decay_up": np.asarray(inp["w_decay_up"], f32)[0],
        "w_aaa_up": np.asarray(inp["w_aaa_up"], f32)[0],
        "w_gate_up": np.asarray(inp["w_gate_up"], f32)[0],
        "w_rwkv_up": np.asarray(inp["w_rwkv_up"], f32)[0],
        "w_out": np.asarray(inp["w_out"], f32)[0],
        "w_mlp_in": np.asarray(inp["w_mlp_in"], f32)[0],
        "w_mlp_out": np.asarray(inp["w_mlp_out"], f32)[0],
        "w_ple_gate": np.asarray(inp["w_ple_gate"], f32)[0],
        "w_ple_proj": np.asarray(inp["w_ple_proj"], f32)[0],
        "norm_mix": np.asarray(inp["norm_mix"], f32).reshape(1, D),
        "qkg": np.ascontiguousarray(qkg),
        "rb33": rb33,
        "mixT": mixT,
        "rwp": np.ascontiguousarray(rwp),
        "gn_w": np.asarray(inp["gn_w"], f32).reshape(1, 2048),
        "gn_b": np.asarray(inp["gn_b"], f32).reshape(1, 2048),
        "nmlpT": np.ascontiguousarray(np.asarray(inp["norm_mlp"], f32).reshape(32, 128).T),
        "npleT": np.ascontiguousarray(np.asarray(inp["norm_ple"], f32).reshape(32, 128).T),
        "ident": c["ident"], "masks": c["masks"], "scanmask": c["scanmask"],
        "blkones": c["blkones"], "oh": c["oh"],
    }
    in_maps = []
    for core in range(8):
        b, th = core // 2, core % 2
        if th == 0:
            xl = np.concatenate([np.zeros((OWN, D), f32), x[b, :OWN]], axis=0)
        else:
            xl = x[b]
        m = dict(shared)
        m["x"] = np.ascontiguousarray(xl)
        m["p"] = np.ascontiguousarray(p[b, th * OWN:(th + 1) * OWN])
        m["pm"] = np.full((128, 1), NEG if th == 0 else 0.0, f32)
        in_maps.append(m)
    return in_maps


def kernel(**inp):
    f32 = np.float32
    x = np.asarray(inp["x"], f32)
    in_maps = make_in_maps(inp)
    if "nc" not in _CACHE:
        _CACHE["nc"] = build_program()
    res = run_bass_kernel_spmd(_CACHE["nc"], in_maps, core_ids=list(range(8)))
    out = np.zeros((4, SEQ, D), f32)
    for core in range(8):
        b, th = core // 2, core % 2
        out[b, th * OWN:(th + 1) * OWN] = np.asarray(res.results[core]["out"], f32)
    return out
```

```python
import math
import numpy as np
from contextlib import ExitStack
import concourse.bass as bass
import concourse.mybir as mybir
from concourse.bass_utils import run_bass_kernel_spmd

F32 = mybir.dt.float32
BF16 = mybir.dt.bfloat16
ALU = mybir.AluOpType
AF = mybir.ActivationFunctionType

D = 4096
SEQ = 4096
OWN = 2048
NEG = -30000.0
DILS = (1, 4, 16)
ZST = [128 * i for i in range(48)] + [6144, 6240, 6336, 6464]
ZM = [128] * 48 + [96, 96, 128, 128]
A_END = 4608
R_END = 4608 + 6592


class Buf:
    __slots__ = ("w", "r")

    def __init__(self):
        self.w = {}
        self.r = {}


class Sched:
    ENG = ["pe", "act", "dve", "pool", "sp"]
    NSLOT = 12

    EPOCH = 8000

    def __init__(self, nc, es):
        self.nc = nc
        self.es = es
        self.e = dict(pe=nc.tensor, act=nc.scalar, dve=nc.vector, pool=nc.gpsimd, sp=nc.sync)
        self.sem = {}
        self.slots = {}
        for q in ["sp", "act", "pool"]:
            for i in range(self.NSLOT):
                k = "d%s%d" % (q, i)
                self.slots[k] = 0
        self.slot_rr = {"sp": 0, "act": 0, "pool": 0}
        self.cnt = {k: 0 for k in self.ENG}
        self.seen = {k: {} for k in self.ENG}

    def semof(self, k, v):
        ep = (v - 1) // self.EPOCH
        kk = (k, ep)
        if kk not in self.sem:
            self.sem[kk] = self.es.enter_context(self.nc.semaphore("s_%s_%d" % (k, ep)))
        return self.sem[kk], v - ep * self.EPOCH

    def _wait(self, en, deps):
        need = {}
        for (k, v) in deps:
            if k == "pe" and en == "pe":
                continue
            if v > need.get(k, 0):
                need[k] = v
        sn = self.seen[en]
        for k, v in need.items():
            if sn.get(k, 0) < v:
                sm, rv = self.semof(k, v)
                self.e[en].wait_ge(sm, rv)
                sn[k] = v

    def _deps(self, reads, writes):
        deps = []
        for b in reads:
            deps.extend(b.w.items())
        for b in writes:
            deps.extend(b.w.items())
            deps.extend(b.r.items())
        return deps

    def _mark(self, tok, reads, writes):
        k, v = tok
        for b in reads:
            if b.r.get(k, 0) < v:
                b.r[k] = v
        for b in writes:
            b.w = {k: v}
            b.r = {}

    def op(self, en, fn, reads=(), writes=()):
        self._wait(en, self._deps(reads, writes))
        ins = fn(self.e[en])
        self.cnt[en] += 1
        sm, _ = self.semof(en, self.cnt[en])
        ins.then_inc(sm, 1)
        self._mark((en, self.cnt[en]), reads, writes)

    def dma(self, out, in_, reads=(), writes=(), q="sp", acc=(), **kw):
        i = self.slot_rr[q]
        self.slot_rr[q] = (i + 1) % self.NSLOT
        k = "d%s%d" % (q, i)
        deps = self._deps(reads, writes)
        if self.slots[k] > 0:
            deps.append((k, self.slots[k]))
        self._wait(q, deps)
        ins = self.e[q].dma_start(out=out, in_=in_, **kw)
        self.slots[k] += 16
        sm, _ = self.semof(k, self.slots[k])
        ins.then_inc(sm, 16)
        self._mark((k, self.slots[k]), reads, writes)
        for b in acc:
            b.w[k] = self.slots[k]

    def drain(self, en="sp"):
        for k, val in self.slots.items():
            if val > 0:
                sm, rv = self.semof(k, val)
                self.e[en].wait_ge(sm, rv)


def t5_bucket_np(dist):
    dist = np.asarray(dist, dtype=np.int64)
    max_exact = 16
    d_f = np.maximum(dist, 1).astype(np.float32)
    large = max_exact + (np.log(d_f / np.float32(max_exact)) / np.float32(math.log(2048 / max_exact))
                         * np.float32(32 - max_exact)).astype(np.int32)
    large = np.minimum(large, 31)
    return np.where(dist < max_exact, dist, large)


def host_consts():
    c = {}
    c["ident"] = np.eye(128, dtype=np.float32)
    s = np.arange(64)[:, None]
    t = np.arange(64)[None, :]
    su = (s < t).astype(np.float32)
    iu = (s <= t).astype(np.float32)
    sl = (s > t).astype(np.float32)
    z = np.zeros((64, 64), np.float32)
    mLT = np.block([[su, z], [z, su]])
    mRB = np.block([[z, iu], [iu, z]])
    mAK = np.block([[su, iu], [iu, su]])
    mL = np.block([[sl, z], [z, sl]])
    c["masks"] = np.stack([mLT, mRB, mAK, mL], axis=1).astype(np.float32)
    sm = np.ones((128, 1024), np.float32)
    sm[:, ::64] = 0.0
    c["scanmask"] = sm
    bo = np.zeros((128, 128), np.float32)
    bo[:64, :64] = 1.0
    bo[64:, 64:] = 1.0
    c["blkones"] = bo
    oh = np.zeros((33, 3, 383), np.float32)
    for g, d in enumerate(DILS):
        for u in range(383):
            rel = u - 127
            if 0 <= rel <= 128:
                oh[int(t5_bucket_np(rel * d)), g, u] = 1.0
            else:
                oh[32, g, u] = NEG
    c["oh"] = oh
    return c


def build_program(stop=None, debug=False):
    nc = bass.Bass("TRN2", target_bir_lowering=False)

    def din(name, shape, dt=F32):
        return nc.dram_tensor(name, list(shape), dt, kind="ExternalInput").ap()

    def dscr(name, shape, dt):
        return nc.dram_tensor(name, list(shape), dt, kind="ExternalOutput" if (debug and name in debug) else "Internal").ap()

    x_d = din("x", [SEQ, D])
    p_d = din("p", [OWN, 256])
    pm_d = din("pm", [128, 1])
    w_in = din("w_in", [D, 19392])
    w_attn_up = din("w_attn_up", [512, D])
    w_decay_up = din("w_decay_up", [96, 2048])
    w_aaa_up = din("w_aaa_up", [96, 2048])
    w_gate_up = din("w_gate_up", [256, 2048])
    w_rwkv_up = din("w_rwkv_up", [2048, D])
    w_out = din("w_out", [D, D])
    w_mlp_in = din("w_mlp_in", [D, 16384])
    w_mlp_out = din("w_mlp_out", [16384, D])
    w_ple_gate = din("w_ple_gate", [D, D])
    w_ple_proj = din("w_ple_proj", [256, D])
    norm_mix = din("norm_mix", [1, D])
    qkg_d = din("qkg", [128, 2])
    rb33_d = din("rb33", [33, 12])
    mixT_d = din("mixT", [128, 52])
    rwp_d = din("rwp", [128, 5, 16])
    gnw_d = din("gn_w", [1, 2048])
    gnb_d = din("gn_b", [1, 2048])
    nmlp_d = din("nmlpT", [128, 32])
    nple_d = din("npleT", [128, 32])
    ident_d = din("ident", [128, 128])
    masks_d = din("masks", [128, 4, 128])
    scanmask_d = din("scanmask", [128, 1024])
    blkones_d = din("blkones", [128, 128])
    oh_d = din("oh", [33, 3, 383])
    out_d = nc.dram_tensor("out", [OWN, D], F32, kind="ExternalOutput").ap()

    KT = dscr("KT", [12, 128, SEQ], BF16)
    VT = dscr("VT", [12, 128, SEQ], BF16)
    QT = dscr("QT", [12, 128, OWN], BF16)
    ZT = dscr("ZT", [6656, SEQ], F32)
    GT = dscr("GT", [8192, OWN], BF16)
    OS = dscr("OS", [OWN, 12, 129], F32)
    RWT = dscr("RWT", [2048, OWN], BF16)
    GS = dscr("GS", [12, 130, 383], F32)
    B_KT, B_VT, B_QT, B_ZT, B_GT, B_OS, B_RWT, B_GS = [Buf() for _ in range(8)]
    tail_blocks = []
    for blk_ in range(32):
        tail_blocks.append((w_attn_up, 0, 512, blk_ * 128))
        tail_blocks.append((w_rwkv_up, 0, 2048, blk_ * 128))
    for blk_ in range(32):
        tail_blocks.append((w_out, 0, D, blk_ * 128))
    for ch_ in range(16):
        for fb_ in range(8):
            tail_blocks.append((w_mlp_in, 0, D, ch_ * 1024 + fb_ * 128))
        for blk_ in range(32):
            tail_blocks.append((w_mlp_out, ch_ * 1024, 1024, blk_ * 128))
    for blk_ in range(32):
        tail_blocks.append((w_ple_gate, 0, D, blk_ * 128))
        tail_blocks.append((w_ple_proj, 0, 256, blk_ * 128))
    wb_off = []
    CAP_ = 900000
    sizes_ = [0]
    for (_, _, krows_, _) in tail_blocks:
        n_ = ((krows_ + 127) // 128) * 128
        if sizes_[-1] + n_ > CAP_:
            sizes_.append(0)
        wb_off.append((len(sizes_) - 1, sizes_[-1]))
        sizes_[-1] += n_
    WBs = [dscr("WB%d" % i_, [128, sz_], BF16) for i_, sz_ in enumerate(sizes_)]
    B_WB = Buf()
    B_out = Buf()

    top = ExitStack()
    with top:
        S = Sched(nc, top)

        def SB(es, name, shape, dt):
            return es.enter_context(nc.sbuf_tensor("sb_" + name, list(shape), dt)), Buf()

        def PS(es, name, shape, dt=F32):
            return es.enter_context(nc.psum_tensor("ps_" + name, list(shape), dt)), Buf()

        identf, Bidf = SB(top, "identf", [128, 128], F32)
        identb, Bidb = SB(top, "identb", [128, 128], BF16)
        S.dma(identf[:], ident_d[:], writes=[Bidf])
        S.op("dve", lambda e: e.tensor_copy(identb[:], identf[:]), reads=[Bidf], writes=[Bidb])
        onesb, Bones = SB(top, "onesb", [128, 128], BF16)
        S.op("dve", lambda e: e.memset(onesb[:], 1.0), writes=[Bones])

        wst = [SB(top, "wst%d" % i, [128, 2048], F32) for i in range(4)]
        wbf = [SB(top, "wbf%d" % i, [128, 4096], BF16) for i in range(2)]
        wctr = [0, 0, 0]

        def load_w(w_ap, r0, krows, c0, m, dst=None):
            kc = (krows + 127) // 128
            if dst is not None:
                wb, Bwb = dst
            else:
                wb, Bwb = wbf[wctr[1] % 2]
                wctr[1] += 1
            pieces = [(0, kc)] if kc <= 16 else [(0, 16), (16, kc)]
            first = True
            for (k0, k1) in pieces:
                st, Bst = wst[wctr[0] % 3]
                q = "sp" if wctr[0] % 2 == 0 else "act"
                wctr[0] += 1
                nk = k1 - k0
                src = w_ap[r0 + k0 * 128:r0 + k1 * 128, c0:c0 + m].rearrange("(k p) c -> p k c", p=128)
                S.dma(st[:, 0:nk * m].rearrange("p (k c) -> p k c", k=nk), src, writes=[Bst], q=q)
                ce = "act" if wctr[0] % 2 == 0 else "dve"
                if ce == "act":
                    fn = lambda e: e.copy(wb[:, k0 * m:k1 * m], st[:, 0:nk * m])
                else:
                    fn = lambda e: e.tensor_copy(wb[:, k0 * m:k1 * m], st[:, 0:nk * m])
                if first:
                    S.op(ce, fn, reads=[Bst], writes=[Bwb])
                else:
                    S.op(ce, fn, reads=[Bst, Bwb], writes=[])
                    Bwb.w[ce] = S.cnt[ce]
                first = False
            return wb, Bwb, kc

        class WQ:
            def __init__(self, specs):
                self.specs = specs
                self.pos = 0
                self.dpos = 0
                self.cpos = 0
                self.staged = {}
                self.tiles = {}

            def _dma(self):
                i = self.dpos
                (w_ap, r0, krows, c0, m, dst) = self.specs[i]
                kc = (krows + 127) // 128
                pieces = [(0, kc)] if kc <= 16 else [(0, 16), (16, kc)]
                lst = []
                for (k0, k1) in pieces:
                    st, Bst = wst[wctr[0] % 4]
                    wctr[0] += 1
                    nk = k1 - k0
                    src = w_ap[r0 + k0 * 128:r0 + k1 * 128, c0:c0 + m].rearrange("(k p) c -> p k c", p=128)
                    S.dma(st[:, 0:nk * m].rearrange("p (k c) -> p k c", k=nk), src, writes=[Bst], q="sp")
                    lst.append((st, Bst, k0, k1))
                self.staged[i] = lst
                self.dpos += 1

            def _cast(self):
                i = self.cpos
                (w_ap, r0, krows, c0, m, dst) = self.specs[i]
                kc = (krows + 127) // 128
                if dst is not None:
                    wb, Bwb = dst
                else:
                    wb, Bwb = wbf[wctr[1] % 2]
                    wctr[1] += 1
                first = True
                for (st, Bst, k0, k1) in self.staged.pop(i):
                    nk = k1 - k0
                    wctr[2] += 1
                    ce = "act" if wctr[2] % 2 == 0 else "dve"
                    if ce == "act":
                        fn = lambda e: e.copy(wb[:, k0 * m:k1 * m], st[:, 0:nk * m])
                    else:
                        fn = lambda e: e.tensor_copy(wb[:, k0 * m:k1 * m], st[:, 0:nk * m])
                    if first:
                        S.op(ce, fn, reads=[Bst], writes=[Bwb])
                    else:
                        S.op(ce, fn, reads=[Bst, Bwb], writes=[])
                        Bwb.w[ce] = S.cnt[ce]
                    first = False
                self.tiles[i] = (wb, Bwb, kc)
                self.cpos += 1

            def next(self):
                n = len(self.specs)
                while self.cpos < min(n, self.pos + 2):
                    if self.dpos <= self.cpos:
                        self._dma()
                    self._cast()
                while self.dpos < min(n, self.pos + 3):
                    self._dma()
                r = self.tiles.pop(self.pos)
                self.pos += 1
                return r

        esA = ExitStack()
        with esA:
            gmix, Bgmix = SB(esA, "gmix", [128, D], F32)
            S.dma(gmix[:], norm_mix.partition_broadcast(128), writes=[Bgmix])
            qkg, Bqkg = SB(esA, "qkg", [128, 2], F32)
            S.dma(qkg[:], qkg_d[:], writes=[Bqkg])
            S.op("dve", lambda e: e.tensor_scalar(qkg[:, 0:1], qkg[:, 0:1], 128.0 ** -0.5, None, ALU.mult), reads=[Bqkg], writes=[Bqkg])
            mixT, Bmix = SB(esA, "mixT", [128, 52], F32)
            omix, Bomix = SB(esA, "omix", [128, 52], F32)
            S.dma(mixT[:], mixT_d[:], writes=[Bmix])
            S.op("dve", lambda e: e.tensor_scalar(omix[:], mixT[:], -1.0, 1.0, ALU.mult, ALU.add), reads=[Bmix], writes=[Bomix])
            zlast, Bzl = SB(esA, "zlast", [128, 52], F32)
            S.op("dve", lambda e: e.memset(zlast[:], 0.0), writes=[Bzl])
            hT, BhT = SB(esA, "hT", [128, 32, 1024], BF16)
            xin = [SB(esA, "xin%d" % i, [128, D], F32) for i in range(1)]
            hb, Bhb = SB(esA, "hb", [128, D], BF16)
            ss, Bss = SB(esA, "ss", [128, 4], F32)
            ptr = [PS(esA, "ptr%d" % i, [128, 1024], BF16) for i in range(2)]
            pmm = [PS(esA, "pmm%d" % i, [128, 512]) for i in range(3)]
            pn = [PS(esA, "pn%d" % i, [128, 512]) for i in range(2)]
            sqb, Bsqb = SB(esA, "sqb", [128, 512], BF16)
            rsb, Brsb = SB(esA, "rsb", [128, 512], F32)
            stg16 = [SB(esA, "stg16_%d" % i, [128, 1024], BF16) for i in range(2)]
            zbufs = [SB(esA, "zbuf%d" % i, [128, 1025], F32) for i in range(2)]
            ztmp, Bztmp = SB(esA, "ztmp", [128, 1024], F32)
            zs = [SB(esA, "zs%d" % i, [128, 1024], F32) for i in range(2)]
            ctr = {"ev": 0, "mm": 0, "pn": 0, "s16": 0, "zs": 0}

            def a_blocks(qi):
                blocks = []
                for h in range(12):
                    blocks.append(("k", 1536 + h * 128, 128, h))
                for h in range(12):
                    blocks.append(("v", 3072 + h * 128, 128, h))
                for b in range(52):
                    blocks.append(("z", A_END + ZST[b], ZM[b], b))
                if qi >= 2:
                    for h in range(12):
                        blocks.append(("q", h * 128, 128, h))
                    for b in range(64):
                        blocks.append(("g", R_END + b * 128, 128, b))
                return blocks
            wqA = WQ([(w_in, 0, D, c0_, m_, None) for qi_ in range(4) for (_, c0_, m_, _) in a_blocks(qi_)])

            for qi in range(4):
                own = qi >= 2
                for sub in range(8):
                    xt, Bxt = xin[0]
                    t0 = qi * 1024 + sub * 128
                    S.dma(xt[:], x_d[t0:t0 + 128, :], writes=[Bxt], q="sp" if sub % 2 == 0 else "act")
                    S.op("dve", lambda e: e.memset(ss[:, 0:1], 0.0), writes=[Bss])
                    S.op("act", lambda e: e.activation(hb[:], xt[:], AF.Square, accum_out=ss[:, 0:1]), reads=[Bxt, Bss], writes=[Bhb, Bss])
                    S.op("act", lambda e: e.activation(ss[:, 1:2], ss[:, 0:1], AF.Ln, bias=1e-6, scale=1.0 / D), reads=[Bss], writes=[Bss])
                    S.op("act", lambda e: e.activation(ss[:, 2:3], ss[:, 1:2], AF.Exp, scale=-0.5), reads=[Bss], writes=[Bss])
                    S.op("dve", lambda e: e.scalar_tensor_tensor(hb[:], xt[:], ss[:, 2:3], gmix[:], ALU.mult, ALU.mult), reads=[Bxt, Bss, Bgmix], writes=[Bhb])
                    for g8 in range(4):
                        pt, Bpt = ptr[g8 % 2]
                        for j in range(8):
                            kc = g8 * 8 + j
                            S.op("pe", lambda e: e.transpose(pt[:, j * 128:(j + 1) * 128], hb[:, kc * 128:(kc + 1) * 128], identb[:]), reads=[Bhb, Bidb], writes=[Bpt])
                        en = "act" if g8 % 2 == 0 else "dve"
                        dst = hT[:, g8 * 8:(g8 + 1) * 8, sub * 128:(sub + 1) * 128]
                        srcp = pt[:].rearrange("p (k t) -> p k t", k=8)
                        if en == "act":
                            S.op("act", lambda e: e.copy(dst, srcp), reads=[Bpt], writes=[BhT])
                        else:
                            S.op("dve", lambda e: e.tensor_copy(dst, srcp), reads=[Bpt], writes=[BhT])
                blocks = []
                for h in range(12):
                    blocks.append(("k", 1536 + h * 128, 128, h))
                for h in range(12):
                    blocks.append(("v", 3072 + h * 128, 128, h))
                for b in range(52):
                    blocks.append(("z", A_END + ZST[b], ZM[b], b))
                if own:
                    for h in range(12):
                        blocks.append(("q", h * 128, 128, h))
                    for b in range(64):
                        blocks.append(("g", R_END + b * 128, 128, b))
                blocks = a_blocks(qi)
                for bi, (kind, c0, m, idx) in enumerate(blocks):
                    wb, Bwb, _ = wqA.next()
                    zbuf, Bzbuf = zbufs[bi % 2]
                    if kind in ("k", "q", "v", "g"):
                        st, Bstg = stg16[ctr["s16"] % 2]
                        ctr["s16"] += 1
                    for half in range(2):
                        pm_, Bpm = pmm[ctr["mm"] % 3]
                        ctr["mm"] += 1
                        for kc in range(32):
                            S.op("pe", lambda e: e.matmul(pm_[0:m, :], wb[:, kc * m:(kc + 1) * m], hT[:, kc, half * 512:(half + 1) * 512], start=(kc == 0), stop=(kc == 31)), reads=[Bwb, BhT], writes=[Bpm])
                        if kind in ("k", "q"):
                            d = DILS[idx // 4]
                            S.op("act", lambda e: e.activation(sqb[:], pm_[:], AF.Square), reads=[Bpm], writes=[Bsqb])
                            pn_, Bpn = pn[ctr["pn"] % 2]
                            ctr["pn"] += 1
                            S.op("pe", lambda e: e.matmul(pn_[:], onesb[:], sqb[:], start=True, stop=True), reads=[Bones, Bsqb], writes=[Bpn])
                            S.op("act", lambda e: e.activation(rsb[:], pn_[:], AF.Ln, bias=1e-6, scale=1.0 / 128), reads=[Bpn], writes=[Brsb])
                            S.op("act", lambda e: e.activation(rsb[:], rsb[:], AF.Exp, scale=-0.5), reads=[Brsb], writes=[Brsb])
                            gcol = qkg[:, 0:1] if kind == "q" else qkg[:, 1:2]
                            n_i = 512 // d
                            dst = st[:].rearrange("p (r i) -> p r i", r=d)[:, :, half * n_i:(half + 1) * n_i]
                            S.op("dve", lambda e: e.scalar_tensor_tensor(dst, pm_[:].rearrange("p (i r) -> p r i", r=d), gcol, rsb[:].rearrange("p (i r) -> p r i", r=d), ALU.mult, ALU.mult), reads=[Bpm, Bqkg, Brsb], writes=[Bstg])
                        elif kind == "v":
                            d = DILS[idx // 4]
                            n_i = 512 // d
                            dst = st[:].rearrange("p (r i) -> p r i", r=d)[:, :, half * n_i:(half + 1) * n_i]
                            S.op("act", lambda e: e.copy(dst, pm_[:].rearrange("p (i r) -> p r i", r=d)), reads=[Bpm], writes=[Bstg])
                        elif kind == "g":
                            S.op("act", lambda e: e.activation(st[:, half * 512:(half + 1) * 512], pm_[:], AF.Sigmoid), reads=[Bpm], writes=[Bstg])
                        else:
                            S.op("act", lambda e: e.copy(zbuf[0:m, 1 + half * 512:1 + (half + 1) * 512], pm_[0:m, :]), reads=[Bpm], writes=[Bzbuf])
                    if kind in ("k", "v", "q"):
                        d = DILS[idx // 4]
                        if kind == "q":
                            dd = QT[idx].rearrange("e (r i) -> e r i", r=d)[:, :, (qi - 2) * (1024 // d):(qi - 1) * (1024 // d)]
                            Bd = B_QT
                        else:
                            base = KT if kind == "k" else VT
                            dd = base[idx].rearrange("e (r i) -> e r i", r=d)[:, :, qi * (1024 // d):(qi + 1) * (1024 // d)]
                            Bd = B_KT if kind == "k" else B_VT
                        S.dma(dd, st[:].rearrange("p (r i) -> p r i", r=d), reads=[Bstg], acc=[Bd])
                    elif kind == "g":
                        S.dma(GT[idx * 128:(idx + 1) * 128, (qi - 2) * 1024:(qi - 1) * 1024], st[:], reads=[Bstg], acc=[B_GT])
                    else:
                        b = idx
                        S.op("dve", lambda e: e.tensor_copy(zbuf[0:m, 0:1], zlast[0:m, b:b + 1]), reads=[Bzl], writes=[Bzbuf])
                        S.op("dve", lambda e: e.tensor_scalar(ztmp[0:m, :], zbuf[0:m, 0:1024], mixT[0:m, b:b + 1], None, ALU.mult), reads=[Bzbuf, Bmix], writes=[Bztmp])
                        zt, Bzt = zs[ctr["zs"] % 2]
                        ctr["zs"] += 1
                        S.op("dve", lambda e: e.scalar_tensor_tensor(zt[0:m, :], zbuf[0:m, 1:1025], omix[0:m, b:b + 1], ztmp[0:m, :], ALU.mult, ALU.add), reads=[Bzbuf, Bomix, Bztmp], writes=[Bzt])
                        S.op("dve", lambda e: e.tensor_copy(zlast[0:m, b:b + 1], zbuf[0:m, 1024:1025]), reads=[Bzbuf], writes=[Bzl])
                        S.dma(ZT[ZST[b]:ZST[b] + m, qi * 1024:(qi + 1) * 1024], zt[0:m, :], reads=[Bzt], acc=[B_ZT])


        esB = ExitStack()
        if stop == "A":
            S.drain("sp")
            return nc
        with esB:
            rb33, Brb = SB(esB, "rb33", [33, 12], F32)
            oh, Boh = SB(esB, "oh", [33, 3, 383], F32)
            S.dma(rb33[:], rb33_d[:], writes=[Brb])
            S.dma(oh[:], oh_d[:], writes=[Boh])
            pmc, Bpmc = SB(esB, "pmc", [128, 1], F32)
            S.dma(pmc[:], pm_d[:], writes=[Bpmc])
            gsb, Bgsb = SB(esB, "gsb", [4, 383], F32)
            pg, Bpg = PS(esB, "pg", [4, 383])
            for g in range(3):
                S.op("pe", lambda e: e.matmul(pg[:], rb33[:, g * 4:(g + 1) * 4], oh[:, g, :], start=True, stop=True), reads=[Brb, Boh], writes=[Bpg])
                S.op("dve", lambda e: e.tensor_copy(gsb[:], pg[:]), reads=[Bpg], writes=[Bgsb])
                srcb = gsb[:].unsqueeze(1).to_broadcast([4, 130, 383])
                S.dma(GS[g * 4:(g + 1) * 4, :, :], srcb, reads=[Bgsb], acc=[B_GS])
            biasT, Bbias = SB(esB, "biasT", [128, 12, 2, 128], F32)
            for h in range(12):
                for tl, cc in ((0, 255), (1, 127)):
                    src = bass.AP(GS.tensor, GS[h].offset + cc, [[382, 128], [1, 128]])
                    S.dma(biasT[:, h, tl, :], src, reads=[B_GS], writes=[Bbias])
            Ksb = [SB(esB, "Ksb%d" % i, [128, SEQ], BF16) for i in range(2)]
            Vsb = [SB(esB, "Vsb%d" % i, [128, SEQ], BF16) for i in range(2)]
            Qsb = [SB(esB, "Qsb%d" % i, [128, OWN], BF16) for i in range(2)]
            Vtok = [SB(esB, "Vtok%d" % i, [128, 32, 130], BF16) for i in range(2)]
            for i in range(2):
                S.op("dve", lambda e: e.memset(Vtok[i][0][:], 1.0), writes=[Vtok[i][1]])
            pvt = [PS(esB, "pvt%d" % i, [128, 128], BF16) for i in range(2)]
            pss = [PS(esB, "pss%d" % i, [128, 2, 128]) for i in range(2)]
            pso = [PS(esB, "pso%d" % i, [128, 129]) for i in range(2)]
            ssb = [SB(esB, "ssb%d" % i, [128, 2, 128], F32) for i in range(2)]
            esb = [SB(esB, "esb%d" % i, [128, 2, 128], BF16) for i in range(2)]
            osb = [SB(esB, "osb%d" % i, [128, 129], F32) for i in range(3)]
            it = 0
            for h in range(12):
                d = DILS[h // 4]
                Lc = SEQ // d
                nbc = Lc // 128
                nb0 = nbc // 2
                K_, BK = Ksb[h % 2]
                V_, BV = Vsb[h % 2]
                Q_, BQ = Qsb[h % 2]
                Vt, BVt = Vtok[h % 2]
                S.dma(K_[:], KT[h], reads=[B_KT], writes=[BK])
                S.dma(V_[:], VT[h], reads=[B_VT], writes=[BV], q="act")
                S.dma(Q_[:], QT[h], reads=[B_QT], writes=[BQ])
                nkb = nb0 + 1
                for r in range(d):
                    for kb in range(nb0 - 1, nbc):
                        vi = r * nkb + (kb - (nb0 - 1))
                        pv, Bpv = pvt[vi % 2]
                        S.op("pe", lambda e: e.transpose(pv[:], V_[:, r * Lc + kb * 128:r * Lc + (kb + 1) * 128], identb[:]), reads=[BV, Bidb], writes=[Bpv])
                        if vi % 2 == 0:
                            S.op("act", lambda e: e.copy(Vt[:, vi, 0:128], pv[:]), reads=[Bpv], writes=[BVt])
                        else:
                            S.op("dve", lambda e: e.tensor_copy(Vt[:, vi, 0:128], pv[:]), reads=[Bpv], writes=[BVt])
                for r in range(d):
                    for n in range(nb0, nbc):
                        ps_, Bps = pss[it % 2]
                        po, Bpo = pso[it % 2]
                        s_, Bs = ssb[it % 2]
                        e_, Be = esb[it % 2]
                        o_, Bo = osb[it % 3]
                        it += 1
                        qt = Q_[:, r * (Lc // 2) + (n - nb0) * 128:r * (Lc // 2) + (n - nb0 + 1) * 128]
                        for tl in range(2):
                            kb = n - 1 + tl
                            S.op("pe", lambda e: e.matmul(ps_[:, tl, :], K_[:, r * Lc + kb * 128:r * Lc + (kb + 1) * 128], qt, start=True, stop=True), reads=[BK, BQ], writes=[Bps])
                        S.op("dve", lambda e: e.tensor_tensor(s_[:], ps_[:], biasT[:, h], ALU.add), reads=[Bps, Bbias], writes=[Bs])
                        if n == nb0:
                            S.op("act", lambda e: e.activation(e_[:, 0, :], s_[:, 0, :], AF.Exp, bias=pmc[:, 0:1]), reads=[Bs, Bpmc], writes=[Be])
                            S.op("act", lambda e: e.activation(e_[:, 1, :], s_[:, 1, :], AF.Exp), reads=[Bs], writes=[Be])
                        else:
                            S.op("act", lambda e: e.activation(e_[:], s_[:], AF.Exp), reads=[Bs], writes=[Be])
                        for tl in range(2):
                            vi = r * nkb + (n - 1 + tl - (nb0 - 1))
                            S.op("pe", lambda e: e.matmul(po[:], e_[:, tl, :], Vt[:, vi, 0:129], start=(tl == 0), stop=(tl == 1)), reads=[Be, BVt], writes=[Bpo])
                        S.op("dve", lambda e: e.tensor_copy(o_[:], po[:]), reads=[Bpo], writes=[Bo])
                        tok0 = (n * 128) * d + r - OWN
                        dst = bass.AP(OS.tensor, (tok0 * 12 + h) * 129, [[d * 12 * 129, 128], [1, 129]])
                        S.dma(dst, o_[:], reads=[Bo], acc=[B_OS])

        esC = ExitStack()
        if stop == "B":
            S.drain("sp")
            return nc
        with esC:
            masks, Bmk = SB(esC, "masks", [128, 4, 128], F32)
            S.dma(masks[:], masks_d[:], writes=[Bmk])
            scm, Bscm = SB(esC, "scm", [128, 1024], F32)
            S.dma(scm[:], scanmask_d[:], writes=[Bscm])
            blk1, Bblk = SB(esC, "blk1", [128, 128], F32)
            S.dma(blk1[:], blkones_d[:], writes=[Bblk])
            blk1b, Bblkb = SB(esC, "blk1b", [128, 128], BF16)
            S.op("dve", lambda e: e.tensor_copy(blk1b[:], blk1[:]), reads=[Bblk], writes=[Bblkb])
            rwp, Brwp = SB(esC, "rwp", [128, 5, 16], F32)
            S.dma(rwp[:], rwp_d[:], writes=[Brwp])
            nrw, Bnrw = SB(esC, "nrw", [128, 2, 16], F32)
            S.op("dve", lambda e: e.tensor_scalar(nrw[:, 0, :], rwp[:, 0, :], -1.0, None, ALU.mult), reads=[Brwp], writes=[Bnrw])
            S.op("dve", lambda e: e.tensor_scalar(nrw[:, 1, :], rwp[:, 3, :], -1.0, 1.0, ALU.mult, ALU.add), reads=[Brwp], writes=[Bnrw])
            txw, Btxw = SB(esC, "txw", [96, SEQ], BF16)
            xab, Bxab = SB(esC, "xab", [96, SEQ], BF16)
            sxg, Bsxg = SB(esC, "sxg", [128, 2, SEQ], BF16)
            f32n = ["r", "k", "v", "ew", "cle", "t0", "t1", "t2", "asig", "kkn", "k2", "bb"]
            F = {n: SB(esC, "f_" + n, [128, 1024], F32) for n in f32n}
            ltmp = [F["t0"], F["t1"]]
            li = 0
            for sgi in range(4):
                for (row0, m, kind) in ((6144, 96, "w"), (6240, 96, "a"), (6336, 128, "g0"), (6464, 128, "g1")):
                    lt, Blt = ltmp[li % 2]
                    li += 1
                    S.dma(lt[0:m, :], ZT[row0:row0 + m, sgi * 1024:(sgi + 1) * 1024], reads=[B_ZT], writes=[Blt], q="sp" if li % 2 else "act")
                    sl = slice(sgi * 1024, (sgi + 1) * 1024)
                    if kind == "w":
                        S.op("act", lambda e: e.activation(txw[:, sl], lt[0:96, :], AF.Tanh), reads=[Blt], writes=[Btxw])
                    elif kind == "a":
                        S.op("dve", lambda e: e.tensor_copy(xab[:, sl], lt[0:96, :]), reads=[Blt], writes=[Bxab])
                    else:
                        gi = 0 if kind == "g0" else 1
                        S.op("act", lambda e: e.activation(sxg[:, gi, sl], lt[:, :], AF.Sigmoid), reads=[Blt], writes=[Bsxg])
            wdec, Bwdec = SB(esC, "wdec", [96, 2048], BF16)
            waaa, Bwaaa = SB(esC, "waaa", [96, 2048], BF16)
            wgat, Bwgat = SB(esC, "wgat", [128, 2, 2048], BF16)
            for (dst, Bd, src, rows) in ((wdec, Bwdec, w_decay_up, 96), (waaa, Bwaaa, w_aaa_up, 96)):
                for cch in range(2):
                    st, Bst = wst[cch]
                    S.dma(st[0:rows, 0:1024], src[:, cch * 1024:(cch + 1) * 1024], writes=[Bst])
                    S.op("pool", lambda e: e.tensor_copy(dst[:, cch * 1024:(cch + 1) * 1024], st[0:rows, 0:1024]), reads=[Bst], writes=[Bd])
            for kc in range(2):
                for cch in range(2):
                    st, Bst = wst[cch]
                    S.dma(st[:, 0:1024], w_gate_up[kc * 128:(kc + 1) * 128, cch * 1024:(cch + 1) * 1024], writes=[Bst])
                    S.op("pool", lambda e: e.tensor_copy(wgat[:, kc, cch * 1024:(cch + 1) * 1024], st[:, 0:1024]), reads=[Bst], writes=[Bwgat])

            b16n = ["bt", "kt", "btc", "ktc", "vb", "rk"]
            Bt = {n: SB(esC, "b_" + n, [128, 1024], BF16) for n in b16n}
            ARt, BARt = SB(esC, "ARt", [128, 16, 128], BF16)
            wcs, Bwcs = SB(esC, "wcs", [128, 16], F32)
            GNW, BGNW = SB(esC, "GNW", [128, 64], F32)
            GNB, BGNB = SB(esC, "GNB", [128, 64], F32)
            S32, BS32 = SB(esC, "S32", [128, 64], F32)
            Sbf_ = [SB(esC, "Sbf%d" % i, [128, 64], BF16) for i in range(2)]
            rwTs, BrwTs = SB(esC, "rwTs", [128, 1024], BF16)

            def rot(name, shape, dt, n):
                return [SB(esC, "%s%d" % (name, i), shape, dt) for i in range(n)]
            ARB_ = rot("ARB", [128, 4, 128], BF16, 2)
            AK_ = rot("AK", [128, 4, 128], BF16, 2)
            N_ = rot("N", [128, 4, 128], BF16, 4)
            NT_ = rot("NT", [128, 4, 128], BF16, 4)
            TT_ = rot("TT", [128, 4, 128], BF16, 8)
            TOK_ = rot("TOK", [128, 4, 192], BF16, 2)
            Xb_ = rot("Xb", [128, 64], BF16, 2)
            Ub_ = rot("Ub", [128, 64], BF16, 2)
            ysb_ = rot("ysb", [128, 64], F32, 2)
            ycn_ = rot("ycn", [128, 64], F32, 2)
            yjk_ = rot("yjk", [128, 64], F32, 2)
            st4_ = rot("st4", [128, 8], F32, 2)
            rwb_ = rot("rwb", [128, 64], BF16, 2)
            ps1, Bps1 = PS(esC, "ps1", [128, 512])
            ps2, Bps2 = PS(esC, "ps2", [128, 512])
            ps3, Bps3 = PS(esC, "ps3", [128, 512])
            pN, BpN = PS(esC, "pN", [128, 512])
            pNT, BpNT = PS(esC, "pNT", [128, 512])
            pT, BpT = PS(esC, "pT", [128, 512])
            pxu, Bpxu = PS(esC, "pxu", [128, 512])
            pys, Bpys = PS(esC, "pys", [128, 512])
            plo, Bplo = pT, BpT
            zb, Bzb = SB(esC, "zb", [128, 512], BF16)
            S.op("dve", lambda e: e.memset(zb[:], 0.0), writes=[Bzb])
            S.op("pe", lambda e: e.matmul(ps3[:], zb[:, 0:128], zb[:], start=True, stop=True), reads=[Bzb], writes=[Bps3])

            def Fv(n):
                return F[n][0], F[n][1]

            def precast_gen():
                pi_ = 0
                for bi_, (w_ap, r0, krows, c0_) in enumerate(tail_blocks):
                    kc = (krows + 127) // 128
                    pieces = [(0, kc)] if kc <= 16 else [(0, 16), (16, kc)]
                    for (k0, k1) in pieces:
                        nk = k1 - k0
                        st, Bst = wst[pi_ % 4]
                        ob, Bob = wbf[pi_ % 2]
                        pi_ += 1
                        src = w_ap[r0 + k0 * 128:r0 + k1 * 128, c0_:c0_ + 128].rearrange("(k p) c -> p k c", p=128)
                        S.dma(st[:, 0:nk * 128].rearrange("p (k c) -> p k c", k=nk), src, writes=[Bst], q="sp")
                        S.op("pool", lambda e: e.tensor_copy(ob[:, 0:nk * 128], st[:, 0:nk * 128]), reads=[Bst], writes=[Bob])
                        S.dma(WBs[wb_off[bi_][0]][:, wb_off[bi_][1] + k0 * 128:wb_off[bi_][1] + k1 * 128], ob[:, 0:nk * 128], reads=[Bob], acc=[B_WB], q="pool")
                        yield None
            precast = precast_gen()

            for hp in range(16):
                S.dma(GNW[0:64, :], gnw_d[0:1, hp * 128:hp * 128 + 64].partition_broadcast(64), writes=[BGNW])
                S.dma(GNW[64:128, :], gnw_d[0:1, hp * 128 + 64:hp * 128 + 128].partition_broadcast(64), writes=[BGNW])
                S.dma(GNB[0:64, :], gnb_d[0:1, hp * 128:hp * 128 + 64].partition_broadcast(64), writes=[BGNB])
                S.dma(GNB[64:128, :], gnb_d[0:1, hp * 128 + 64:hp * 128 + 128].partition_broadcast(64), writes=[BGNB])
                S.op("dve", lambda e: e.memset(S32[:], 0.0), writes=[BS32])
                S.op("dve", lambda e: e.memset(Sbf_[0][0][:], 0.0), writes=[Sbf_[0][1]])
                S.op("dve", lambda e: e.memset(Sbf_[1][0][:], 0.0), writes=[Sbf_[1][1]])
                c0 = hp * 128
                for sgi in range(4):
                    own = sgi >= 2
                    tsl = slice(sgi * 1024, (sgi + 1) * 1024)
                    r_, Br = Fv("r"); k_, Bk = Fv("k"); v_, Bv = Fv("v")
                    S.dma(r_[:], ZT[c0:c0 + 128, tsl], reads=[B_ZT], writes=[Br])
                    S.dma(k_[:], ZT[2048 + c0:2048 + c0 + 128, tsl], reads=[B_ZT], writes=[Bk], q="act")
                    S.dma(v_[:], ZT[4096 + c0:4096 + c0 + 128, tsl], reads=[B_ZT], writes=[Bv])
                    ew, Bew = Fv("ew"); cle, Bcle = Fv("cle"); t0_, Bt0 = Fv("t0"); t1_, Bt1 = Fv("t1"); t2_, Bt2 = Fv("t2")
                    asig, Basig = Fv("asig"); kkn, Bkkn = Fv("kkn"); k2, Bk2 = Fv("k2"); bb, Bbb = Fv("bb")
                    for hf in range(2):
                        hs = slice(hf * 512, (hf + 1) * 512)
                        gs = slice(sgi * 1024 + hf * 512, sgi * 1024 + (hf + 1) * 512)
                        S.op("pe", lambda e: e.matmul(plo[:], wdec[:, c0:c0 + 128], txw[:, gs], start=True, stop=True), reads=[Bwdec, Btxw], writes=[Bplo])
                        S.op("act", lambda e: e.activation(t0_[:, hs], plo[:], AF.Exp, bias=nrw[:, 0, hp:hp + 1], scale=-1.0), reads=[Bplo, Bnrw], writes=[Bt0])
                        S.op("pe", lambda e: e.matmul(plo[:], waaa[:, c0:c0 + 128], xab[:, gs], start=True, stop=True), reads=[Bwaaa, Bxab], writes=[Bplo])
                        S.op("act", lambda e: e.activation(asig[:, hs], plo[:], AF.Sigmoid, bias=rwp[:, 1, hp:hp + 1]), reads=[Bplo, Brwp], writes=[Basig])
                    S.op("act", lambda e: e.activation(t0_[:], t0_[:], AF.Ln, bias=1.0), reads=[Bt0], writes=[Bt0])
                    S.op("act", lambda e: e.activation(ew[:], t0_[:], AF.Exp, bias=-0.5, scale=-1.0), reads=[Bt0], writes=[Bew])
                    S.op("dve", lambda e: e.tensor_tensor_scan(cle[:], scm[:], ew[:], 0.0, ALU.mult, ALU.add), reads=[Bscm, Bew], writes=[Bcle])
                    S.op("pool", lambda e: e.tensor_scalar(kkn[:], k_[:], rwp[:, 2, hp:hp + 1], None, ALU.mult), reads=[Bk, Brwp], writes=[Bkkn])
                    S.op("pool", lambda e: e.tensor_tensor(t1_[:], kkn[:], kkn[:], ALU.mult), reads=[Bkkn], writes=[Bt1])
                    for hf in range(2):
                        hs = slice(hf * 512, (hf + 1) * 512)
                        S.op("pe", lambda e: e.matmul(plo[:], blk1[:], t1_[:, hs], start=True, stop=True), reads=[Bblk, Bt1], writes=[Bplo])
                        S.op("act", lambda e: e.activation(t2_[:, hs], plo[:], AF.Ln, bias=1e-12), reads=[Bplo], writes=[Bt2])
                    S.op("act", lambda e: e.activation(t2_[:], t2_[:], AF.Exp, scale=-0.5), reads=[Bt2], writes=[Bt2])
                    S.op("dve", lambda e: e.tensor_tensor(kkn[:], kkn[:], t2_[:], ALU.mult), reads=[Bkkn, Bt2], writes=[Bkkn])
                    S.op("pool", lambda e: e.tensor_scalar(t1_[:], asig[:], rwp[:, 3, hp:hp + 1], nrw[:, 1, hp:hp + 1], ALU.mult, ALU.add), reads=[Basig, Brwp, Bnrw], writes=[Bt1])
                    S.op("dve", lambda e: e.tensor_tensor(k2[:], k_[:], t1_[:], ALU.mult), reads=[Bk, Bt1], writes=[Bk2])
                    S.op("pool", lambda e: e.tensor_tensor(bb[:], kkn[:], asig[:], ALU.mult), reads=[Bkkn, Basig], writes=[Bbb])
                    S.op("act", lambda e: e.activation(t0_[:], cle[:], AF.Exp), reads=[Bcle], writes=[Bt0])
                    S.op("dve", lambda e: e.tensor_tensor(Bt["bt"][0][:], bb[:], t0_[:], ALU.mult), reads=[Bbb, Bt0], writes=[Bt["bt"][1]])
                    S.op("pool", lambda e: e.tensor_tensor(Bt["kt"][0][:], k2[:], t0_[:], ALU.mult), reads=[Bk2, Bt0], writes=[Bt["kt"][1]])
                    S.op("act", lambda e: e.activation(t1_[:], cle[:], AF.Exp, scale=-1.0), reads=[Bcle], writes=[Bt1])
                    S.op("dve", lambda e: e.tensor_tensor(t2_[:], ew[:], cle[:], ALU.subtract), reads=[Bew, Bcle], writes=[Bt2])
                    S.op("act", lambda e: e.activation(t2_[:], t2_[:], AF.Exp), reads=[Bt2], writes=[Bt2])
                    for pb in (0, 64):
                        ca, cr = pb, 64 - pb
                        ps_ = slice(pb, pb + 64)
                        S.op("dve", lambda e: e.tensor_tensor(ARt[ps_, :, cr:cr + 64], r_[ps_, :].rearrange("p (c t) -> p c t", t=64), t1_[ps_, :].rearrange("p (c t) -> p c t", t=64), ALU.mult), reads=[Br, Bt1], writes=[BARt])
                        S.op("dve", lambda e: e.scalar_tensor_tensor(ARt[ps_, :, ca:ca + 64], kkn[ps_, :].rearrange("p (c t) -> p c t", t=64), -1.0, t2_[ps_, :].rearrange("p (c t) -> p c t", t=64), ALU.mult, ALU.mult), reads=[Bkkn, Bt2], writes=[BARt])
                    cle3 = cle[:].rearrange("p (c t) -> p c t", t=64)
                    S.op("dve", lambda e: e.tensor_tensor(t0_[:].rearrange("p (c t) -> p c t", t=64), cle3, cle3[:, :, 63:64].to_broadcast([128, 16, 64]), ALU.subtract), reads=[Bcle], writes=[Bt0])
                    S.op("act", lambda e: e.activation(t0_[:], t0_[:], AF.Exp), reads=[Bt0], writes=[Bt0])
                    S.op("act", lambda e: e.activation(wcs[:], cle3[:, :, 63], AF.Exp, scale=-1.0), reads=[Bcle], writes=[Bwcs])
                    S.op("dve", lambda e: e.tensor_tensor(Bt["btc"][0][:], bb[:], t0_[:], ALU.mult), reads=[Bbb, Bt0], writes=[Bt["btc"][1]])
                    S.op("pool", lambda e: e.tensor_tensor(Bt["ktc"][0][:], k2[:], t0_[:], ALU.mult), reads=[Bk2, Bt0], writes=[Bt["ktc"][1]])
                    S.op("act", lambda e: e.copy(Bt["vb"][0][:], v_[:]), reads=[Bv], writes=[Bt["vb"][1]])
                    if own:
                        S.op("pool", lambda e: e.tensor_tensor(t1_[:], r_[:], k2[:], ALU.mult), reads=[Br, Bk2, Bt1], writes=[Bt1])
                        S.op("pool", lambda e: e.tensor_scalar(Bt["rk"][0][:], t1_[:], rwp[:, 4, hp:hp + 1], None, ALU.mult), reads=[Bt1, Brwp], writes=[Bt["rk"][1]])
                    bt, Bbt = Bt["bt"]; kt, Bkt = Bt["kt"]; btc, Bbtc = Bt["btc"]; ktc, Bktc = Bt["ktc"]; vb, Bvb = Bt["vb"]; rk, Brk = Bt["rk"]


                    def stage12(b4):
                        par = b4 % 2
                        ARB, BARB = ARB_[par]; AK, BAK = AK_[par]; TOK, BTOK = TOK_[par]
                        for j in range(4):
                            c = b4 * 4 + j
                            cs = slice(c * 64, (c + 1) * 64)
                            pk, Bpk = (pN, BpN) if j < 2 else (pNT, BpNT)
                            ko = (j % 2) * 192
                            for pb in (0, 64):
                                ca = pb
                                P = slice(pb, pb + 64)
                                tp = (pb, pb)
                                S.op("pe", lambda e: e.matmul(ps1[P, j * 128:(j + 1) * 128], bt[P, cs], ARt[P, c, :], start=True, stop=True, tile_position=tp), reads=[Bbt, BARt], writes=[Bps1])
                                S.op("pe", lambda e: e.matmul(ps2[P, j * 128:(j + 1) * 128], kt[P, cs], ARt[P, c, :], start=True, stop=True, tile_position=tp), reads=[Bkt, BARt], writes=[Bps2])
                                S.op("pe", lambda e: e.matmul(ps3[P, j * 128 + ca:j * 128 + ca + 64], ARt[P, c, ca:ca + 64], bt[P, cs], start=True, stop=True, tile_position=tp), reads=[Bbt, BARt], writes=[Bps3])
                                S.op("pe", lambda e: e.matmul(pk[P, ko:ko + 64], btc[P, cs], identb[P, pb:pb + 64], start=True, stop=True, tile_position=tp), reads=[Bbtc, Bidb], writes=[Bpk])
                                S.op("pe", lambda e: e.matmul(pk[P, ko + 64:ko + 128], ktc[P, cs], identb[P, pb:pb + 64], start=True, stop=True, tile_position=tp), reads=[Bktc, Bidb], writes=[Bpk])
                                S.op("pe", lambda e: e.matmul(pk[P, ko + 128:ko + 192], vb[P, cs], identb[P, pb:pb + 64], start=True, stop=True, tile_position=tp), reads=[Bvb, Bidb], writes=[Bpk])
                        N0, BN0 = N_[0]; NT0, BNT0 = NT_[0]; TT0, BTT0 = TT_[par * 4]
                        v4 = lambda t: t[:].rearrange("p (j c) -> p j c", j=4)
                        mb = lambda i: masks[:, i, :].unsqueeze(1).to_broadcast([128, 4, 128])
                        S.op("dve", lambda e: e.tensor_tensor(NT0[:], v4(ps1), mb(0), ALU.mult), reads=[Bps1, Bmk], writes=[BNT0])
                        S.op("dve", lambda e: e.tensor_tensor(N0[:], v4(ps3), mb(3), ALU.mult), reads=[Bps3, Bmk], writes=[BN0])
                        S.op("dve", lambda e: e.tensor_tensor(TT0[:], NT0[:], identb[:].unsqueeze(1).to_broadcast([128, 4, 128]), ALU.add), reads=[BNT0, Bidb], writes=[BTT0])
                        S.op("dve", lambda e: e.tensor_tensor(AK[:], v4(ps2), mb(2), ALU.mult), reads=[Bps2, Bmk], writes=[BAK])
                        if own:
                            S.op("dve", lambda e: e.tensor_tensor(ARB[:], v4(ps1), mb(1), ALU.mult), reads=[Bps1, Bmk], writes=[BARB])
                        S.op("act", lambda e: e.copy(TOK[:, 0:2, :], pN[:, 0:384].rearrange("p (j c) -> p j c", j=2)), reads=[BpN], writes=[BTOK])
                        S.op("act", lambda e: e.copy(TOK[:, 2:4, :], pNT[:, 0:384].rearrange("p (j c) -> p j c", j=2)), reads=[BpNT], writes=[BTOK])
                        yield None
                        Nc, BNc = N0, BN0
                        NTc, BNTc = NT0, BNT0
                        TTc, BTTc = TT0, BTT0
                        for kq in range(1, 6):
                            Nn, BNn = N_[1 + (kq % 3)]
                            NTn, BNTn = NT_[1 + (kq % 3)]
                            TTn, BTTn = TT_[par * 4 + 1 + (kq % 3)]
                            for j in range(4):
                                S.op("pe", lambda e: e.matmul(pN[:, j * 128:(j + 1) * 128], NTc[:, j, :], Nc[:, j, :], start=True, stop=True), reads=[BNTc, BNc], writes=[BpN])
                            if kq < 5:
                                for j in range(4):
                                    S.op("pe", lambda e: e.matmul(pNT[:, j * 128:(j + 1) * 128], Nc[:, j, :], NTc[:, j, :], start=True, stop=True), reads=[BNTc, BNc], writes=[BpNT])
                            S.op("dve", lambda e: e.tensor_copy(Nn[:], v4(pN)), reads=[BpN], writes=[BNn])
                            if kq < 5:
                                S.op("act", lambda e: e.copy(NTn[:], v4(pNT)), reads=[BpNT], writes=[BNTn])
                            for j in range(4):
                                S.op("pe", lambda e: e.matmul(pT[:, j * 128:(j + 1) * 128], Nn[:, j, :], TTc[:, j, :], start=True, stop=True), reads=[BNn, BTTc], writes=[BpT])
                            S.op("dve", lambda e: e.tensor_tensor(TTn[:], v4(pT), TTc[:], ALU.add), reads=[BpT, BTTc], writes=[BTTn])
                            Nc, BNc, NTc, BNTc, TTc, BTTc = Nn, BNn, NTn, BNTn, TTn, BTTn
                            yield None
                        yield (TTc, BTTc)

                    def recur(b4, j, TTc, BTTc):
                        par = b4 % 2
                        ARB, BARB = ARB_[par]; AK, BAK = AK_[par]; TOK, BTOK = TOK_[par]
                        c = b4 * 4 + j
                        gc = sgi * 16 + c
                        cs = slice(c * 64, (c + 1) * 64)
                        Xb, BXb = Xb_[gc % 2]; Ub, BUb = Ub_[gc % 2]
                        Sbf, BSbf = Sbf_[gc % 2]
                        Sbn, BSbn = Sbf_[(gc + 1) % 2]
                        for pb in (0, 64):
                            ca = pb
                            P = slice(pb, pb + 64)
                            tp = (pb, pb)
                            S.op("pe", lambda e: e.matmul(pxu[P, 0:64], ARt[P, c, ca:ca + 64], Sbf[P, :], start=True, stop=False, tile_position=tp), reads=[BARt, BSbf], writes=[Bpxu])
                            S.op("pe", lambda e: e.matmul(pxu[P, 0:64], AK[P, j, ca:ca + 64], TOK[P, j, 128:192], start=False, stop=True, tile_position=tp), reads=[BAK, BTOK], writes=[Bpxu])
                        S.op("act", lambda e: e.copy(Xb[:], pxu[:, 0:64]), reads=[Bpxu], writes=[BXb])
                        for pb in (0, 64):
                            P = slice(pb, pb + 64)
                            S.op("pe", lambda e: e.matmul(pxu[P, 64:128], TTc[P, j, pb:pb + 64], Xb[P, :], start=True, stop=True, tile_position=(pb, pb)), reads=[BTTc, BXb], writes=[Bpxu])
                        S.op("dve", lambda e: e.tensor_copy(Ub[:], pxu[:, 64:128]), reads=[Bpxu], writes=[BUb])
                        for pb in (0, 64):
                            P = slice(pb, pb + 64)
                            tp = (pb, pb)
                            S.op("pe", lambda e: e.matmul(pxu[P, 128:192], TOK[P, j, 0:64], Ub[P, :], start=True, stop=False, tile_position=tp), reads=[BTOK, BUb], writes=[Bpxu])
                            S.op("pe", lambda e: e.matmul(pxu[P, 128:192], TOK[P, j, 64:128], TOK[P, j, 128:192], start=False, stop=True, tile_position=tp), reads=[BTOK], writes=[Bpxu])
                        if own:
                            for pb in (0, 64):
                                cr = 64 - pb
                                P = slice(pb, pb + 64)
                                tp = (pb, pb)
                                S.op("pe", lambda e: e.matmul(pys[P, 0:64], ARt[P, c, cr:cr + 64], Sbf[P, :], start=True, stop=False, tile_position=tp), reads=[BARt, BSbf], writes=[Bpys])
                                S.op("pe", lambda e: e.matmul(pys[P, 0:64], AK[P, j, cr:cr + 64], TOK[P, j, 128:192], start=False, stop=False, tile_position=tp), reads=[BAK, BTOK], writes=[Bpys])
                                S.op("pe", lambda e: e.matmul(pys[P, 0:64], ARB[P, j, cr:cr + 64], Ub[P, :], start=False, stop=True, tile_position=tp), reads=[BARB, BUb], writes=[Bpys])
                                S.op("pe", lambda e: e.matmul(pys[P, 128:129], rk[P, cs], blk1b[P, pb:pb + 1], start=True, stop=True, tile_position=tp), reads=[Brk, Bblkb], writes=[Bpys])
                                gtok = slice(sgi * 1024 + c * 64, sgi * 1024 + (c + 1) * 64)
                                hc = c0 + pb
                                for kc in range(2):
                                    S.op("pe", lambda e: e.matmul(pys[P, 64:128], sxg[:, kc, gtok], wgat[:, kc, hc:hc + 64], start=(kc == 0), stop=(kc == 1), tile_position=(0, pb)), reads=[Bsxg, Bwgat], writes=[Bpys])
                        S.op("dve", lambda e: e.scalar_tensor_tensor(S32[:], S32[:], wcs[:, c:c + 1], pxu[:, 128:192], ALU.mult, ALU.add), reads=[BS32, Bwcs, Bpxu], writes=[BS32])
                        S.op("act", lambda e: e.copy(Sbn[:], S32[:]), reads=[BS32], writes=[BSbn])
                        if own:
                            ysb, Bysb = ysb_[gc % 2]; ycn, Bycn = ycn_[gc % 2]; yjk, Byjk = yjk_[gc % 2]
                            s4, Bs4 = st4_[gc % 2]; rwb, Brwb = rwb_[gc % 2]
                            S.op("dve", lambda e: e.memset(s4[:], 0.0), writes=[Bs4])
                            S.op("act", lambda e: e.activation(ysb[:], pys[:, 0:64], AF.Identity, accum_out=s4[:, 0:1]), reads=[Bpys, Bs4], writes=[Bysb, Bs4])
                            S.op("dve", lambda e: e.tensor_scalar(s4[:, 1:2], s4[:, 0:1], -1.0 / 64, None, ALU.mult), reads=[Bs4], writes=[Bs4])
                            S.op("dve", lambda e: e.tensor_scalar(ycn[:], ysb[:], s4[:, 1:2], None, ALU.add), reads=[Bysb, Bs4], writes=[Bycn])
                            S.op("act", lambda e: e.activation(yjk[:], ycn[:], AF.Square, accum_out=s4[:, 2:3]), reads=[Bycn], writes=[Byjk, Bs4])
                            S.op("act", lambda e: e.activation(s4[:, 3:4], s4[:, 2:3], AF.Ln, bias=64e-5, scale=1.0 / 64), reads=[Bs4], writes=[Bs4])
                            S.op("act", lambda e: e.activation(s4[:, 4:5], s4[:, 3:4], AF.Exp, scale=-0.5), reads=[Bs4], writes=[Bs4])
                            S.op("dve", lambda e: e.scalar_tensor_tensor(ycn[:], ycn[:], s4[:, 4:5], GNW[:], ALU.mult, ALU.mult), reads=[Bycn, Bs4, BGNW], writes=[Bycn])
                            S.op("dve", lambda e: e.tensor_tensor(ycn[:], ycn[:], GNB[:], ALU.add), reads=[Bycn, BGNB], writes=[Bycn])
                            S.op("act", lambda e: e.copy(s4[:, 5:6], pys[:, 128:129]), reads=[Bpys], writes=[Bs4])
                            S.op("dve", lambda e: e.scalar_tensor_tensor(ycn[:], TOK[:, j, 128:192], s4[:, 5:6], ycn[:], ALU.mult, ALU.add), reads=[BTOK, Bs4, Bycn], writes=[Bycn])
                            S.op("dve", lambda e: e.tensor_tensor(rwb[:], ycn[:], pys[:, 64:128], ALU.mult), reads=[Bycn, Bpys], writes=[Brwb])
                            for pb in (0, 64):
                                P = slice(pb, pb + 64)
                                S.op("pe", lambda e: e.matmul(pys[P, 192:256], rwb[P, :], identb[P, pb:pb + 64], start=True, stop=True, tile_position=(pb, pb)), reads=[Brwb, Bidb], writes=[Bpys])
                            S.op("act", lambda e: e.copy(rwTs[:, cs], pys[:, 192:256]), reads=[Bpys], writes=[BrwTs])

                    def run_all(g):
                        r = None
                        for r in g:
                            pass
                        return r

                    cur = run_all(stage12(0))
                    for b4 in range(4):
                        g = stage12(b4 + 1) if b4 < 3 else None
                        if g is not None:
                            next(g)
                        for j in range(4):
                            recur(b4, j, cur[0], cur[1])
                            next(precast, None)
                            if g is not None:
                                next(g)
                                if j == 3:
                                    next(g)
                                    cur = next(g)

                    if own:
                        S.dma(RWT[c0:c0 + 128, (sgi - 2) * 1024:(sgi - 1) * 1024], rwTs[:], reads=[BrwTs], acc=[B_RWT])
            for _ in precast:
                pass


        esD = ExitStack()
        if stop == "C":
            S.drain("sp")
            return nc
        with esD:
            accT, Bacc = SB(esD, "accT", [128, 32, 512], F32)
            actA, BactA = SB(esD, "actA", [128, 32, 512], BF16)
            actB, BactB = SB(esD, "actB", [128, 20, 512], BF16)
            hid, Bhid = actB, BactB
            gts = [SB(esD, "gts%d" % i, [128, 2, 512], BF16) for i in range(2)]
            nmlp, Bnmlp = SB(esD, "nmlp", [128, 32], F32)
            nple, Bnple = SB(esD, "nple", [128, 32], F32)
            S.dma(nmlp[:], nmlp_d[:], writes=[Bnmlp])
            S.dma(nple[:], nple_d[:], writes=[Bnple])
            xrow = [SB(esD, "xrow%d" % i, [128, 1024], F32) for i in range(2)]
            osd = [SB(esD, "osd%d" % i, [128, 12, 129], F32) for i in range(1)]
            numt, Bnum = SB(esD, "numt", [128, 4, 129], F32)
            rden, Brden = SB(esD, "rden", [128, 4], F32)
            attb, Battb = SB(esD, "attb", [128, 4, 128], BF16)
            pld, Bpld = SB(esD, "pld", [128, 256], F32)
            pldb, Bpldb = SB(esD, "pldb", [128, 256], BF16)
            pT, BpT = SB(esD, "pT", [128, 2, 512], BF16)
            tmpA = [SB(esD, "tmpA%d" % i, [128, 512], F32) for i in range(2)]
            sqd, Bsqd = SB(esD, "sqd", [128, 512], BF16)
            rstd, Brstd = SB(esD, "rstd", [128, 512], F32)
            pdm = [PS(esD, "pdm%d" % i, [128, 512]) for i in range(3)]
            pdb = [PS(esD, "pdb%d" % i, [128, 512]) for i in range(2)]
            pdn, Bpdn = PS(esD, "pdn", [128, 512])
            pdt = [PS(esD, "pdt%d" % i, [128, 1024], BF16) for i in range(1)]
            pdf, Bpdf = PS(esD, "pdf", [128, 512])
            wpb = [SB(esD, "wpb%d" % i, [128, 256], BF16) for i in range(2)]
            dctr = {"mm": 0, "b": 0, "ta": 0, "x": 0}

            def nextp():
                dctr["mm"] += 1
                return pdm[dctr["mm"] % 3]

            def norm_to(actdst, Bactdst, gainT, BgainT):
                for blk in range(32):
                    S.op("act", lambda e: e.activation(sqd[:], accT[:, blk, :], AF.Square), reads=[Bacc], writes=[Bsqd])
                    S.op("pe", lambda e: e.matmul(pdn[:], onesb[:], sqd[:], start=(blk == 0), stop=(blk == 31)), reads=[Bones, Bsqd], writes=[Bpdn])
                S.op("act", lambda e: e.activation(rstd[:], pdn[:], AF.Ln, bias=1e-6, scale=1.0 / D), reads=[Bpdn], writes=[Brstd])
                S.op("act", lambda e: e.activation(rstd[:], rstd[:], AF.Exp, scale=-0.5), reads=[Brstd], writes=[Brstd])
                for blk in range(32):
                    S.op("dve", lambda e: e.scalar_tensor_tensor(actdst[:, blk, :], accT[:, blk, :], gainT[:, blk:blk + 1], rstd[:], ALU.mult, ALU.mult), reads=[Bacc, BgainT, Brstd], writes=[Bactdst])

            dspecs = []
            for ti_ in range(4):
                for blk_ in range(32):
                    dspecs.append((w_attn_up, 0, 512, blk_ * 128, 128, None))
                    dspecs.append((w_rwkv_up, 0, 2048, blk_ * 128, 128, None))
                for blk_ in range(32):
                    dspecs.append((w_out, 0, D, blk_ * 128, 128, None))
                for ch_ in range(16):
                    for fb_ in range(8):
                        dspecs.append((w_mlp_in, 0, D, ch_ * 1024 + fb_ * 128, 128, None))
                    for blk_ in range(32):
                        dspecs.append((w_mlp_out, ch_ * 1024, 1024, blk_ * 128, 128, None))
                for blk_ in range(32):
                    dspecs.append((w_ple_gate, 0, D, blk_ * 128, 128, None))
                    dspecs.append((w_ple_proj, 0, 256, blk_ * 128, 128, wpb[blk_ % 2]))
            ring = [wbf[0], wbf[1]] + [(w_[:].bitcast(BF16), bw_) for (w_, bw_) in wst]
            key2off = {}
            for bi_, (w_ap, r0, krows, c0_) in enumerate(tail_blocks):
                key2off[(w_ap.tensor.name, r0, krows, c0_)] = wb_off[bi_]

            class WQ2:
                def __init__(self, specs):
                    self.specs = specs
                    self.pos = 0
                    self.dpos = 0
                    self.tiles = {}
                    self.rr = 0

                def _dma(self):
                    (w_ap, r0, krows, c0_, m, dst) = self.specs[self.dpos]
                    kc = (krows + 127) // 128
                    wti, off = key2off[(w_ap.tensor.name, r0, krows, c0_)]
                    if dst is not None:
                        wb, Bwb = dst
                        view = wb[:, 0:kc * 128]
                    else:
                        wb, Bwb = ring[self.rr % len(ring)]
                        self.rr += 1
                        view = wb[:, 0:kc * 128]
                    S.dma(view, WBs[wti][:, off:off + kc * 128], reads=[B_WB], writes=[Bwb], q="sp")
                    self.tiles[self.dpos] = (wb, Bwb, kc)
                    self.dpos += 1

                def next(self):
                    n = len(self.specs)
                    while self.dpos < min(n, self.pos + 4):
                        self._dma()
                    r = self.tiles.pop(self.pos)
                    self.pos += 1
                    return r
            wqD = WQ2(dspecs)

            for ti in range(4):
                T0 = ti * 512
                for sub in range(4):
                    for cq in range(4):
                        xr, Bxr = xrow[dctr["x"] % 2]
                        dctr["x"] += 1
                        S.dma(xr[:], x_d[OWN + T0 + sub * 128:OWN + T0 + (sub + 1) * 128, cq * 1024:(cq + 1) * 1024], writes=[Bxr], q="sp" if cq % 2 == 0 else "act")
                        for half in range(2):
                            for j in range(4):
                                S.op("pe", lambda e: e.transpose(pdf[:, j * 128:(j + 1) * 128], xr[:, (half * 4 + j) * 128:(half * 4 + j + 1) * 128], identf[:]), reads=[Bxr, Bidf], writes=[Bpdf])
                            blk0 = cq * 8 + half * 4
                            S.op("dve", lambda e: e.tensor_copy(accT[:, blk0:blk0 + 4, sub * 128:(sub + 1) * 128], pdf[:].rearrange("p (k t) -> p k t", k=4)), reads=[Bpdf], writes=[Bacc])
                for sub in range(4):
                    od, Bod = osd[0]
                    tk = T0 + sub * 128
                    S.dma(od[:], OS[tk:tk + 128], reads=[B_OS], writes=[Bod])
                    S.op("dve", lambda e: e.tensor_tensor(numt[:], od[:, 0:4, :], od[:, 4:8, :], ALU.add), reads=[Bod], writes=[Bnum])
                    S.op("dve", lambda e: e.tensor_tensor(numt[:], numt[:], od[:, 8:12, :], ALU.add), reads=[Bod, Bnum], writes=[Bnum])
                    S.op("dve", lambda e: e.reciprocal(rden[:], numt[:, :, 128]), reads=[Bnum], writes=[Brden])
                    S.op("dve", lambda e: e.tensor_tensor(attb[:], numt[:, :, 0:128], rden[:].unsqueeze(2).to_broadcast([128, 4, 128]), ALU.mult), reads=[Bnum, Brden], writes=[Battb])
                    pt, Bpt = pdt[0]
                    for j in range(4):
                        S.op("pe", lambda e: e.transpose(pt[:, j * 128:(j + 1) * 128], attb[:, j, :], identb[:]), reads=[Battb, Bidb], writes=[Bpt])
                    S.op("act", lambda e: e.copy(actB[:, 0:4, sub * 128:(sub + 1) * 128], pt[:, 0:512].rearrange("p (k t) -> p k t", k=4)), reads=[Bpt], writes=[BactB])
                    S.dma(pld[:], p_d[tk:tk + 128, :], writes=[Bpld], q="act")
                    S.op("dve", lambda e: e.tensor_copy(pldb[:], pld[:]), reads=[Bpld], writes=[Bpldb])
                    for j in range(2):
                        S.op("pe", lambda e: e.transpose(pt[:, 512 + j * 128:512 + (j + 1) * 128], pldb[:, j * 128:(j + 1) * 128], identb[:]), reads=[Bpldb, Bidb], writes=[Bpt])
                    S.op("act", lambda e: e.copy(pT[:, :, sub * 128:(sub + 1) * 128], pt[:, 512:768].rearrange("p (k t) -> p k t", k=2)), reads=[Bpt], writes=[BpT])
                S.dma(actB[:, 4:20, :], RWT[:, T0:T0 + 512].rearrange("(k p) t -> p k t", p=128), reads=[B_RWT], writes=[BactB])
                for blk in range(32):
                    gt, Bgt = gts[blk % 2]
                    S.dma(gt[:, 0, :], GT[blk * 128:(blk + 1) * 128, T0:T0 + 512], reads=[B_GT], writes=[Bgt])
                    S.dma(gt[:, 1, :], GT[4096 + blk * 128:4096 + (blk + 1) * 128, T0:T0 + 512], reads=[B_GT], writes=[Bgt], q="act")
                    pa, Bpa = pdb[0]
                    pr, Bpr = pdb[1]
                    wa, Bwa, _ = wqD.next()
                    for kc in range(4):
                        S.op("pe", lambda e: e.matmul(pa[:], wa[:, kc * 128:(kc + 1) * 128], actB[:, kc, :], start=(kc == 0), stop=(kc == 3)), reads=[Bwa, BactB], writes=[Bpa])
                    wr, Bwr, _ = wqD.next()
                    for kc in range(16):
                        S.op("pe", lambda e: e.matmul(pr[:], wr[:, kc * 128:(kc + 1) * 128], actB[:, 4 + kc, :], start=(kc == 0), stop=(kc == 15)), reads=[Bwr, BactB], writes=[Bpr])
                    ta, Bta = tmpA[0]
                    tb, Btb = tmpA[1]
                    S.op("dve", lambda e: e.tensor_tensor(ta[:], pa[:], gt[:, 0, :], ALU.mult), reads=[Bpa, Bgt], writes=[Bta])
                    S.op("dve", lambda e: e.tensor_tensor(tb[:], pr[:], gt[:, 1, :], ALU.mult), reads=[Bpr, Bgt], writes=[Btb])
                    S.op("pool", lambda e: e.tensor_tensor(actA[:, blk, :], ta[:], tb[:], ALU.add), reads=[Bta, Btb], writes=[BactA])
                for blk in range(32):
                    wb, Bwb, _ = wqD.next()
                    pm_, Bpm = nextp()
                    for kc in range(32):
                        S.op("pe", lambda e: e.matmul(pm_[:], wb[:, kc * 128:(kc + 1) * 128], actA[:, kc, :], start=(kc == 0), stop=(kc == 31)), reads=[Bwb, BactA], writes=[Bpm])
                    S.op("dve", lambda e: e.tensor_tensor(accT[:, blk, :], accT[:, blk, :], pm_[:], ALU.add), reads=[Bacc, Bpm], writes=[Bacc])
                norm_to(actA, BactA, nmlp, Bnmlp)
                for ch in range(16):
                    for fb in range(8):
                        wb, Bwb, _ = wqD.next()
                        pm_, Bpm = nextp()
                        for kc in range(32):
                            S.op("pe", lambda e: e.matmul(pm_[:], wb[:, kc * 128:(kc + 1) * 128], actA[:, kc, :], start=(kc == 0), stop=(kc == 31)), reads=[Bwb, BactA], writes=[Bpm])
                        ta, Bta = tmpA[fb % 2]
                        S.op("act", lambda e: e.activation(ta[:], pm_[:], AF.Relu), reads=[Bpm], writes=[Bta])
                        S.op("dve", lambda e: e.tensor_tensor(hid[:, fb, :], ta[:], ta[:], ALU.mult), reads=[Bta], writes=[Bhid])
                    for blk in range(32):
                        wb, Bwb, _ = wqD.next()
                        pm_, Bpm = nextp()
                        for kc in range(8):
                            S.op("pe", lambda e: e.matmul(pm_[:], wb[:, kc * 128:(kc + 1) * 128], hid[:, kc, :], start=(kc == 0), stop=(kc == 7)), reads=[Bwb, Bhid], writes=[Bpm])
                        S.op("dve", lambda e: e.tensor_tensor(accT[:, blk, :], accT[:, blk, :], pm_[:], ALU.add), reads=[Bacc, Bpm], writes=[Bacc])
                norm_to(actA, BactA, nple, Bnple)
                for blk in range(32):
                    wb, Bwb, _ = wqD.next()
                    pm_, Bpm = nextp()
                    pp, Bpp = pdb[blk % 2]
                    for kc in range(32):
                        S.op("pe", lambda e: e.matmul(pm_[:], wb[:, kc * 128:(kc + 1) * 128], actA[:, kc, :], start=(kc == 0), stop=(kc == 31)), reads=[Bwb, BactA], writes=[Bpm])
                    wp, Bwp, _ = wqD.next()
                    for kc in range(2):
                        S.op("pe", lambda e: e.matmul(pp[:], wp[:, kc * 128:(kc + 1) * 128], pT[:, kc, :], start=(kc == 0), stop=(kc == 1)), reads=[Bwp, BpT], writes=[Bpp])
                    ta, Bta = tmpA[blk % 2]
                    S.op("act", lambda e: e.activation(ta[:], pm_[:], AF.Sigmoid), reads=[Bpm], writes=[Bta])
                    S.op("dve", lambda e: e.tensor_tensor(ta[:], ta[:], pp[:], ALU.mult), reads=[Bta, Bpp], writes=[Bta])
                    S.op("pool", lambda e: e.tensor_tensor(accT[:, blk, :], accT[:, blk, :], ta[:], ALU.add), reads=[Bacc, Bta], writes=[Bacc])
                for sub in range(4):
                    for cq in range(4):
                        xr, Bxr = xrow[dctr["x"] % 2]
                        dctr["x"] += 1
                        for half in range(2):
                            for j in range(4):
                                blk = cq * 8 + half * 4 + j
                                S.op("pe", lambda e: e.transpose(pdf[:, j * 128:(j + 1) * 128], accT[:, blk, sub * 128:(sub + 1) * 128], identf[:]), reads=[Bacc, Bidf], writes=[Bpdf])
                            S.op("act", lambda e: e.copy(xr[:, half * 512:(half + 1) * 512], pdf[:]), reads=[Bpdf], writes=[Bxr])
                        S.dma(out_d[T0 + sub * 128:T0 + (sub + 1) * 128, cq * 1024:(cq + 1) * 1024], xr[:], reads=[Bxr], writes=[])
        S.drain("sp")
        print("instr counts", S.cnt)
    return nc


_CACHE = {}


def make_in_maps(inp):
    f32 = np.float32
    x = np.asarray(inp["x"], f32)
    p = np.asarray(inp["p"], f32)[0]
    c = host_consts()
    shift_mix = np.asarray(inp["shift_mix"], f32)[0]
    mixT = np.zeros((128, 52), f32)
    for b in range(52):
        mixT[:ZM[b], b] = shift_mix[ZST[b]:ZST[b] + ZM[b]]
    rwp = np.stack([np.asarray(inp[k], f32).reshape(2048).reshape(16, 128).T
                    for k in ("w0", "a0", "k_k", "k_a", "r_k")], axis=1)
    rb33 = np.concatenate([np.asarray(inp["rel_bias"], f32), np.ones((1, 12), f32)], axis=0)
    qkg = np.stack([np.asarray(inp["q_gain"], f32)[0], np.asarray(inp["k_gain"], f32)[0]], axis=1)
    shared = {
        "w_in": np.asarray(inp["w_in"], f32)[0],
        "w_attn_up": np.asarray(inp["w_attn_up"], f32)[0],
        "w_decay_up": np.asarray(inp["w_decay_up"], f32)[0],
        "w_aaa_up": np.asarray(inp["w_aaa_up"], f32)[0],
        "w_gate_up": np.asarray(inp["w_gate_up"], f32)[0],
        "w_rwkv_up": np.asarray(inp["w_rwkv_up"], f32)[0],
        "w_out": np.asarray(inp["w_out"], f32)[0],
        "w_mlp_in": np.asarray(inp["w_mlp_in"], f32)[0],
        "w_mlp_out": np.asarray(inp["w_mlp_out"], f32)[0],
        "w_ple_gate": np.asarray(inp["w_ple_gate"], f32)[0],
        "w_ple_proj": np.asarray(inp["w_ple_proj"], f32)[0],
        "norm_mix": np.asarray(inp["norm_mix"], f32).reshape(1, D),
        "qkg": np.ascontiguousarray(qkg),
        "rb33": rb33,
        "mixT": mixT,
        "rwp": np.ascontiguousarray(rwp),
        "gn_w": np.asarray(inp["gn_w"], f32).reshape(1, 2048),
        "gn_b": np.asarray(inp["gn_b"], f32).reshape(1, 2048),
        "nmlpT": np.ascontiguousarray(np.asarray(inp["norm_mlp"], f32).reshape(32, 128).T),
        "npleT": np.ascontiguousarray(np.asarray(inp["norm_ple"], f32).reshape(32, 128).T),
        "ident": c["ident"], "masks": c["masks"], "scanmask": c["scanmask"],
        "blkones": c["blkones"], "oh": c["oh"],
    }
    in_maps = []
    for core in range(8):
        b, th = core // 2, core % 2
        if th == 0:
            xl = np.concatenate([np.zeros((OWN, D), f32), x[b, :OWN]], axis=0)
        else:
            xl = x[b]
        m = dict(shared)
        m["x"] = np.ascontiguousarray(xl)
        m["p"] = np.ascontiguousarray(p[b, th * OWN:(th + 1) * OWN])
        m["pm"] = np.full((128, 1), NEG if th == 0 else 0.0, f32)
        in_maps.append(m)
    return in_maps


def kernel(**inp):
    f32 = np.float32
    x = np.asarray(inp["x"], f32)
    in_maps = make_in_maps(inp)
    if "nc" not in _CACHE:
        _CACHE["nc"] = build_program()
    res = run_bass_kernel_spmd(_CACHE["nc"], in_maps, core_ids=list(range(8)))
    out = np.zeros((4, SEQ, D), f32)
    for core in range(8):
        b, th = core // 2, core % 2
        out[b, th * OWN:(th + 1) * OWN] = np.asarray(res.results[core]["out"], f32)
    return out
```

```python
import math
import numpy as np
from contextlib import ExitStack
import concourse.bass as bass
import concourse.mybir as mybir
from concourse.bass_utils import run_bass_kernel_spmd

F32 = mybir.dt.float32
BF16 = mybir.dt.bfloat16
ALU = mybir.AluOpType
AF = mybir.ActivationFunctionType

D = 4096
SEQ = 4096
OWN = 2048
NEG = -30000.0
DILS = (1, 4, 16)
ZST = [128 * i for i in range(48)] + [6144, 6240, 6336, 6464]
ZM = [128] * 48 + [96, 96, 128, 128]
A_END = 4608
R_END = 4608 + 6592


class Buf:
    __slots__ = ("w", "r")

    def __init__(self):
        self.w = {}
        self.r = {}


class Sched:
    ENG = ["pe", "act", "dve", "pool", "sp"]
    NSLOT = 12

    EPOCH = 8000

    def __init__(self, nc, es):
        self.nc = nc
        self.es = es
        self.e = dict(pe=nc.tensor, act=nc.scalar, dve=nc.vector, pool=nc.gpsimd, sp=nc.sync)
        self.sem = {}
        self.slots = {}
        for q in ["sp", "act", "pool"]:
            for i in range(self.NSLOT):
                k = "d%s%d" % (q, i)
                self.slots[k] = 0
        self.slot_rr = {"sp": 0, "act": 0, "pool": 0}
        self.cnt = {k: 0 for k in self.ENG}
        self.seen = {k: {} for k in self.ENG}

    def semof(self, k, v):
        ep = (v - 1) // self.EPOCH
        kk = (k, ep)
        if kk not in self.sem:
            self.sem[kk] = self.es.enter_context(self.nc.semaphore("s_%s_%d" % (k, ep)))
        return self.sem[kk], v - ep * self.EPOCH

    def _wait(self, en, deps):
        need = {}
        for (k, v) in deps:
            if k == "pe" and en == "pe":
                continue
            if v > need.get(k, 0):
                need[k] = v
        sn = self.seen[en]
        for k, v in need.items():
            if sn.get(k, 0) < v:
                sm, rv = self.semof(k, v)
                self.e[en].wait_ge(sm, rv)
                sn[k] = v

    def _deps(self, reads, writes):
        deps = []
        for b in reads:
            deps.extend(b.w.items())
        for b in writes:
            deps.extend(b.w.items())
            deps.extend(b.r.items())
        return deps

    def _mark(self, tok, reads, writes):
        k, v = tok
        for b in reads:
            if b.r.get(k, 0) < v:
                b.r[k] = v
        for b in writes:
            b.w = {k: v}
            b.r = {}

    def op(self, en, fn, reads=(), writes=()):
        self._wait(en, self._deps(reads, writes))
        ins = fn(self.e[en])
        self.cnt[en] += 1
        sm, _ = self.semof(en, self.cnt[en])
        ins.then_inc(sm, 1)
        self._mark((en, self.cnt[en]), reads, writes)

    def dma(self, out, in_, reads=(), writes=(), q="sp", acc=(), **kw):
        i = self.slot_rr[q]
        self.slot_rr[q] = (i + 1) % self.NSLOT
        k = "d%s%d" % (q, i)
        deps = self._deps(reads, writes)
        if self.slots[k] > 0:
            deps.append((k, self.slots[k]))
        self._wait(q, deps)
        ins = self.e[q].dma_start(out=out, in_=in_, **kw)
        self.slots[k] += 16
        sm, _ = self.semof(k, self.slots[k])
        ins.then_inc(sm, 16)
        self._mark((k, self.slots[k]), reads, writes)
        for b in acc:
            b.w[k] = self.slots[k]

    def drain(self, en="sp"):
        for k, val in self.slots.items():
            if val > 0:
                sm, rv = self.semof(k, val)
                self.e[en].wait_ge(sm, rv)


def t5_bucket_np(dist):
    dist = np.asarray(dist, dtype=np.int64)
    max_exact = 16
    d_f = np.maximum(dist, 1).astype(np.float32)
    large = max_exact + (np.log(d_f / np.float32(max_exact)) / np.float32(math.log(2048 / max_exact))
                         * np.float32(32 - max_exact)).astype(np.int32)
    large = np.minimum(large, 31)
    return np.where(dist < max_exact, dist, large)


def host_consts():
    c = {}
    c["ident"] = np.eye(128, dtype=np.float32)
    s = np.arange(64)[:, None]
    t = np.arange(64)[None, :]
    su = (s < t).astype(np.float32)
    iu = (s <= t).astype(np.float32)
    sl = (s > t).astype(np.float32)
    z = np.zeros((64, 64), np.float32)
    mLT = np.block([[su, z], [z, su]])
    mRB = np.block([[z, iu], [iu, z]])
    mAK = np.block([[su, iu], [iu, su]])
    mL = np.block([[sl, z], [z, sl]])
    c["masks"] = np.stack([mLT, mRB, mAK, mL], axis=1).astype(np.float32)
    sm = np.ones((128, 1024), np.float32)
    sm[:, ::64] = 0.0
    c["scanmask"] = sm
    bo = np.zeros((128, 128), np.float32)
    bo[:64, :64] = 1.0
    bo[64:, 64:] = 1.0
    c["blkones"] = bo
    oh = np.zeros((33, 3, 383), np.float32)
    for g, d in enumerate(DILS):
        for u in range(383):
            rel = u - 127
            if 0 <= rel <= 128:
                oh[int(t5_bucket_np(rel * d)), g, u] = 1.0
            else:
                oh[32, g, u] = NEG
    c["oh"] = oh
    return c


def build_program(stop=None, debug=False):
    nc = bass.Bass("TRN2", target_bir_lowering=False)

    def din(name, shape, dt=F32):
        return nc.dram_tensor(name, list(shape), dt, kind="ExternalInput").ap()

    def dscr(name, shape, dt):
        return nc.dram_tensor(name, list(shape), dt, kind="ExternalOutput" if (debug and name in debug) else "Internal").ap()

    x_d = din("x", [SEQ, D])
    p_d = din("p", [OWN, 256])
    pm_d = din("pm", [128, 1])
    w_in = din("w_in", [D, 19392])
    w_attn_up = din("w_attn_up", [512, D])
    w_decay_up = din("w_decay_up", [96, 2048])
    w_aaa_up = din("w_aaa_up", [96, 2048])
    w_gate_up = din("w_gate_up", [256, 2048])
    w_rwkv_up = din("w_rwkv_up", [2048, D])
    w_out = din("w_out", [D, D])
    w_mlp_in = din("w_mlp_in", [D, 16384])
    w_mlp_out = din("w_mlp_out", [16384, D])
    w_ple_gate = din("w_ple_gate", [D, D])
    w_ple_proj = din("w_ple_proj", [256, D])
    norm_mix = din("norm_mix", [1, D])
    qkg_d = din("qkg", [128, 2])
    rb33_d = din("rb33", [33, 12])
    mixT_d = din("mixT", [128, 52])
    rwp_d = din("rwp", [128, 5, 16])
    gnw_d = din("gn_w", [1, 2048])
    gnb_d = din("gn_b", [1, 2048])
    nmlp_d = din("nmlpT", [128, 32])
    nple_d = din("npleT", [128, 32])
    ident_d = din("ident", [128, 128])
    masks_d = din("masks", [128, 4, 128])
    scanmask_d = din("scanmask", [128, 1024])
    blkones_d = din("blkones", [128, 128])
    oh_d = din("oh", [33, 3, 383])
    out_d = nc.dram_tensor("out", [OWN, D], F32, kind="ExternalOutput").ap()

    KT = dscr("KT", [12, 128, SEQ], BF16)
    VT = dscr("VT", [12, 128, SEQ], BF16)
    QT = dscr("QT", [12, 128, OWN], BF16)
    ZT = dscr("ZT", [6656, SEQ], F32)
    GT = dscr("GT", [8192, OWN], BF16)
    OS = dscr("OS", [OWN, 12, 129], F32)
    RWT = dscr("RWT", [2048, OWN], BF16)
    GS = dscr("GS", [12, 130, 383], F32)
    B_KT, B_VT, B_QT, B_ZT, B_GT, B_OS, B_RWT, B_GS = [Buf() for _ in range(8)]
    tail_blocks = []
    for blk_ in range(32):
        tail_blocks.append((w_attn_up, 0, 512, blk_ * 128))
        tail_blocks.append((w_rwkv_up, 0, 2048, blk_ * 128))
    for blk_ in range(32):
        tail_blocks.append((w_out, 0, D, blk_ * 128))
    for ch_ in range(16):
        for fb_ in range(8):
            tail_blocks.append((w_mlp_in, 0, D, ch_ * 1024 + fb_ * 128))
        for blk_ in range(32):
            tail_blocks.append((w_mlp_out, ch_ * 1024, 1024, blk_ * 128))
    for blk_ in range(32):
        tail_blocks.append((w_ple_gate, 0, D, blk_ * 128))
        tail_blocks.append((w_ple_proj, 0, 256, blk_ * 128))
    wb_off = []
    CAP_ = 900000
    sizes_ = [0]
    for (_, _, krows_, _) in tail_blocks:
        n_ = ((krows_ + 127) // 128) * 128
        if sizes_[-1] + n_ > CAP_:
            sizes_.append(0)
        wb_off.append((len(sizes_) - 1, sizes_[-1]))
        sizes_[-1] += n_
    WBs = [dscr("WB%d" % i_, [128, sz_], BF16) for i_, sz_ in enumerate(sizes_)]
    B_WB = Buf()
    B_out = Buf()

    top = ExitStack()
    with top:
        S = Sched(nc, top)

        def SB(es, name, shape, dt):
            return es.enter_context(nc.sbuf_tensor("sb_" + name, list(shape), dt)), Buf()

        def PS(es, name, shape, dt=F32):
            return es.enter_context(nc.psum_tensor("ps_" + name, list(shape), dt)), Buf()

        identf, Bidf = SB(top, "identf", [128, 128], F32)
        identb, Bidb = SB(top, "identb", [128, 128], BF16)
        S.dma(identf[:], ident_d[:], writes=[Bidf])
        S.op("dve", lambda e: e.tensor_copy(identb[:], identf[:]), reads=[Bidf], writes=[Bidb])
        onesb, Bones = SB(top, "onesb", [128, 128], BF16)
        S.op("dve", lambda e: e.memset(onesb[:], 1.0), writes=[Bones])

        wst = [SB(top, "wst%d" % i, [128, 2048], F32) for i in range(4)]
        wbf = [SB(top, "wbf%d" % i, [128, 4096], BF16) for i in range(2)]
        wctr = [0, 0, 0]

        def load_w(w_ap, r0, krows, c0, m, dst=None):
            kc = (krows + 127) // 128
            if dst is not None:
                wb, Bwb = dst
            else:
                wb, Bwb = wbf[wctr[1] % 2]
                wctr[1] += 1
            pieces = [(0, kc)] if kc <= 16 else [(0, 16), (16, kc)]
            first = True
            for (k0, k1) in pieces:
                st, Bst = wst[wctr[0] % 3]
                q = "sp" if wctr[0] % 2 == 0 else "act"
                wctr[0] += 1
                nk = k1 - k0
                src = w_ap[r0 + k0 * 128:r0 + k1 * 128, c0:c0 + m].rearrange("(k p) c -> p k c", p=128)
                S.dma(st[:, 0:nk * m].rearrange("p (k c) -> p k c", k=nk), src, writes=[Bst], q=q)
                ce = "act" if wctr[0] % 2 == 0 else "dve"
                if ce == "act":
                    fn = lambda e: e.copy(wb[:, k0 * m:k1 * m], st[:, 0:nk * m])
                else:
                    fn = lambda e: e.tensor_copy(wb[:, k0 * m:k1 * m], st[:, 0:nk * m])
                if first:
                    S.op(ce, fn, reads=[Bst], writes=[Bwb])
                else:
                    S.op(ce, fn, reads=[Bst, Bwb], writes=[])
                    Bwb.w[ce] = S.cnt[ce]
                first = False
            return wb, Bwb, kc

        class WQ:
            def __init__(self, specs):
                self.specs = specs
                self.pos = 0
                self.dpos = 0
                self.cpos = 0
                self.staged = {}
                self.tiles = {}

            def _dma(self):
                i = self.dpos
                (w_ap, r0, krows, c0, m, dst) = self.specs[i]
                kc = (krows + 127) // 128
                pieces = [(0, kc)] if kc <= 16 else [(0, 16), (16, kc)]
                lst = []
                for (k0, k1) in pieces:
                    st, Bst = wst[wctr[0] % 4]
                    wctr[0] += 1
                    nk = k1 - k0
                    src = w_ap[r0 + k0 * 128:r0 + k1 * 128, c0:c0 + m].rearrange("(k p) c -> p k c", p=128)
                    S.dma(st[:, 0:nk * m].rearrange("p (k c) -> p k c", k=nk), src, writes=[Bst], q="sp")
                    lst.append((st, Bst, k0, k1))
                self.staged[i] = lst
                self.dpos += 1

            def _cast(self):
                i = self.cpos
                (w_ap, r0, krows, c0, m, dst) = self.specs[i]
                kc = (krows + 127) // 128
                if dst is not None:
                    wb, Bwb = dst
                else:
                    wb, Bwb = wbf[wctr[1] % 2]
                    wctr[1] += 1
                first = True
                for (st, Bst, k0, k1) in self.staged.pop(i):
                    nk = k1 - k0
                    wctr[2] += 1
                    ce = "act" if wctr[2] % 2 == 0 else "dve"
                    if ce == "act":
                        fn = lambda e: e.copy(wb[:, k0 * m:k1 * m], st[:, 0:nk * m])
                    else:
                        fn = lambda e: e.tensor_copy(wb[:, k0 * m:k1 * m], st[:, 0:nk * m])
                    if first:
                        S.op(ce, fn, reads=[Bst], writes=[Bwb])
                    else:
                        S.op(ce, fn, reads=[Bst, Bwb], writes=[])
                        Bwb.w[ce] = S.cnt[ce]
                    first = False
                self.tiles[i] = (wb, Bwb, kc)
                self.cpos += 1

            def next(self):
                n = len(self.specs)
                while self.cpos < min(n, self.pos + 2):
                    if self.dpos <= self.cpos:
                        self._dma()
                    self._cast()
                while self.dpos < min(n, self.pos + 3):
                    self._dma()
                r = self.tiles.pop(self.pos)
                self.pos += 1
                return r

        esA = ExitStack()
        with esA:
            gmix, Bgmix = SB(esA, "gmix", [128, D], F32)
            S.dma(gmix[:], norm_mix.partition_broadcast(128), writes=[Bgmix])
            qkg, Bqkg = SB(esA, "qkg", [128, 2], F32)
            S.dma(qkg[:], qkg_d[:], writes=[Bqkg])
            S.op("dve", lambda e: e.tensor_scalar(qkg[:, 0:1], qkg[:, 0:1], 128.0 ** -0.5, None, ALU.mult), reads=[Bqkg], writes=[Bqkg])
            mixT, Bmix = SB(esA, "mixT", [128, 52], F32)
            omix, Bomix = SB(esA, "omix", [128, 52], F32)
            S.dma(mixT[:], mixT_d[:], writes=[Bmix])
            S.op("dve", lambda e: e.tensor_scalar(omix[:], mixT[:], -1.0, 1.0, ALU.mult, ALU.add), reads=[Bmix], writes=[Bomix])
            zlast, Bzl = SB(esA, "zlast", [128, 52], F32)
            S.op("dve", lambda e: e.memset(zlast[:], 0.0), writes=[Bzl])
            hT, BhT = SB(esA, "hT", [128, 32, 1024], BF16)
            xin = [SB(esA, "xin%d" % i, [128, D], F32) for i in range(1)]
            hb, Bhb = SB(esA, "hb", [128, D], BF16)
            ss, Bss = SB(esA, "ss", [128, 4], F32)
            ptr = [PS(esA, "ptr%d" % i, [128, 1024], BF16) for i in range(2)]
            pmm = [PS(esA, "pmm%d" % i, [128, 512]) for i in range(3)]
            pn = [PS(esA, "pn%d" % i, [128, 512]) for i in range(2)]
            sqb, Bsqb = SB(esA, "sqb", [128, 512], BF16)
            rsb, Brsb = SB(esA, "rsb", [128, 512], F32)
            stg16 = [SB(esA, "stg16_%d" % i, [128, 1024], BF16) for i in range(2)]
            zbufs = [SB(esA, "zbuf%d" % i, [128, 1025], F32) for i in range(2)]
            ztmp, Bztmp = SB(esA, "ztmp", [128, 1024], F32)
            zs = [SB(esA, "zs%d" % i, [128, 1024], F32) for i in range(2)]
            ctr = {"ev": 0, "mm": 0, "pn": 0, "s16": 0, "zs": 0}

            def a_blocks(qi):
                blocks = []
                for h in range(12):
                    blocks.append(("k", 1536 + h * 128, 128, h))
                for h in range(12):
                    blocks.append(("v", 3072 + h * 128, 128, h))
                for b in range(52):
                    blocks.append(("z", A_END + ZST[b], ZM[b], b))
                if qi >= 2:
                    for h in range(12):
                        blocks.append(("q", h * 128, 128, h))
                    for b in range(64):
                        blocks.append(("g", R_END + b * 128, 128, b))
                return blocks
            wqA = WQ([(w_in, 0, D, c0_, m_, None) for qi_ in range(4) for (_, c0_, m_, _) in a_blocks(qi_)])

            for qi in range(4):
                own = qi >= 2
                for sub in range(8):
                    xt, Bxt = xin[0]
                    t0 = qi * 1024 + sub * 128
                    S.dma(xt[:], x_d[t0:t0 + 128, :], writes=[Bxt], q="sp" if sub % 2 == 0 else "act")
                    S.op("dve", lambda e: e.memset(ss[:, 0:1], 0.0), writes=[Bss])
                    S.op("act", lambda e: e.activation(hb[:], xt[:], AF.Square, accum_out=ss[:, 0:1]), reads=[Bxt, Bss], writes=[Bhb, Bss])
                    S.op("act", lambda e: e.activation(ss[:, 1:2], ss[:, 0:1], AF.Ln, bias=1e-6, scale=1.0 / D), reads=[Bss], writes=[Bss])
                    S.op("act", lambda e: e.activation(ss[:, 2:3], ss[:, 1:2], AF.Exp, scale=-0.5), reads=[Bss], writes=[Bss])
                    S.op("dve", lambda e: e.scalar_tensor_tensor(hb[:], xt[:], ss[:, 2:3], gmix[:], ALU.mult, ALU.mult), reads=[Bxt, Bss, Bgmix], writes=[Bhb])
                    for g8 in range(4):
                        pt, Bpt = ptr[g8 % 2]
                        for j in range(8):
                            kc = g8 * 8 + j
                            S.op("pe", lambda e: e.transpose(pt[:, j * 128:(j + 1) * 128], hb[:, kc * 128:(kc + 1) * 128], identb[:]), reads=[Bhb, Bidb], writes=[Bpt])
                        en = "act" if g8 % 2 == 0 else "dve"
                        dst = hT[:, g8 * 8:(g8 + 1) * 8, sub * 128:(sub + 1) * 128]
                        srcp = pt[:].rearrange("p (k t) -> p k t", k=8)
                        if en == "act":
                            S.op("act", lambda e: e.copy(dst, srcp), reads=[Bpt], writes=[BhT])
                        else:
                            S.op("dve", lambda e: e.tensor_copy(dst, srcp), reads=[Bpt], writes=[BhT])
                blocks = []
                for h in range(12):
                    blocks.append(("k", 1536 + h * 128, 128, h))
                for h in range(12):
                    blocks.append(("v", 3072 + h * 128, 128, h))
                for b in range(52):
                    blocks.append(("z", A_END + ZST[b], ZM[b], b))
                if own:
                    for h in range(12):
                        blocks.append(("q", h * 128, 128, h))
                    for b in range(64):
                        blocks.append(("g", R_END + b * 128, 128, b))
                blocks = a_blocks(qi)
                for bi, (kind, c0, m, idx) in enumerate(blocks):
                    wb, Bwb, _ = wqA.next()
                    zbuf, Bzbuf = zbufs[bi % 2]
                    if kind in ("k", "q", "v", "g"):
                        st, Bstg = stg16[ctr["s16"] % 2]
                        ctr["s16"] += 1
                    for half in range(2):
                        pm_, Bpm = pmm[ctr["mm"] % 3]
                        ctr["mm"] += 1
                        for kc in range(32):
                            S.op("pe", lambda e: e.matmul(pm_[0:m, :], wb[:, kc * m:(kc + 1) * m], hT[:, kc, half * 512:(half + 1) * 512], start=(kc == 0), stop=(kc == 31)), reads=[Bwb, BhT], writes=[Bpm])
                        if kind in ("k", "q"):
                            d = DILS[idx // 4]
                            S.op("act", lambda e: e.activation(sqb[:], pm_[:], AF.Square), reads=[Bpm], writes=[Bsqb])
                            pn_, Bpn = pn[ctr["pn"] % 2]
                            ctr["pn"] += 1
                            S.op("pe", lambda e: e.matmul(pn_[:], onesb[:], sqb[:], start=True, stop=True), reads=[Bones, Bsqb], writes=[Bpn])
                            S.op("act", lambda e: e.activation(rsb[:], pn_[:], AF.Ln, bias=1e-6, scale=1.0 / 128), reads=[Bpn], writes=[Brsb])
                            S.op("act", lambda e: e.activation(rsb[:], rsb[:], AF.Exp, scale=-0.5), reads=[Brsb], writes=[Brsb])
                            gcol = qkg[:, 0:1] if kind == "q" else qkg[:, 1:2]
                            n_i = 512 // d
                            dst = st[:].rearrange("p (r i) -> p r i", r=d)[:, :, half * n_i:(half + 1) * n_i]
                            S.op("dve", lambda e: e.scalar_tensor_tensor(dst, pm_[:].rearrange("p (i r) -> p r i", r=d), gcol, rsb[:].rearrange("p (i r) -> p r i", r=d), ALU.mult, ALU.mult), reads=[Bpm, Bqkg, Brsb], writes=[Bstg])
                        elif kind == "v":
                            d = DILS[idx // 4]
                            n_i = 512 // d
                            dst = st[:].rearrange("p (r i) -> p r i", r=d)[:, :, half * n_i:(half + 1) * n_i]
                            S.op("act", lambda e: e.copy(dst, pm_[:].rearrange("p (i r) -> p r i", r=d)), reads=[Bpm], writes=[Bstg])
                        elif kind == "g":
                            S.op("act", lambda e: e.activation(st[:, half * 512:(half + 1) * 512], pm_[:], AF.Sigmoid), reads=[Bpm], writes=[Bstg])
                        else:
                            S.op("act", lambda e: e.copy(zbuf[0:m, 1 + half * 512:1 + (half + 1) * 512], pm_[0:m, :]), reads=[Bpm], writes=[Bzbuf])
                    if kind in ("k", "v", "q"):
                        d = DILS[idx // 4]
                        if kind == "q":
                            dd = QT[idx].rearrange("e (r i) -> e r i", r=d)[:, :, (qi - 2) * (1024 // d):(qi - 1) * (1024 // d)]
                            Bd = B_QT
                        else:
                            base = KT if kind == "k" else VT
                            dd = base[idx].rearrange("e (r i) -> e r i", r=d)[:, :, qi * (1024 // d):(qi + 1) * (1024 // d)]
                            Bd = B_KT if kind == "k" else B_VT
                        S.dma(dd, st[:].rearrange("p (r i) -> p r i", r=d), reads=[Bstg], acc=[Bd])
                    elif kind == "g":
                        S.dma(GT[idx * 128:(idx + 1) * 128, (qi - 2) * 1024:(qi - 1) * 1024], st[:], reads=[Bstg], acc=[B_GT])
                    else:
                        b = idx
                        S.op("dve", lambda e: e.tensor_copy(zbuf[0:m, 0:1], zlast[0:m, b:b + 1]), reads=[Bzl], writes=[Bzbuf])
                        S.op("dve", lambda e: e.tensor_scalar(ztmp[0:m, :], zbuf[0:m, 0:1024], mixT[0:m, b:b + 1], None, ALU.mult), reads=[Bzbuf, Bmix], writes=[Bztmp])
                        zt, Bzt = zs[ctr["zs"] % 2]
                        ctr["zs"] += 1
                        S.op("dve", lambda e: e.scalar_tensor_tensor(zt[0:m, :], zbuf[0:m, 1:1025], omix[0:m, b:b + 1], ztmp[0:m, :], ALU.mult, ALU.add), reads=[Bzbuf, Bomix, Bztmp], writes=[Bzt])
                        S.op("dve", lambda e: e.tensor_copy(zlast[0:m, b:b + 1], zbuf[0:m, 1024:1025]), reads=[Bzbuf], writes=[Bzl])
                        S.dma(ZT[ZST[b]:ZST[b] + m, qi * 1024:(qi + 1) * 1024], zt[0:m, :], reads=[Bzt], acc=[B_ZT])


        esB = ExitStack()
        if stop == "A":
            S.drain("sp")
            return nc
        with esB:
            rb33, Brb = SB(esB, "rb33", [33, 12], F32)
            oh, Boh = SB(esB, "oh", [33, 3, 383], F32)
            S.dma(rb33[:], rb33_d[:], writes=[Brb])
            S.dma(oh[:], oh_d[:], writes=[Boh])
            pmc, Bpmc = SB(esB, "pmc", [128, 1], F32)
            S.dma(pmc[:], pm_d[:], writes=[Bpmc])
            gsb, Bgsb = SB(esB, "gsb", [4, 383], F32)
            pg, Bpg = PS(esB, "pg", [4, 383])
            for g in range(3):
                S.op("pe", lambda e: e.matmul(pg[:], rb33[:, g * 4:(g + 1) * 4], oh[:, g, :], start=True, stop=True), reads=[Brb, Boh], writes=[Bpg])
                S.op("dve", lambda e: e.tensor_copy(gsb[:], pg[:]), reads=[Bpg], writes=[Bgsb])
                srcb = gsb[:].unsqueeze(1).to_broadcast([4, 130, 383])
                S.dma(GS[g * 4:(g + 1) * 4, :, :], srcb, reads=[Bgsb], acc=[B_GS])
            biasT, Bbias = SB(esB, "biasT", [128, 12, 2, 128], F32)
            for h in range(12):
                for tl, cc in ((0, 255), (1, 127)):
                    src = bass.AP(GS.tensor, GS[h].offset + cc, [[382, 128], [1, 128]])
                    S.dma(biasT[:, h, tl, :], src, reads=[B_GS], writes=[Bbias])
            Ksb = [SB(esB, "Ksb%d" % i, [128, SEQ], BF16) for i in range(2)]
            Vsb = [SB(esB, "Vsb%d" % i, [128, SEQ], BF16) for i in range(2)]
            Qsb = [SB(esB, "Qsb%d" % i, [128, OWN], BF16) for i in range(2)]
            Vtok = [SB(esB, "Vtok%d" % i, [128, 32, 130], BF16) for i in range(2)]
            for i in range(2):
                S.op("dve", lambda e: e.memset(Vtok[i][0][:], 1.0), writes=[Vtok[i][1]])
            pvt = [PS(esB, "pvt%d" % i, [128, 128], BF16) for i in range(2)]
            pss = [PS(esB, "pss%d" % i, [128, 2, 128]) for i in range(2)]
            pso = [PS(esB, "pso%d" % i, [128, 129]) for i in range(2)]
            ssb = [SB(esB, "ssb%d" % i, [128, 2, 128], F32) for i in range(2)]
            esb = [SB(esB, "esb%d" % i, [128, 2, 128], BF16) for i in range(2)]
            osb = [SB(esB, "osb%d" % i, [128, 129], F32) for i in range(3)]
            it = 0
            for h in range(12):
                d = DILS[h // 4]
                Lc = SEQ // d
                nbc = Lc // 128
                nb0 = nbc // 2
                K_, BK = Ksb[h % 2]
                V_, BV = Vsb[h % 2]
                Q_, BQ = Qsb[h % 2]
                Vt, BVt = Vtok[h % 2]
                S.dma(K_[:], KT[h], reads=[B_KT], writes=[BK])
                S.dma(V_[:], VT[h], reads=[B_VT], writes=[BV], q="act")
                S.dma(Q_[:], QT[h], reads=[B_QT], writes=[BQ])
                nkb = nb0 + 1
                for r in range(d):
                    for kb in range(nb0 - 1, nbc):
                        vi = r * nkb + (kb - (nb0 - 1))
                        pv, Bpv = pvt[vi % 2]
                        S.op("pe", lambda e: e.transpose(pv[:], V_[:, r * Lc + kb * 128:r * Lc + (kb + 1) * 128], identb[:]), reads=[BV, Bidb], writes=[Bpv])
                        if vi % 2 == 0:
                            S.op("act", lambda e: e.copy(Vt[:, vi, 0:128], pv[:]), reads=[Bpv], writes=[BVt])
                        else:
                            S.op("dve", lambda e: e.tensor_copy(Vt[:, vi, 0:128], pv[:]), reads=[Bpv], writes=[BVt])
                for r in range(d):
                    for n in range(nb0, nbc):
                        ps_, Bps = pss[it % 2]
                        po, Bpo = pso[it % 2]
                        s_, Bs = ssb[it % 2]
                        e_, Be = esb[it % 2]
                        o_, Bo = osb[it % 3]
                        it += 1
                        qt = Q_[:, r * (Lc // 2) + (n - nb0) * 128:r * (Lc // 2) + (n - nb0 + 1) * 128]
                        for tl in range(2):
                            kb = n - 1 + tl
                            S.op("pe", lambda e: e.matmul(ps_[:, tl, :], K_[:, r * Lc + kb * 128:r * Lc + (kb + 1) * 128], qt, start=True, stop=True), reads=[BK, BQ], writes=[Bps])
                        S.op("dve", lambda e: e.tensor_tensor(s_[:], ps_[:], biasT[:, h], ALU.add), reads=[Bps, Bbias], writes=[Bs])
                        if n == nb0:
                            S.op("act", lambda e: e.activation(e_[:, 0, :], s_[:, 0, :], AF.Exp, bias=pmc[:, 0:1]), reads=[Bs, Bpmc], writes=[Be])
                            S.op("act", lambda e: e.activation(e_[:, 1, :], s_[:, 1, :], AF.Exp), reads=[Bs], writes=[Be])
                        else:
                            S.op("act", lambda e: e.activation(e_[:], s_[:], AF.Exp), reads=[Bs], writes=[Be])
                        for tl in range(2):
                            vi = r * nkb + (n - 1 + tl - (nb0 - 1))
                            S.op("pe", lambda e: e.matmul(po[:], e_[:, tl, :], Vt[:, vi, 0:129], start=(tl == 0), stop=(tl == 1)), reads=[Be, BVt], writes=[Bpo])
                        S.op("dve", lambda e: e.tensor_copy(o_[:], po[:]), reads=[Bpo], writes=[Bo])
                        tok0 = (n * 128) * d + r - OWN
                        dst = bass.AP(OS.tensor, (tok0 * 12 + h) * 129, [[d * 12 * 129, 128], [1, 129]])
                        S.dma(dst, o_[:], reads=[Bo], acc=[B_OS])

        esC = ExitStack()
        if stop == "B":
            S.drain("sp")
            return nc
        with esC:
            masks, Bmk = SB(esC, "masks", [128, 4, 128], F32)
            S.dma(masks[:], masks_d[:], writes=[Bmk])
            scm, Bscm = SB(esC, "scm", [128, 1024], F32)
            S.dma(scm[:], scanmask_d[:], writes=[Bscm])
            blk1, Bblk = SB(esC, "blk1", [128, 128], F32)
            S.dma(blk1[:], blkones_d[:], writes=[Bblk])
            blk1b, Bblkb = SB(esC, "blk1b", [128, 128], BF16)
            S.op("dve", lambda e: e.tensor_copy(blk1b[:], blk1[:]), reads=[Bblk], writes=[Bblkb])
            rwp, Brwp = SB(esC, "rwp", [128, 5, 16], F32)
            S.dma(rwp[:], rwp_d[:], writes=[Brwp])
            nrw, Bnrw = SB(esC, "nrw", [128, 2, 16], F32)
            S.op("dve", lambda e: e.tensor_scalar(nrw[:, 0, :], rwp[:, 0, :], -1.0, None, ALU.mult), reads=[Brwp], writes=[Bnrw])
            S.op("dve", lambda e: e.tensor_scalar(nrw[:, 1, :], rwp[:, 3, :], -1.0, 1.0, ALU.mult, ALU.add), reads=[Brwp], writes=[Bnrw])
            txw, Btxw = SB(esC, "txw", [96, SEQ], BF16)
            xab, Bxab = SB(esC, "xab", [96, SEQ], BF16)
            sxg, Bsxg = SB(esC, "sxg", [128, 2, SEQ], BF16)
            f32n = ["r", "k", "v", "ew", "cle", "t0", "t1", "t2", "asig", "kkn", "k2", "bb"]
            F = {n: SB(esC, "f_" + n, [128, 1024], F32) for n in f32n}
            ltmp = [F["t0"], F["t1"]]
            li = 0
            for sgi in range(4):
                for (row0, m, kind) in ((6144, 96, "w"), (6240, 96, "a"), (6336, 128, "g0"), (6464, 128, "g1")):
                    lt, Blt = ltmp[li % 2]
                    li += 1
                    S.dma(lt[0:m, :], ZT[row0:row0 + m, sgi * 1024:(sgi + 1) * 1024], reads=[B_ZT], writes=[Blt], q="sp" if li % 2 else "act")
                    sl = slice(sgi * 1024, (sgi + 1) * 1024)
                    if kind == "w":
                        S.op("act", lambda e: e.activation(txw[:, sl], lt[0:96, :], AF.Tanh), reads=[Blt], writes=[Btxw])
                    elif kind == "a":
                        S.op("dve", lambda e: e.tensor_copy(xab[:, sl], lt[0:96, :]), reads=[Blt], writes=[Bxab])
                    else:
                        gi = 0 if kind == "g0" else 1
                        S.op("act", lambda e: e.activation(sxg[:, gi, sl], lt[:, :], AF.Sigmoid), reads=[Blt], writes=[Bsxg])
            wdec, Bwdec = SB(esC, "wdec", [96, 2048], BF16)
            waaa, Bwaaa = SB(esC, "waaa", [96, 2048], BF16)
            wgat, Bwgat = SB(esC, "wgat", [128, 2, 2048], BF16)
            for (dst, Bd, src, rows) in ((wdec, Bwdec, w_decay_up, 96), (waaa, Bwaaa, w_aaa_up, 96)):
                for cch in range(2):
                    st, Bst = wst[cch]
                    S.dma(st[0:rows, 0:1024], src[:, cch * 1024:(cch + 1) * 1024], writes=[Bst])
                    S.op("pool", lambda e: e.tensor_copy(dst[:, cch * 1024:(cch + 1) * 1024], st[0:rows, 0:1024]), reads=[Bst], writes=[Bd])
            for kc in range(2):
                for cch in range(2):
                    st, Bst = wst[cch]
                    S.dma(st[:, 0:1024], w_gate_up[kc * 128:(kc + 1) * 128, cch * 1024:(cch + 1) * 1024], writes=[Bst])
                    S.op("pool", lambda e: e.tensor_copy(wgat[:, kc, cch * 1024:(cch + 1) * 1024], st[:, 0:1024]), reads=[Bst], writes=[Bwgat])

            b16n = ["bt", "kt", "btc", "ktc", "vb", "rk"]
            Bt = {n: SB(esC, "b_" + n, [128, 1024], BF16) for n in b16n}
            ARt, BARt = SB(esC, "ARt", [128, 16, 128], BF16)
            wcs, Bwcs = SB(esC, "wcs", [128, 16], F32)
            GNW, BGNW = SB(esC, "GNW", [128, 64], F32)
            GNB, BGNB = SB(esC, "GNB", [128, 64], F32)
            S32, BS32 = SB(esC, "S32", [128, 64], F32)
            Sbf_ = [SB(esC, "Sbf%d" % i, [128, 64], BF16) for i in range(2)]
            rwTs, BrwTs = SB(esC, "rwTs", [128, 1024], BF16)

            def rot(name, shape, dt, n):
                return [SB(esC, "%s%d" % (name, i), shape, dt) for i in range(n)]
            ARB_ = rot("ARB", [128, 4, 128], BF16, 2)
            AK_ = rot("AK", [128, 4, 128], BF16, 2)
            N_ = rot("N", [128, 4, 128], BF16, 4)
            NT_ = rot("NT", [128, 4, 128], BF16, 4)
            TT_ = rot("TT", [128, 4, 128], BF16, 8)
            TOK_ = rot("TOK", [128, 4, 192], BF16, 2)
            Xb_ = rot("Xb", [128, 64], BF16, 2)
            Ub_ = rot("Ub", [128, 64], BF16, 2)
            ysb_ = rot("ysb", [128, 64], F32, 2)
            ycn_ = rot("ycn", [128, 64], F32, 2)
            yjk_ = rot("yjk", [128, 64], F32, 2)
            st4_ = rot("st4", [128, 8], F32, 2)
            rwb_ = rot("rwb", [128, 64], BF16, 2)
            ps1, Bps1 = PS(esC, "ps1", [128, 512])
            ps2, Bps2 = PS(esC, "ps2", [128, 512])
            ps3, Bps3 = PS(esC, "ps3", [128, 512])
            pN, BpN = PS(esC, "pN", [128, 512])
            pNT, BpNT = PS(esC, "pNT", [128, 512])
            pT, BpT = PS(esC, "pT", [128, 512])
            pxu, Bpxu = PS(esC, "pxu", [128, 512])
            pys, Bpys = PS(esC, "pys", [128, 512])
            plo, Bplo = pT, BpT
            zb, Bzb = SB(esC, "zb", [128, 512], BF16)
            S.op("dve", lambda e: e.memset(zb[:], 0.0), writes=[Bzb])
            S.op("pe", lambda e: e.matmul(ps3[:], zb[:, 0:128], zb[:], start=True, stop=True), reads=[Bzb], writes=[Bps3])

            def Fv(n):
                return F[n][0], F[n][1]

            def precast_gen():
                pi_ = 0
                for bi_, (w_ap, r0, krows, c0_) in enumerate(tail_blocks):
                    kc = (krows + 127) // 128
                    pieces = [(0, kc)] if kc <= 16 else [(0, 16), (16, kc)]
                    for (k0, k1) in pieces:
                        nk = k1 - k0
                        st, Bst = wst[pi_ % 4]
                        ob, Bob = wbf[pi_ % 2]
                        pi_ += 1
                        src = w_ap[r0 + k0 * 128:r0 + k1 * 128, c0_:c0_ + 128].rearrange("(k p) c -> p k c", p=128)
                        S.dma(st[:, 0:nk * 128].rearrange("p (k c) -> p k c", k=nk), src, writes=[Bst], q="sp")
                        S.op("pool", lambda e: e.tensor_copy(ob[:, 0:nk * 128], st[:, 0:nk * 128]), reads=[Bst], writes=[Bob])
                        S.dma(WBs[wb_off[bi_][0]][:, wb_off[bi_][1] + k0 * 128:wb_off[bi_][1] + k1 * 128], ob[:, 0:nk * 128], reads=[Bob], acc=[B_WB], q="pool")
                        yield None
            precast = precast_gen()

            for hp in range(16):
                S.dma(GNW[0:64, :], gnw_d[0:1, hp * 128:hp * 128 + 64].partition_broadcast(64), writes=[BGNW])
                S.dma(GNW[64:128, :], gnw_d[0:1, hp * 128 + 64:hp * 128 + 128].partition_broadcast(64), writes=[BGNW])
                S.dma(GNB[0:64, :], gnb_d[0:1, hp * 128:hp * 128 + 64].partition_broadcast(64), writes=[BGNB])
                S.dma(GNB[64:128, :], gnb_d[0:1, hp * 128 + 64:hp * 128 + 128].partition_broadcast(64), writes=[BGNB])
                S.op("dve", lambda e: e.memset(S32[:], 0.0), writes=[BS32])
                S.op("dve", lambda e: e.memset(Sbf_[0][0][:], 0.0), writes=[Sbf_[0][1]])
                S.op("dve", lambda e: e.memset(Sbf_[1][0][:], 0.0), writes=[Sbf_[1][1]])
                c0 = hp * 128
                for sgi in range(4):
                    own = sgi >= 2
                    tsl = slice(sgi * 1024, (sgi + 1) * 1024)
                    r_, Br = Fv("r"); k_, Bk = Fv("k"); v_, Bv = Fv("v")
                    S.dma(r_[:], ZT[c0:c0 + 128, tsl], reads=[B_ZT], writes=[Br])
                    S.dma(k_[:], ZT[2048 + c0:2048 + c0 + 128, tsl], reads=[B_ZT], writes=[Bk], q="act")
                    S.dma(v_[:], ZT[4096 + c0:4096 + c0 + 128, tsl], reads=[B_ZT], writes=[Bv])
                    ew, Bew = Fv("ew"); cle, Bcle = Fv("cle"); t0_, Bt0 = Fv("t0"); t1_, Bt1 = Fv("t1"); t2_, Bt2 = Fv("t2")
                    asig, Basig = Fv("asig"); kkn, Bkkn = Fv("kkn"); k2, Bk2 = Fv("k2"); bb, Bbb = Fv("bb")
                    for hf in range(2):
                        hs = slice(hf * 512, (hf + 1) * 512)
                        gs = slice(sgi * 1024 + hf * 512, sgi * 1024 + (hf + 1) * 512)
                        S.op("pe", lambda e: e.matmul(plo[:], wdec[:, c0:c0 + 128], txw[:, gs], start=True, stop=True), reads=[Bwdec, Btxw], writes=[Bplo])
                        S.op("act", lambda e: e.activation(t0_[:, hs], plo[:], AF.Exp, bias=nrw[:, 0, hp:hp + 1], scale=-1.0), reads=[Bplo, Bnrw], writes=[Bt0])
                        S.op("pe", lambda e: e.matmul(plo[:], waaa[:, c0:c0 + 128], xab[:, gs], start=True, stop=True), reads=[Bwaaa, Bxab], writes=[Bplo])
                        S.op("act", lambda e: e.activation(asig[:, hs], plo[:], AF.Sigmoid, bias=rwp[:, 1, hp:hp + 1]), reads=[Bplo, Brwp], writes=[Basig])
                    S.op("act", lambda e: e.activation(t0_[:], t0_[:], AF.Ln, bias=1.0), reads=[Bt0], writes=[Bt0])
                    S.op("act", lambda e: e.activation(ew[:], t0_[:], AF.Exp, bias=-0.5, scale=-1.0), reads=[Bt0], writes=[Bew])
                    S.op("dve", lambda e: e.tensor_tensor_scan(cle[:], scm[:], ew[:], 0.0, ALU.mult, ALU.add), reads=[Bscm, Bew], writes=[Bcle])
                    S.op("pool", lambda e: e.tensor_scalar(kkn[:], k_[:], rwp[:, 2, hp:hp + 1], None, ALU.mult), reads=[Bk, Brwp], writes=[Bkkn])
                    S.op("pool", lambda e: e.tensor_tensor(t1_[:], kkn[:], kkn[:], ALU.mult), reads=[Bkkn], writes=[Bt1])
                    for hf in range(2):
                        hs = slice(hf * 512, (hf + 1) * 512)
                        S.op("pe", lambda e: e.matmul(plo[:], blk1[:], t1_[:, hs], start=True, stop=True), reads=[Bblk, Bt1], writes=[Bplo])
                        S.op("act", lambda e: e.activation(t2_[:, hs], plo[:], AF.Ln, bias=1e-12), reads=[Bplo], writes=[Bt2])
                    S.op("act", lambda e: e.activation(t2_[:], t2_[:], AF.Exp, scale=-0.5), reads=[Bt2], writes=[Bt2])
                    S.op("dve", lambda e: e.tensor_tensor(kkn[:], kkn[:], t2_[:], ALU.mult), reads=[Bkkn, Bt2], writes=[Bkkn])
                    S.op("pool", lambda e: e.tensor_scalar(t1_[:], asig[:], rwp[:, 3, hp:hp + 1], nrw[:, 1, hp:hp + 1], ALU.mult, ALU.add), reads=[Basig, Brwp, Bnrw], writes=[Bt1])
                    S.op("dve", lambda e: e.tensor_tensor(k2[:], k_[:], t1_[:], ALU.mult), reads=[Bk, Bt1], writes=[Bk2])
                    S.op("pool", lambda e: e.tensor_tensor(bb[:], kkn[:], asig[:], ALU.mult), reads=[Bkkn, Basig], writes=[Bbb])
                    S.op("act", lambda e: e.activation(t0_[:], cle[:], AF.Exp), reads=[Bcle], writes=[Bt0])
                    S.op("dve", lambda e: e.tensor_tensor(Bt["bt"][0][:], bb[:], t0_[:], ALU.mult), reads=[Bbb, Bt0], writes=[Bt["bt"][1]])
                    S.op("pool", lambda e: e.tensor_tensor(Bt["kt"][0][:], k2[:], t0_[:], ALU.mult), reads=[Bk2, Bt0], writes=[Bt["kt"][1]])
                    S.op("act", lambda e: e.activation(t1_[:], cle[:], AF.Exp, scale=-1.0), reads=[Bcle], writes=[Bt1])
                    S.op("dve", lambda e: e.tensor_tensor(t2_[:], ew[:], cle[:], ALU.subtract), reads=[Bew, Bcle], writes=[Bt2])
                    S.op("act", lambda e: e.activation(t2_[:], t2_[:], AF.Exp), reads=[Bt2], writes=[Bt2])
                    for pb in (0, 64):
                        ca, cr = pb, 64 - pb
                        ps_ = slice(pb, pb + 64)
                        S.op("dve", lambda e: e.tensor_tensor(ARt[ps_, :, cr:cr + 64], r_[ps_, :].rearrange("p (c t) -> p c t", t=64), t1_[ps_, :].rearrange("p (c t) -> p c t", t=64), ALU.mult), reads=[Br, Bt1], writes=[BARt])
                        S.op("dve", lambda e: e.scalar_tensor_tensor(ARt[ps_, :, ca:ca + 64], kkn[ps_, :].rearrange("p (c t) -> p c t", t=64), -1.0, t2_[ps_, :].rearrange("p (c t) -> p c t", t=64), ALU.mult, ALU.mult), reads=[Bkkn, Bt2], writes=[BARt])
                    cle3 = cle[:].rearrange("p (c t) -> p c t", t=64)
                    S.op("dve", lambda e: e.tensor_tensor(t0_[:].rearrange("p (c t) -> p c t", t=64), cle3, cle3[:, :, 63:64].to_broadcast([128, 16, 64]), ALU.subtract), reads=[Bcle], writes=[Bt0])
                    S.op("act", lambda e: e.activation(t0_[:], t0_[:], AF.Exp), reads=[Bt0], writes=[Bt0])
                    S.op("act", lambda e: e.activation(wcs[:], cle3[:, :, 63], AF.Exp, scale=-1.0), reads=[Bcle], writes=[Bwcs])
                    S.op("dve", lambda e: e.tensor_tensor(Bt["btc"][0][:], bb[:], t0_[:], ALU.mult), reads=[Bbb, Bt0], writes=[Bt["btc"][1]])
                    S.op("pool", lambda e: e.tensor_tensor(Bt["ktc"][0][:], k2[:], t0_[:], ALU.mult), reads=[Bk2, Bt0], writes=[Bt["ktc"][1]])
                    S.op("act", lambda e: e.copy(Bt["vb"][0][:], v_[:]), reads=[Bv], writes=[Bt["vb"][1]])
                    if own:
                        S.op("pool", lambda e: e.tensor_tensor(t1_[:], r_[:], k2[:], ALU.mult), reads=[Br, Bk2, Bt1], writes=[Bt1])
                        S.op("pool", lambda e: e.tensor_scalar(Bt["rk"][0][:], t1_[:], rwp[:, 4, hp:hp + 1], None, ALU.mult), reads=[Bt1, Brwp], writes=[Bt["rk"][1]])
                    bt, Bbt = Bt["bt"]; kt, Bkt = Bt["kt"]; btc, Bbtc = Bt["btc"]; ktc, Bktc = Bt["ktc"]; vb, Bvb = Bt["vb"]; rk, Brk = Bt["rk"]


                    def stage12(b4):
                        par = b4 % 2
                        ARB, BARB = ARB_[par]; AK, BAK = AK_[par]; TOK, BTOK = TOK_[par]
                        for j in range(4):
                            c = b4 * 4 + j
                            cs = slice(c * 64, (c + 1) * 64)
                            pk, Bpk = (pN, BpN) if j < 2 else (pNT, BpNT)
                            ko = (j % 2) * 192
                            for pb in (0, 64):
                                ca = pb
                                P = slice(pb, pb + 64)
                                tp = (pb, pb)
                                S.op("pe", lambda e: e.matmul(ps1[P, j * 128:(j + 1) * 128], bt[P, cs], ARt[P, c, :], start=True, stop=True, tile_position=tp), reads=[Bbt, BARt], writes=[Bps1])
                                S.op("pe", lambda e: e.matmul(ps2[P, j * 128:(j + 1) * 128], kt[P, cs], ARt[P, c, :], start=True, stop=True, tile_position=tp), reads=[Bkt, BARt], writes=[Bps2])
                                S.op("pe", lambda e: e.matmul(ps3[P, j * 128 + ca:j * 128 + ca + 64], ARt[P, c, ca:ca + 64], bt[P, cs], start=True, stop=True, tile_position=tp), reads=[Bbt, BARt], writes=[Bps3])
                                S.op("pe", lambda e: e.matmul(pk[P, ko:ko + 64], btc[P, cs], identb[P, pb:pb + 64], start=True, stop=True, tile_position=tp), reads=[Bbtc, Bidb], writes=[Bpk])
                                S.op("pe", lambda e: e.matmul(pk[P, ko + 64:ko + 128], ktc[P, cs], identb[P, pb:pb + 64], start=True, stop=True, tile_position=tp), reads=[Bktc, Bidb], writes=[Bpk])
                                S.op("pe", lambda e: e.matmul(pk[P, ko + 128:ko + 192], vb[P, cs], identb[P, pb:pb + 64], start=True, stop=True, tile_position=tp), reads=[Bvb, Bidb], writes=[Bpk])
                        N0, BN0 = N_[0]; NT0, BNT0 = NT_[0]; TT0, BTT0 = TT_[par * 4]
                        v4 = lambda t: t[:].rearrange("p (j c) -> p j c", j=4)
                        mb = lambda i: masks[:, i, :].unsqueeze(1).to_broadcast([128, 4, 128])
                        S.op("dve", lambda e: e.tensor_tensor(NT0[:], v4(ps1), mb(0), ALU.mult), reads=[Bps1, Bmk], writes=[BNT0])
                        S.op("dve", lambda e: e.tensor_tensor(N0[:], v4(ps3), mb(3), ALU.mult), reads=[Bps3, Bmk], writes=[BN0])
                        S.op("dve", lambda e: e.tensor_tensor(TT0[:], NT0[:], identb[:].unsqueeze(1).to_broadcast([128, 4, 128]), ALU.add), reads=[BNT0, Bidb], writes=[BTT0])
                        S.op("dve", lambda e: e.tensor_tensor(AK[:], v4(ps2), mb(2), ALU.mult), reads=[Bps2, Bmk], writes=[BAK])
                        S.op("dve", lambda e: e.tensor_tensor(ARB[:], v4(ps1), mb(1), ALU.mult), reads=[Bps1, Bmk], writes=[BARB])
                        S.op("act", lambda e: e.copy(TOK[:, 0:2, :], pN[:, 0:384].rearrange("p (j c) -> p j c", j=2)), reads=[BpN], writes=[BTOK])
                        S.op("act", lambda e: e.copy(TOK[:, 2:4, :], pNT[:, 0:384].rearrange("p (j c) -> p j c", j=2)), reads=[BpNT], writes=[BTOK])
                        yield None
                        Nc, BNc = N0, BN0
                        NTc, BNTc = NT0, BNT0
                        TTc, BTTc = TT0, BTT0
                        for kq in range(1, 6):
                            Nn, BNn = N_[1 + (kq % 3)]
                            NTn, BNTn = NT_[1 + (kq % 3)]
                            TTn, BTTn = TT_[par * 4 + 1 + (kq % 3)]
                            for j in range(4):
                                S.op("pe", lambda e: e.matmul(pN[:, j * 128:(j + 1) * 128], NTc[:, j, :], Nc[:, j, :], start=True, stop=True), reads=[BNTc, BNc], writes=[BpN])
                            if kq < 5:
                                for j in range(4):
                                    S.op("pe", lambda e: e.matmul(pNT[:, j * 128:(j + 1) * 128], Nc[:, j, :], NTc[:, j, :], start=True, stop=True), reads=[BNTc, BNc], writes=[BpNT])
                            S.op("dve", lambda e: e.tensor_copy(Nn[:], v4(pN)), reads=[BpN], writes=[BNn])
                            if kq < 5:
                                S.op("act", lambda e: e.copy(NTn[:], v4(pNT)), reads=[BpNT], writes=[BNTn])
                            for j in range(4):
                                S.op("pe", lambda e: e.matmul(pT[:, j * 128:(j + 1) * 128], Nn[:, j, :], TTc[:, j, :], start=True, stop=True), reads=[BNn, BTTc], writes=[BpT])
                            S.op("dve", lambda e: e.tensor_tensor(TTn[:], v4(pT), TTc[:], ALU.add), reads=[BpT, BTTc], writes=[BTTn])
                            Nc, BNc, NTc, BNTc, TTc, BTTc = Nn, BNn, NTn, BNTn, TTn, BTTn
                            yield None
                        yield (TTc, BTTc)

                    def recur(b4, j, TTc, BTTc):
                        par = b4 % 2
                        ARB, BARB = ARB_[par]; AK, BAK = AK_[par]; TOK, BTOK = TOK_[par]
                        c = b4 * 4 + j
                        gc = sgi * 16 + c
                        cs = slice(c * 64, (c + 1) * 64)
                        Xb, BXb = Xb_[gc % 2]; Ub, BUb = Ub_[gc % 2]
                        Sbf, BSbf = Sbf_[gc % 2]
                        Sbn, BSbn = Sbf_[(gc + 1) % 2]
                        for pb in (0, 64):
                            ca = pb
                            P = slice(pb, pb + 64)
                            tp = (pb, pb)
                            S.op("pe", lambda e: e.matmul(pxu[P, 0:64], ARt[P, c, ca:ca + 64], Sbf[P, :], start=True, stop=False, tile_position=tp), reads=[BARt, BSbf], writes=[Bpxu])
                            S.op("pe", lambda e: e.matmul(pxu[P, 0:64], AK[P, j, ca:ca + 64], TOK[P, j, 128:192], start=False, stop=True, tile_position=tp), reads=[BAK, BTOK], writes=[Bpxu])
                        S.op("act", lambda e: e.copy(Xb[:], pxu[:, 0:64]), reads=[Bpxu], writes=[BXb])
                        for pb in (0, 64):
                            P = slice(pb, pb + 64)
                            S.op("pe", lambda e: e.matmul(pxu[P, 64:128], TTc[P, j, pb:pb + 64], Xb[P, :], start=True, stop=True, tile_position=(pb, pb)), reads=[BTTc, BXb], writes=[Bpxu])
                        S.op("dve", lambda e: e.tensor_copy(Ub[:], pxu[:, 64:128]), reads=[Bpxu], writes=[BUb])
                        for pb in (0, 64):
                            P = slice(pb, pb + 64)
                            tp = (pb, pb)
                            S.op("pe", lambda e: e.matmul(pxu[P, 128:192], TOK[P, j, 0:64], Ub[P, :], start=True, stop=False, tile_position=tp), reads=[BTOK, BUb], writes=[Bpxu])
                            S.op("pe", lambda e: e.matmul(pxu[P, 128:192], TOK[P, j, 64:128], TOK[P, j, 128:192], start=False, stop=True, tile_position=tp), reads=[BTOK], writes=[Bpxu])
                        if own:
                            for pb in (0, 64):
                                cr = 64 - pb
                                P = slice(pb, pb + 64)
                                tp = (pb, pb)
                                S.op("pe", lambda e: e.matmul(pys[P, 0:64], ARt[P, c, cr:cr + 64], Sbf[P, :], start=True, stop=False, tile_position=tp), reads=[BARt, BSbf], writes=[Bpys])
                                S.op("pe", lambda e: e.matmul(pys[P, 0:64], AK[P, j, cr:cr + 64], TOK[P, j, 128:192], start=False, stop=False, tile_position=tp), reads=[BAK, BTOK], writes=[Bpys])
                                S.op("pe", lambda e: e.matmul(pys[P, 0:64], ARB[P, j, cr:cr + 64], Ub[P, :], start=False, stop=True, tile_position=tp), reads=[BARB, BUb], writes=[Bpys])
                                S.op("pe", lambda e: e.matmul(pys[P, 128:129], rk[P, cs], blk1b[P, pb:pb + 1], start=True, stop=True, tile_position=tp), reads=[Brk, Bblkb], writes=[Bpys])
                                gtok = slice(sgi * 1024 + c * 64, sgi * 1024 + (c + 1) * 64)
                                hc = c0 + pb
                                for kc in range(2):
                                    S.op("pe", lambda e: e.matmul(pys[P, 64:128], sxg[:, kc, gtok], wgat[:, kc, hc:hc + 64], start=(kc == 0), stop=(kc == 1), tile_position=(0, pb)), reads=[Bsxg, Bwgat], writes=[Bpys])
                        S.op("dve", lambda e: e.scalar_tensor_tensor(S32[:], S32[:], wcs[:, c:c + 1], pxu[:, 128:192], ALU.mult, ALU.add), reads=[BS32, Bwcs, Bpxu], writes=[BS32])
                        S.op("act", lambda e: e.copy(Sbn[:], S32[:]), reads=[BS32], writes=[BSbn])
                        if own:
                            ysb, Bysb = ysb_[gc % 2]; ycn, Bycn = ycn_[gc % 2]; yjk, Byjk = yjk_[gc % 2]
                            s4, Bs4 = st4_[gc % 2]; rwb, Brwb = rwb_[gc % 2]
                            S.op("dve", lambda e: e.memset(s4[:], 0.0), writes=[Bs4])
                            S.op("act", lambda e: e.activation(ysb[:], pys[:, 0:64], AF.Identity, accum_out=s4[:, 0:1]), reads=[Bpys, Bs4], writes=[Bysb, Bs4])
                            S.op("dve", lambda e: e.tensor_scalar(s4[:, 1:2], s4[:, 0:1], -1.0 / 64, None, ALU.mult), reads=[Bs4], writes=[Bs4])
                            S.op("dve", lambda e: e.tensor_scalar(ycn[:], ysb[:], s4[:, 1:2], None, ALU.add), reads=[Bysb, Bs4], writes=[Bycn])
                            S.op("act", lambda e: e.activation(yjk[:], ycn[:], AF.Square, accum_out=s4[:, 2:3]), reads=[Bycn], writes=[Byjk, Bs4])
                            S.op("act", lambda e: e.activation(s4[:, 3:4], s4[:, 2:3], AF.Ln, bias=64e-5, scale=1.0 / 64), reads=[Bs4], writes=[Bs4])
                            S.op("act", lambda e: e.activation(s4[:, 4:5], s4[:, 3:4], AF.Exp, scale=-0.5), reads=[Bs4], writes=[Bs4])
                            S.op("dve", lambda e: e.scalar_tensor_tensor(ycn[:], ycn[:], s4[:, 4:5], GNW[:], ALU.mult, ALU.mult), reads=[Bycn, Bs4, BGNW], writes=[Bycn])
                            S.op("dve", lambda e: e.tensor_tensor(ycn[:], ycn[:], GNB[:], ALU.add), reads=[Bycn, BGNB], writes=[Bycn])
                            S.op("act", lambda e: e.copy(s4[:, 5:6], pys[:, 128:129]), reads=[Bpys], writes=[Bs4])
                            S.op("dve", lambda e: e.scalar_tensor_tensor(ycn[:], TOK[:, j, 128:192], s4[:, 5:6], ycn[:], ALU.mult, ALU.add), reads=[BTOK, Bs4, Bycn], writes=[Bycn])
                            S.op("dve", lambda e: e.tensor_tensor(rwb[:], ycn[:], pys[:, 64:128], ALU.mult), reads=[Bycn, Bpys], writes=[Brwb])
                            for pb in (0, 64):
                                P = slice(pb, pb + 64)
                                S.op("pe", lambda e: e.matmul(pys[P, 192:256], rwb[P, :], identb[P, pb:pb + 64], start=True, stop=True, tile_position=(pb, pb)), reads=[Brwb, Bidb], writes=[Bpys])
                            S.op("act", lambda e: e.copy(rwTs[:, cs], pys[:, 192:256]), reads=[Bpys], writes=[BrwTs])

                    def run_all(g):
                        r = None
                        for r in g:
                            pass
                        return r

                    cur = run_all(stage12(0))
                    for b4 in range(4):
                        g = stage12(b4 + 1) if b4 < 3 else None
                        if g is not None:
                            next(g)
                        for j in range(4):
                            recur(b4, j, cur[0], cur[1])
                            next(precast, None)
                            if g is not None:
                                next(g)
                                if j == 3:
                                    next(g)
                                    cur = next(g)

                    if own:
                        S.dma(RWT[c0:c0 + 128, (sgi - 2) * 1024:(sgi - 1) * 1024], rwTs[:], reads=[BrwTs], acc=[B_RWT])
            for _ in precast:
                pass


        esD = ExitStack()
        if stop == "C":
            S.drain("sp")
            return nc
        with esD:
            accT, Bacc = SB(esD, "accT", [128, 32, 512], F32)
            actA, BactA = SB(esD, "actA", [128, 32, 512], BF16)
            actB, BactB = SB(esD, "actB", [128, 20, 512], BF16)
            hid, Bhid = actB, BactB
            gts = [SB(esD, "gts%d" % i, [128, 2, 512], BF16) for i in range(2)]
            nmlp, Bnmlp = SB(esD, "nmlp", [128, 32], F32)
            nple, Bnple = SB(esD, "nple", [128, 32], F32)
            S.dma(nmlp[:], nmlp_d[:], writes=[Bnmlp])
            S.dma(nple[:], nple_d[:], writes=[Bnple])
            xrow = [SB(esD, "xrow%d" % i, [128, 1024], F32) for i in range(2)]
            osd = [SB(esD, "osd%d" % i, [128, 12, 129], F32) for i in range(1)]
            numt, Bnum = SB(esD, "numt", [128, 4, 129], F32)
            rden, Brden = SB(esD, "rden", [128, 4], F32)
            attb, Battb = SB(esD, "attb", [128, 4, 128], BF16)
            pld, Bpld = SB(esD, "pld", [128, 256], F32)
            pldb, Bpldb = SB(esD, "pldb", [128, 256], BF16)
            pT, BpT = SB(esD, "pT", [128, 2, 512], BF16)
            tmpA = [SB(esD, "tmpA%d" % i, [128, 512], F32) for i in range(2)]
            sqd, Bsqd = SB(esD, "sqd", [128, 512], BF16)
            rstd, Brstd = SB(esD, "rstd", [128, 512], F32)
            pdm = [PS(esD, "pdm%d" % i, [128, 512]) for i in range(3)]
            pdb = [PS(esD, "pdb%d" % i, [128, 512]) for i in range(2)]
            pdn, Bpdn = PS(esD, "pdn", [128, 512])
            pdt = [PS(esD, "pdt%d" % i, [128, 1024], BF16) for i in range(1)]
            pdf, Bpdf = PS(esD, "pdf", [128, 512])
            wpb = [SB(esD, "wpb%d" % i, [128, 256], BF16) for i in range(2)]
            dctr = {"mm": 0, "b": 0, "ta": 0, "x": 0}

            def nextp():
                dctr["mm"] += 1
                return pdm[dctr["mm"] % 3]

            def norm_to(actdst, Bactdst, gainT, BgainT):
                for blk in range(32):
                    S.op("act", lambda e: e.activation(sqd[:], accT[:, blk, :], AF.Square), reads=[Bacc], writes=[Bsqd])
                    S.op("pe", lambda e: e.matmul(pdn[:], onesb[:], sqd[:], start=(blk == 0), stop=(blk == 31)), reads=[Bones, Bsqd], writes=[Bpdn])
                S.op("act", lambda e: e.activation(rstd[:], pdn[:], AF.Ln, bias=1e-6, scale=1.0 / D), reads=[Bpdn], writes=[Brstd])
                S.op("act", lambda e: e.activation(rstd[:], rstd[:], AF.Exp, scale=-0.5), reads=[Brstd], writes=[Brstd])
                for blk in range(32):
                    S.op("dve", lambda e: e.scalar_tensor_tensor(actdst[:, blk, :], accT[:, blk, :], gainT[:, blk:blk + 1], rstd[:], ALU.mult, ALU.mult), reads=[Bacc, BgainT, Brstd], writes=[Bactdst])

            dspecs = []
            for ti_ in range(4):
                for blk_ in range(32):
                    dspecs.append((w_attn_up, 0, 512, blk_ * 128, 128, None))
                    dspecs.append((w_rwkv_up, 0, 2048, blk_ * 128, 128, None))
                for blk_ in range(32):
                    dspecs.append((w_out, 0, D, blk_ * 128, 128, None))
                for ch_ in range(16):
                    for fb_ in range(8):
                        dspecs.append((w_mlp_in, 0, D, ch_ * 1024 + fb_ * 128, 128, None))
                    for blk_ in range(32):
                        dspecs.append((w_mlp_out, ch_ * 1024, 1024, blk_ * 128, 128, None))
                for blk_ in range(32):
                    dspecs.append((w_ple_gate, 0, D, blk_ * 128, 128, None))
                    dspecs.append((w_ple_proj, 0, 256, blk_ * 128, 128, wpb[blk_ % 2]))
            ring = [wbf[0], wbf[1]] + [(w_[:].bitcast(BF16), bw_) for (w_, bw_) in wst]
            key2off = {}
            for bi_, (w_ap, r0, krows, c0_) in enumerate(tail_blocks):
                key2off[(w_ap.tensor.name, r0, krows, c0_)] = wb_off[bi_]

            class WQ2:
                def __init__(self, specs):
                    self.specs = specs
                    self.pos = 0
                    self.dpos = 0
                    self.tiles = {}
                    self.rr = 0

                def _dma(self):
                    (w_ap, r0, krows, c0_, m, dst) = self.specs[self.dpos]
                    kc = (krows + 127) // 128
                    wti, off = key2off[(w_ap.tensor.name, r0, krows, c0_)]
                    if dst is not None:
                        wb, Bwb = dst
                        view = wb[:, 0:kc * 128]
                    else:
                        wb, Bwb = ring[self.rr % len(ring)]
                        self.rr += 1
                        view = wb[:, 0:kc * 128]
                    S.dma(view, WBs[wti][:, off:off + kc * 128], reads=[B_WB], writes=[Bwb], q="sp")
                    self.tiles[self.dpos] = (wb, Bwb, kc)
                    self.dpos += 1

                def next(self):
                    n = len(self.specs)
                    while self.dpos < min(n, self.pos + 4):
                        self._dma()
                    r = self.tiles.pop(self.pos)
                    self.pos += 1
                    return r
            wqD = WQ2(dspecs)

            for ti in range(4):
                T0 = ti * 512
                for sub in range(4):
                    for cq in range(4):
                        xr, Bxr = xrow[dctr["x"] % 2]
                        dctr["x"] += 1
                        S.dma(xr[:], x_d[OWN + T0 + sub * 128:OWN + T0 + (sub + 1) * 128, cq * 1024:(cq + 1) * 1024], writes=[Bxr], q="sp" if cq % 2 == 0 else "act")
                        for half in range(2):
                            for j in range(4):
                                S.op("pe", lambda e: e.transpose(pdf[:, j * 128:(j + 1) * 128], xr[:, (half * 4 + j) * 128:(half * 4 + j + 1) * 128], identf[:]), reads=[Bxr, Bidf], writes=[Bpdf])
                            blk0 = cq * 8 + half * 4
                            S.op("dve", lambda e: e.tensor_copy(accT[:, blk0:blk0 + 4, sub * 128:(sub + 1) * 128], pdf[:].rearrange("p (k t) -> p k t", k=4)), reads=[Bpdf], writes=[Bacc])
                for sub in range(4):
                    od, Bod = osd[0]
                    tk = T0 + sub * 128
                    S.dma(od[:], OS[tk:tk + 128], reads=[B_OS], writes=[Bod])
                    S.op("dve", lambda e: e.tensor_tensor(numt[:], od[:, 0:4, :], od[:, 4:8, :], ALU.add), reads=[Bod], writes=[Bnum])
                    S.op("dve", lambda e: e.tensor_tensor(numt[:], numt[:], od[:, 8:12, :], ALU.add), reads=[Bod, Bnum], writes=[Bnum])
                    S.op("dve", lambda e: e.reciprocal(rden[:], numt[:, :, 128]), reads=[Bnum], writes=[Brden])
                    S.op("dve", lambda e: e.tensor_tensor(attb[:], numt[:, :, 0:128], rden[:].unsqueeze(2).to_broadcast([128, 4, 128]), ALU.mult), reads=[Bnum, Brden], writes=[Battb])
                    pt, Bpt = pdt[0]
                    for j in range(4):
                        S.op("pe", lambda e: e.transpose(pt[:, j * 128:(j + 1) * 128], attb[:, j, :], identb[:]), reads=[Battb, Bidb], writes=[Bpt])
                    S.op("act", lambda e: e.copy(actB[:, 0:4, sub * 128:(sub + 1) * 128], pt[:, 0:512].rearrange("p (k t) -> p k t", k=4)), reads=[Bpt], writes=[BactB])
                    S.dma(pld[:], p_d[tk:tk + 128, :], writes=[Bpld], q="act")
                    S.op("dve", lambda e: e.tensor_copy(pldb[:], pld[:]), reads=[Bpld], writes=[Bpldb])
                    for j in range(2):
                        S.op("pe", lambda e: e.transpose(pt[:, 512 + j * 128:512 + (j + 1) * 128], pldb[:, j * 128:(j + 1) * 128], identb[:]), reads=[Bpldb, Bidb], writes=[Bpt])
                    S.op("act", lambda e: e.copy(pT[:, :, sub * 128:(sub + 1) * 128], pt[:, 512:768].rearrange("p (k t) -> p k t", k=2)), reads=[Bpt], writes=[BpT])
                S.dma(actB[:, 4:20, :], RWT[:, T0:T0 + 512].rearrange("(k p) t -> p k t", p=128), reads=[B_RWT], writes=[BactB])
                for blk in range(32):
                    gt, Bgt = gts[blk % 2]
                    S.dma(gt[:, 0, :], GT[blk * 128:(blk + 1) * 128, T0:T0 + 512], reads=[B_GT], writes=[Bgt])
                    S.dma(gt[:, 1, :], GT[4096 + blk * 128:4096 + (blk + 1) * 128, T0:T0 + 512], reads=[B_GT], writes=[Bgt], q="act")
                    pa, Bpa = pdb[0]
                    pr, Bpr = pdb[1]
                    wa, Bwa, _ = wqD.next()
                    for kc in range(4):
                        S.op("pe", lambda e: e.matmul(pa[:], wa[:, kc * 128:(kc + 1) * 128], actB[:, kc, :], start=(kc == 0), stop=(kc == 3)), reads=[Bwa, BactB], writes=[Bpa])
                    wr, Bwr, _ = wqD.next()
                    for kc in range(16):
                        S.op("pe", lambda e: e.matmul(pr[:], wr[:, kc * 128:(kc + 1) * 128], actB[:, 4 + kc, :], start=(kc == 0), stop=(kc == 15)), reads=[Bwr, BactB], writes=[Bpr])
                    ta, Bta = tmpA[0]
                    tb, Btb = tmpA[1]
                    S.op("dve", lambda e: e.tensor_tensor(ta[:], pa[:], gt[:, 0, :], ALU.mult), reads=[Bpa, Bgt], writes=[Bta])
                    S.op("dve", lambda e: e.tensor_tensor(tb[:], pr[:], gt[:, 1, :], ALU.mult), reads=[Bpr, Bgt], writes=[Btb])
                    S.op("pool", lambda e: e.tensor_tensor(actA[:, blk, :], ta[:], tb[:], ALU.add), reads=[Bta, Btb], writes=[BactA])
                for blk in range(32):
                    wb, Bwb, _ = wqD.next()
                    pm_, Bpm = nextp()
                    for kc in range(32):
                        S.op("pe", lambda e: e.matmul(pm_[:], wb[:, kc * 128:(kc + 1) * 128], actA[:, kc, :], start=(kc == 0), stop=(kc == 31)), reads=[Bwb, BactA], writes=[Bpm])
                    S.op("dve", lambda e: e.tensor_tensor(accT[:, blk, :], accT[:, blk, :], pm_[:], ALU.add), reads=[Bacc, Bpm], writes=[Bacc])
                norm_to(actA, BactA, nmlp, Bnmlp)
                for ch in range(16):
                    for fb in range(8):
                        wb, Bwb, _ = wqD.next()
                        pm_, Bpm = nextp()
                        for kc in range(32):
                            S.op("pe", lambda e: e.matmul(pm_[:], wb[:, kc * 128:(kc + 1) * 128], actA[:, kc, :], start=(kc == 0), stop=(kc == 31)), reads=[Bwb, BactA], writes=[Bpm])
                        ta, Bta = tmpA[fb % 2]
                        S.op("act", lambda e: e.activation(ta[:], pm_[:], AF.Relu), reads=[Bpm], writes=[Bta])
                        S.op("dve", lambda e: e.tensor_tensor(hid[:, fb, :], ta[:], ta[:], ALU.mult), reads=[Bta], writes=[Bhid])
                    for blk in range(32):
                        wb, Bwb, _ = wqD.next()
                        pm_, Bpm = nextp()
                        for kc in range(8):
                            S.op("pe", lambda e: e.matmul(pm_[:], wb[:, kc * 128:(kc + 1) * 128], hid[:, kc, :], start=(kc == 0), stop=(kc == 7)), reads=[Bwb, Bhid], writes=[Bpm])
                        S.op("dve", lambda e: e.tensor_tensor(accT[:, blk, :], accT[:, blk, :], pm_[:], ALU.add), reads=[Bacc, Bpm], writes=[Bacc])
                norm_to(actA, BactA, nple, Bnple)
                for blk in range(32):
                    wb, Bwb, _ = wqD.next()
                    pm_, Bpm = nextp()
                    pp, Bpp = pdb[blk % 2]
                    for kc in range(32):
                        S.op("pe", lambda e: e.matmul(pm_[:], wb[:, kc * 128:(kc + 1) * 128], actA[:, kc, :], start=(kc == 0), stop=(kc == 31)), reads=[Bwb, BactA], writes=[Bpm])
                    wp, Bwp, _ = wqD.next()
                    for kc in range(2):
                        S.op("pe", lambda e: e.matmul(pp[:], wp[:, kc * 128:(kc + 1) * 128], pT[:, kc, :], start=(kc == 0), stop=(kc == 1)), reads=[Bwp, BpT], writes=[Bpp])
                    ta, Bta = tmpA[blk % 2]
                    S.op("act", lambda e: e.activation(ta[:], pm_[:], AF.Sigmoid), reads=[Bpm], writes=[Bta])
                    S.op("dve", lambda e: e.tensor_tensor(ta[:], ta[:], pp[:], ALU.mult), reads=[Bta, Bpp], writes=[Bta])
                    S.op("pool", lambda e: e.tensor_tensor(accT[:, blk, :], accT[:, blk, :], ta[:], ALU.add), reads=[Bacc, Bta], writes=[Bacc])
                for sub in range(4):
                    for cq in range(4):
                        xr, Bxr = xrow[dctr["x"] % 2]
                        dctr["x"] += 1
                        for half in range(2):
                            for j in range(4):
                                blk = cq * 8 + half * 4 + j
                                S.op("pe", lambda e: e.transpose(pdf[:, j * 128:(j + 1) * 128], accT[:, blk, sub * 128:(sub + 1) * 128], identf[:]), reads=[Bacc, Bidf], writes=[Bpdf])
                            S.op("act", lambda e: e.copy(xr[:, half * 512:(half + 1) * 512], pdf[:]), reads=[Bpdf], writes=[Bxr])
                        S.dma(out_d[T0 + sub * 128:T0 + (sub + 1) * 128, cq * 1024:(cq + 1) * 1024], xr[:], reads=[Bxr], writes=[])
        S.drain("sp")
        print("instr counts", S.cnt)
    return nc


_CACHE = {}


def make_in_maps(inp):
    f32 = np.float32
    x = np.asarray(inp["x"], f32)
    p = np.asarray(inp["p"], f32)[0]
    c = host_consts()
    shift_mix = np.asarray(inp["shift_mix"], f32)[0]
    mixT = np.zeros((128, 52), f32)
    for b in range(52):
        mixT[:ZM[b], b] = shift_mix[ZST[b]:ZST[b] + ZM[b]]
    rwp = np.stack([np.asarray(inp[k], f32).reshape(2048).reshape(16, 128).T
                    for k in ("w0", "a0", "k_k", "k_a", "r_k")], axis=1)
    rb33 = np.concatenate([np.asarray(inp["rel_bias"], f32), np.ones((1, 12), f32)], axis=0)
    qkg = np.stack([np.asarray(inp["q_gain"], f32)[0], np.asarray(inp["k_gain"], f32)[0]], axis=1)
    shared = {
        "w_in": np.asarray(inp["w_in"], f32)[0],
        "w_attn_up": np.asarray(inp["w_attn_up"], f32)[0],
        "w_decay_up": np.asarray(inp["w_decay_up"], f32)[0],
        "w_aaa_up": np.asarray(inp["w_aaa_up"], f32)[0],
        "w_gate_up": np.asarray(inp["w_gate_up"], f32)[0],
        "w_rwkv_up": np.asarray(inp["w_rwkv_up"], f32)[0],
        "w_out": np.asarray(inp["w_out"], f32)[0],
        "w_mlp_in": np.asarray(inp["w_mlp_in"], f32)[0],
        "w_mlp_out": np.asarray(inp["w_mlp_out"], f32)[0],
        "w_ple_gate": np.asarray(inp["w_ple_gate"], f32)[0],
        "w_ple_proj": np.asarray(inp["w_ple_proj"], f32)[0],
        "norm_mix": np.asarray(inp["norm_mix"], f32).reshape(1, D),
        "qkg": np.ascontiguousarray(qkg),
        "rb33": rb33,
        "mixT": mixT,
        "rwp": np.ascontiguousarray(rwp),
        "gn_w": np.asarray(inp["gn_w"], f32).reshape(1, 2048),
        "gn_b": np.asarray(inp["gn_b"], f32).reshape(1, 2048),
        "nmlpT": np.ascontiguousarray(np.asarray(inp["norm_mlp"], f32).reshape(32, 128).T),
        "npleT": np.ascontiguousarray(np.asarray(inp["norm_ple"], f32).reshape(32, 128).T),
        "ident": c["ident"], "masks": c["masks"], "scanmask": c["scanmask"],
        "blkones": c["blkones"], "oh": c["oh"],
    }
    in_maps = []
    for core in range(8):
        b, th = core // 2, core % 2
        if th == 0:
            xl = np.concatenate([np.zeros((OWN, D), f32), x[b, :OWN]], axis=0)
        else:
            xl = x[b]
        m = dict(shared)
        m["x"] = np.ascontiguousarray(xl)
        m["p"] = np.ascontiguousarray(p[b, th * OWN:(th + 1) * OWN])
        m["pm"] = np.full((128, 1), NEG if th == 0 else 0.0, f32)
        in_maps.append(m)
    return in_maps


def kernel(**inp):
    f32 = np.float32
    x = np.asarray(inp["x"], f32)
    in_maps = make_in_maps(inp)
    if "nc" not in _CACHE:
        _CACHE["nc"] = build_program()
    res = run_bass_kernel_spmd(_CACHE["nc"], in_maps, core_ids=list(range(8)))
    out = np.zeros((4, SEQ, D), f32)
    for core in range(8):
        b, th = core // 2, core % 2
        out[b, th * OWN:(th + 1) * OWN] = np.asarray(res.results[core]["out"], f32)
    return out
```
